# Optimizing a Trainium2 kernel written in Bass

```python
import math
import jax, jax.numpy as jnp
from jax import lax
import numpy as np

D_MODEL = 1024
BATCH = 2
SEQ = 16384
DEPTH = 4
DEC_BATCH = 8
DEC_SEQ = 2048
PAST_LEN = 128

N_EVEN = (DEPTH + 1) // 2
N_ODD = DEPTH // 2
D_FF = 2816
ROPE_THETA = 10000.0
NORM_EPS = 1e-6
Q_BLOCK = 128
CONV_WIDTH = D_MODEL // 2
CONV_K = 3
DIFF_WIDTH = D_MODEL // 2
DIFF_HEADS = 4
DIFF_HEAD_DIM = DIFF_WIDTH // DIFF_HEADS // 2
EVEN_IN = 3 * CONV_WIDTH + 3 * DIFF_WIDTH
MLA_HEADS = 8
MLA_NOPE = 128
MLA_ROPE = 64
MLA_V = 128
MLA_QK = MLA_NOPE + MLA_ROPE
MLA_Q_RANK = 384
MLA_KV_RANK = 256
MLA_DOWN = MLA_Q_RANK + MLA_KV_RANK + MLA_ROPE

kernel_name = 'hybrid_conv_diffattn_mla_macaron_encoder'


def _rms_norm(x, g):
    x32 = x.astype(jnp.float32)
    y = x32 * lax.rsqrt(jnp.mean(x32 * x32, axis=-1, keepdims=True) + NORM_EPS)
    return (y * g.astype(jnp.float32)).astype(x.dtype)


def _rope(x):
    s, d = x.shape[1], x.shape[-1]
    inv = 1.0 / (ROPE_THETA ** (jnp.arange(0, d, 2, dtype=jnp.float32) / d))
    ang = jnp.arange(s, dtype=jnp.float32)[:, None] * inv[None, :]
    shape = (s,) + (1,) * (x.ndim - 3) + (d // 2,)
    cos = jnp.cos(ang).reshape(shape)
    sin = jnp.sin(ang).reshape(shape)
    x32 = x.astype(jnp.float32)
    x1, x2 = x32[..., : d // 2], x32[..., d // 2:]
    return jnp.concatenate([x1 * cos - x2 * sin, x1 * sin + x2 * cos], axis=-1).astype(x.dtype)


def _swiglu(h, w_in, w_out):
    g, u = jnp.split(h @ w_in, 2, axis=-1)
    return (jax.nn.silu(g) * u) @ w_out


def _blocks(t):
    b, s = t.shape[0], t.shape[1]
    return jnp.swapaxes(t.reshape((b, s // Q_BLOCK, Q_BLOCK) + t.shape[2:]), 0, 1)


def _unblocks(t):
    nb, b, qb = t.shape[0], t.shape[1], t.shape[2]
    return jnp.swapaxes(t, 0, 1).reshape((b, nb * qb) + t.shape[3:])


def _softmax_attention(q, k, v, scale):
    def one_block(qb):
        s = jnp.einsum('bqhd,bkhd->bhqk', qb, k).astype(jnp.float32) * scale
        p = jax.nn.softmax(s, axis=-1).astype(v.dtype)
        return jnp.einsum('bhqk,bkhe->bqhe', p, v)
    return _unblocks(lax.map(one_block, _blocks(q)))


def _diff_attention(q, k, v, lam, scale):
    def one_block(qb):
        s = jnp.einsum('bqhcd,bkhcd->bchqk', qb, k).astype(jnp.float32) * scale
        p = jax.nn.softmax(s, axis=-1)
        w = (p[:, 0] - lam * p[:, 1]).astype(v.dtype)
        return jnp.einsum('bhqk,bkhe->bqhe', w, v)
    return _unblocks(lax.map(one_block, _blocks(q)))


def _conv_diff_mixer(h, w_in, conv_w, qn_g, kn_g, lam_vecs, subln_g, w_out, lambda_init):
    b, s, _ = h.shape
    proj = h @ w_in
    c1, c2, c3 = CONV_WIDTH, 2 * CONV_WIDTH, 3 * CONV_WIDTH
    g_b, g_c, u, q, k, v = jnp.split(proj, [c1, c2, c3, c3 + DIFF_WIDTH, c3 + 2 * DIFF_WIDTH], axis=-1)
    z = jnp.pad(g_c * u, ((0, 0), (1, 1), (0, 0)))
    conv = conv_w[0] * z[:, :-2] + conv_w[1] * z[:, 1:-1] + conv_w[2] * z[:, 2:]
    y_a = g_b * conv
    q = _rope(_rms_norm(q.reshape(b, s, DIFF_HEADS, 2, DIFF_HEAD_DIM), qn_g))
    k = _rope(_rms_norm(k.reshape(b, s, DIFF_HEADS, 2, DIFF_HEAD_DIM), kn_g))
    v = v.reshape(b, s, DIFF_HEADS, 2 * DIFF_HEAD_DIM)
    lv = lam_vecs.astype(jnp.float32)
    lam = jnp.exp(jnp.sum(lv[0] * lv[1])) - jnp.exp(jnp.sum(lv[2] * lv[3])) + lambda_init
    o = _diff_attention(q, k, v, lam, DIFF_HEAD_DIM ** -0.5)
    o = _rms_norm(o, subln_g) * (1.0 - lambda_init)
    y_b = o.reshape(b, s, DIFF_WIDTH)
    return jnp.concatenate([y_a, y_b], axis=-1) @ w_out


def _mla_mixer(h, w_down, q_lat_g, kv_lat_g, w_uq, w_ukv, qn_g, kn_g, w_o):
    b, s, _ = h.shape
    lat = h @ w_down
    c_q, c_kv, k_rope = jnp.split(lat, [MLA_Q_RANK, MLA_Q_RANK + MLA_KV_RANK], axis=-1)
    q = (_rms_norm(c_q, q_lat_g) @ w_uq).reshape(b, s, MLA_HEADS, MLA_QK)
    kv = (_rms_norm(c_kv, kv_lat_g) @ w_ukv).reshape(b, s, MLA_HEADS, MLA_NOPE + MLA_V)
    k_nope, v = kv[..., :MLA_NOPE], kv[..., MLA_NOPE:]
    k_rope = jnp.broadcast_to(k_rope[:, :, None, :], (b, s, MLA_HEADS, MLA_ROPE))
    k = jnp.concatenate([k_nope, k_rope], axis=-1)
    q = _rms_norm(q, qn_g)
    k = _rms_norm(k, kn_g)
    q = jnp.concatenate([q[..., :MLA_NOPE], _rope(q[..., MLA_NOPE:])], axis=-1)
    k = jnp.concatenate([k[..., :MLA_NOPE], _rope(k[..., MLA_NOPE:])], axis=-1)
    o = _softmax_attention(q, k, v, MLA_QK ** -0.5)
    return o.reshape(b, s, MLA_HEADS * MLA_V) @ w_o


def _trunk(x, ffn1_norm, ffn1_w_in, ffn1_w_out, mix_norm, ffn2_norm, ffn2_w_in, ffn2_w_out,
           even_w_in, even_conv_w, even_q_norm, even_k_norm, even_lambda, even_subln, even_w_out,
           mla_w_down, mla_q_lat_norm, mla_kv_lat_norm, mla_w_uq, mla_w_ukv, mla_q_norm, mla_k_norm, mla_w_o):
    for l in range(DEPTH):
        x = x + 0.5 * _swiglu(_rms_norm(x, ffn1_norm[l]), ffn1_w_in[l], ffn1_w_out[l])
        h = _rms_norm(x, mix_norm[l])
        i = l // 2
        if l % 2 == 0:
            lambda_init = 0.8 - 0.6 * math.exp(-0.3 * l)
            x = x + _conv_diff_mixer(h, even_w_in[i], even_conv_w[i], even_q_norm[i], even_k_norm[i],
                                     even_lambda[i], even_subln[i], even_w_out[i], lambda_init)
        else:
            x = x + _mla_mixer(h, mla_w_down[i], mla_q_lat_norm[i], mla_kv_lat_norm[i], mla_w_uq[i],
                               mla_w_ukv[i], mla_q_norm[i], mla_k_norm[i], mla_w_o[i])
        x = x + 0.5 * _swiglu(_rms_norm(x, ffn2_norm[l]), ffn2_w_in[l], ffn2_w_out[l])
    return x


def setup_inputs(seed: int = 0) -> dict:
    key = jax.random.key(seed)
    ks = iter(jax.random.split(key, 32))

    def w(shape, fan_in):
        return jax.random.normal(next(ks), shape, jnp.float32) * (fan_in ** -0.5)

    def gain(shape):
        return 1.0 + 0.02 * jax.random.normal(next(ks), shape, jnp.float32)

    return {
        'x_prompt': jax.random.normal(next(ks), (BATCH, SEQ, D_MODEL), jnp.float32),
        'x_sample': jax.random.normal(next(ks), (DEC_BATCH, DEC_SEQ, D_MODEL), jnp.float32),
        'ffn1_norm': gain((DEPTH, D_MODEL)),
        'ffn1_w_in': w((DEPTH, D_MODEL, 2 * D_FF), D_MODEL),
        'ffn1_w_out': w((DEPTH, D_FF, D_MODEL), D_FF),
        'mix_norm': gain((DEPTH, D_MODEL)),
        'ffn2_norm': gain((DEPTH, D_MODEL)),
        'ffn2_w_in': w((DEPTH, D_MODEL, 2 * D_FF), D_MODEL),
        'ffn2_w_out': w((DEPTH, D_FF, D_MODEL), D_FF),
        'even_w_in': w((N_EVEN, D_MODEL, EVEN_IN), D_MODEL),
        'even_conv_w': w((N_EVEN, CONV_K, CONV_WIDTH), CONV_K),
        'even_q_norm': gain((N_EVEN, DIFF_HEAD_DIM)),
        'even_k_norm': gain((N_EVEN, DIFF_HEAD_DIM)),
        'even_lambda': 0.1 * jax.random.normal(next(ks), (N_EVEN, 4, DIFF_HEAD_DIM), jnp.float32),
        'even_subln': gain((N_EVEN, 2 * DIFF_HEAD_DIM)),
        'even_w_out': w((N_EVEN, CONV_WIDTH + DIFF_WIDTH, D_MODEL), CONV_WIDTH + DIFF_WIDTH),
        'mla_w_down': w((N_ODD, D_MODEL, MLA_DOWN), D_MODEL),
        'mla_q_lat_norm': gain((N_ODD, MLA_Q_RANK)),
        'mla_kv_lat_norm': gain((N_ODD, MLA_KV_RANK)),
        'mla_w_uq': w((N_ODD, MLA_Q_RANK, MLA_HEADS * MLA_QK), MLA_Q_RANK),
        'mla_w_ukv': w((N_ODD, MLA_KV_RANK, MLA_HEADS * (MLA_NOPE + MLA_V)), MLA_KV_RANK),
        'mla_q_norm': gain((N_ODD, MLA_QK)),
        'mla_k_norm': gain((N_ODD, MLA_QK)),
        'mla_w_o': w((N_ODD, MLA_HEADS * MLA_V, D_MODEL), MLA_HEADS * MLA_V),
    }


def reference(x_prompt, x_sample, ffn1_norm, ffn1_w_in, ffn1_w_out, mix_norm, ffn2_norm, ffn2_w_in, ffn2_w_out,
              even_w_in, even_conv_w, even_q_norm, even_k_norm, even_lambda, even_subln, even_w_out,
              mla_w_down, mla_q_lat_norm, mla_kv_lat_norm, mla_w_uq, mla_w_ukv, mla_q_norm, mla_k_norm, mla_w_o):
    y_prompt = _trunk(x_prompt, ffn1_norm, ffn1_w_in, ffn1_w_out, mix_norm, ffn2_norm, ffn2_w_in, ffn2_w_out,
                      even_w_in, even_conv_w, even_q_norm, even_k_norm, even_lambda, even_subln, even_w_out,
                      mla_w_down, mla_q_lat_norm, mla_kv_lat_norm, mla_w_uq, mla_w_ukv, mla_q_norm, mla_k_norm, mla_w_o)
    y_sample = _trunk(x_sample, ffn1_norm, ffn1_w_in, ffn1_w_out, mix_norm, ffn2_norm, ffn2_w_in, ffn2_w_out,
                      even_w_in, even_conv_w, even_q_norm, even_k_norm, even_lambda, even_subln, even_w_out,
                      mla_w_down, mla_q_lat_norm, mla_kv_lat_norm, mla_w_uq, mla_w_ukv, mla_q_norm, mla_k_norm, mla_w_o)
    return (y_prompt, y_sample)
```

```python
import math
import numpy as np
import ml_dtypes
import concourse.bass as bass
import concourse.mybir as mybir
from concourse.bass_utils import run_bass_kernel_spmd

F32 = mybir.dt.float32
BF16 = mybir.dt.bfloat16
U8 = mybir.dt.uint8
AF = mybir.ActivationFunctionType
ALU = mybir.AluOpType

D = 1024
KC = 8
DFF = 2816
FJ = 22
T = 512
EPS = 1e-6
ROPE_THETA = 10000.0
NCORES = 8
G = 4


class Cfg:
    def __init__(self, seq=16384, dec_seq=2048, depth=4):
        self.SEQ = seq
        self.DEC_SEQ = dec_seq
        self.DEPTH = depth
        self.NE = (depth + 1) // 2
        self.NO = depth // 2
        self.NP = seq // G
        self.NS = dec_seq
        self.NT = self.NP + self.NS
        assert self.NP % T == 0 and self.NS % T == 0


CFG = Cfg()
DBG = set()


def gain_layout(cfg):
    cols = {}
    n = [0]

    def add(name, k):
        cols[name] = n[0]
        n[0] += k

    for l in range(cfg.DEPTH):
        add(('f1', l), 8)
        add(('mx', l), 8)
        add(('f2', l), 8)
    for i in range(cfg.NE):
        add(('cw', i), 12)
        for nm in ('qA', 'qB', 'kA', 'kB', 'sub'):
            add((nm, i), 1)
    for i in range(cfg.NO):
        add(('ql', i), 3)
        add(('kvl', i), 2)
        for nm in ('qn', 'qrA', 'qrB', 'kn', 'krA', 'krB'):
            add((nm, i), 1)
    return cols, n[0]


ENGS = ('pe', 'act', 'dve', 'pool', 'sp')


class Op:
    __slots__ = ('fn', 'sig', 'idx', 'val', 'dsem', 'dinc')

    def __init__(self, fn):
        self.fn = fn
        self.sig = False
        self.val = 0
        self.dsem = None
        self.dinc = 0


class Buf:
    __slots__ = ('name', 'w', 'r', 'dsem', 'dcnt')

    def __init__(self, name):
        self.name = name
        self.w = {}
        self.r = {}
        self.dsem = None
        self.dcnt = 0


class K:
    def __init__(self, nc):
        self.nc = nc
        self.q = {e: [] for e in ENGS}
        self.waited = {e: {} for e in ENGS}
        self.esem = {e: nc.alloc_semaphore(name=f"es_{e}") for e in ENGS}
        self.dsems = []
        self.bufs = {}
        self.scope_name = 'g'
        self.scope_cnt = 0

    def scope(self, name):
        self.scope_name = name
        self.scope_cnt = 0

    def buf(self, name=None):
        if name is None:
            self.scope_cnt += 1
            name = f"{self.scope_name}_{self.scope_cnt}"
        if name not in self.bufs:
            self.bufs[name] = Buf(name)
        return self.bufs[name]

    def _wait(self, eng, key, ev):
        o = ev[2]
        if self.waited[eng].get(key, -1) >= o:
            return
        self.waited[eng][key] = o
        if ev[0] == 'e':
            ev[3].sig = True
        self.q[eng].append(('w', ev))

    def _deps(self, eng, reads, writes):
        deps = {}
        for b in reads:
            for key, ev in b.w.items():
                if key not in deps or ev[2] > deps[key][2]:
                    deps[key] = ev
        for b in writes:
            for dct in (b.w, b.r):
                for key, ev in dct.items():
                    if key == eng:
                        continue
                    if key not in deps or ev[2] > deps[key][2]:
                        deps[key] = ev
        for key, ev in deps.items():
            self._wait(eng, key, ev)

    def op(self, eng, fn, reads=(), writes=()):
        self._deps(eng, reads, writes)
        o = Op(fn)
        o.idx = len(self.q[eng])
        self.q[eng].append(o)
        ev = ('e', eng, o.idx, o)
        for b in reads:
            b.r[eng] = ev
        for b in writes:
            b.w[eng] = ev
        return o

    def dma(self, eng, out, in_, sb, reads=(), writes=(), inc=16, fn=None, slow=False):
        if sb.dsem is None:
            sb.dsem = self.nc.alloc_semaphore(name=f"ds_{len(self.dsems)}")
            self.dsems.append(sb)
        self._deps(eng, reads, writes)
        sb.dcnt += inc
        if fn is None:
            def fn(e, out=out, in_=in_, slow=slow):
                if slow:
                    return e.dma_start(out=out, in_=in_, allow_slow_non_contiguous=True)
                return e.dma_start(out=out, in_=in_)
        o = Op(fn)
        o.idx = len(self.q[eng])
        o.dsem = sb.dsem
        o.dinc = inc
        self.q[eng].append(o)
        key = ('d', id(sb))
        ev = ('d', key, sb.dcnt, sb.dsem)
        for b in reads:
            b.r[key] = ev
        for b in writes:
            b.w[key] = ev
        return ev

    def barrier(self):
        lasts = {}
        for e in ENGS:
            for it in reversed(self.q[e]):
                if isinstance(it, Op) and it.dsem is None:
                    lasts[e] = ('e', e, it.idx, it)
                    break
        for e in ENGS:
            for f, ev in lasts.items():
                if f != e:
                    self._wait(e, f, ev)
            for sb in self.dsems:
                if sb.dcnt > 0:
                    self._wait(e, ('d', id(sb)), ('d', ('d', id(sb)), sb.dcnt, sb.dsem))

    def finish(self, final_events):
        for ev in final_events:
            self._wait('sp', ev[1], ev)

    def replay(self):
        for e in ENGS:
            c = 0
            for it in self.q[e]:
                if isinstance(it, Op) and it.sig:
                    c += 1
                    it.val = c
        nc = self.nc
        K_ = self

        def run(ename, eng):
            sem = K_.esem[ename]
            for it in K_.q[ename]:
                if isinstance(it, Op):
                    ins = it.fn(eng)
                    if it.dsem is not None:
                        ins.then_inc(it.dsem, it.dinc)
                    elif it.sig:
                        ins.then_inc(sem, 1)
                else:
                    ev = it[1]
                    if ev[0] == 'e':
                        eng.wait_ge(K_.esem[ev[1]], ev[3].val)
                    else:
                        eng.wait_ge(ev[3], ev[2])

        with nc.Block() as block:
            @block.tensor
            def _(e):
                run('pe', e)

            @block.scalar
            def _(e):
                run('act', e)

            @block.vector
            def _(e):
                run('dve', e)

            @block.gpsimd
            def _(e):
                run('pool', e)

            @block.sync
            def _(e):
                run('sp', e)


class Arena:
    def __init__(self, nc, nbytes):
        self.t = nc.alloc_sbuf_tensor("arena", [128, nbytes], U8)
        self.nbytes = nbytes
        self.off = 0

    def reset(self):
        self.off = 0

    def alloc(self, shape, dtype):
        esz = 4 if dtype == F32 else 2
        n = 1
        for s in shape:
            n *= s
        nb = (n * esz + 63) // 64 * 64
        assert self.off + nb <= self.nbytes, f"arena overflow {self.off + nb} > {self.nbytes}"
        ap = self.t[:, self.off:self.off + n * esz].bitcast(dtype)
        self.off += nb
        if len(shape) == 2:
            ap = ap.rearrange("p (a b) -> p a b", a=shape[0])
        elif len(shape) == 3:
            ap = ap.rearrange("p (a b c) -> p a b c", a=shape[0], b=shape[1])
        return ap


def lambda_init(l):
    return 0.8 - 0.6 * math.exp(-0.3 * l)


def build_program(cfg):
    nc = bass.Bass("TRN2", target_bir_lowering=False)
    k = K(nc)
    L, NE, NO, NP, NS, NT = cfg.DEPTH, cfg.NE, cfg.NO, cfg.NP, cfg.NS, cfg.NT
    gcols, NG = gain_layout(cfg)
    NBP = NP // 128
    NBS = NS // 128

    def din(name, shape, dt=F32):
        return nc.dram_tensor(name, list(shape), dt, kind="ExternalInput").ap()

    def dscr(name, shape, dt):
        if 'dump' in DBG and name in ('AO', 'QTa', 'zP', 'gbD', 'KTa_s', 'V_s', 'zS', 'xres', 'QTb'):
            return nc.dram_tensor(name, list(shape), dt, kind="ExternalOutput").ap()
        return nc.dram_tensor(name, list(shape), dt, kind="Internal").ap()

    x_in = din("x_in", [NT, D])
    y_out = nc.dram_tensor("y_out", [NT, D], F32, kind="ExternalOutput").ap()
    w_f1_in = din("ffn1_w_in", [L, D, 2 * DFF])
    w_f1_out = din("ffn1_w_out", [L, DFF, D])
    w_f2_in = din("ffn2_w_in", [L, D, 2 * DFF])
    w_f2_out = din("ffn2_w_out", [L, DFF, D])
    w_e_in = din("even_w_in", [NE, D, 3072])
    w_e_out = din("even_w_out", [NE, D, D])
    if NO:
        w_m_down = din("mla_w_down", [NO, D, 704])
        w_m_uq = din("mla_w_uq", [NO, 384, 1536])
        w_m_ukv = din("mla_w_ukv", [NO, 256, 2048])
        w_m_o = din("mla_w_o", [NO, D, D])
    gains_d = din("gains", [128, NG])
    gscale_d = din("gscale", [128, NG])
    lamT_d = din("lamT", [128, 4 * NE])
    ident_d = din("ident", [128, 128])
    onesf_d = din("onesf", [128, 128])
    rswap_d = din("rswap", [128, 128])
    onesb_d = din("onesb", [128, 128], BF16)
    bd64_d = din("bd64", [128, 128], BF16)
    cos_d = din("cos_t", [128, NT])
    sin_d = din("sin_t", [128, NT])
    hmask_d = din("hmask", [128, 2 * G])
    zero_d = din("zeros", [128, 64])

    xres = dscr("xres", [KC, 128, NT], F32)
    B_xres = [k.buf(f"xres{i}") for i in range(NT // T)]
    WIN = {}
    WOUT = {}
    for l in range(L):
        for w in (1, 2):
            WIN[(l, w)] = dscr(f"win_{l}_{w}", [11, 128, 8, 2, 256], BF16)
            WOUT[(l, w)] = dscr(f"wout_{l}_{w}", [DFF, D], BF16)
    EWIN = [dscr(f"ewin{i}", [D, 3072], BF16) for i in range(NE)]
    EWOUT = [dscr(f"ewout{i}", [D, D], BF16) for i in range(NE)]
    MDOWN = [dscr(f"mdown{i}", [D, 704], BF16) for i in range(NO)]
    MUQ = [dscr(f"muq{i}", [384, 1536], BF16) for i in range(NO)]
    MUKV = [dscr(f"mukv{i}", [256, 2048], BF16) for i in range(NO)]
    MWO = [dscr(f"mwo{i}", [D, D], BF16) for i in range(NO)]
    B_W = k.buf("weights_bf16")

    HMAX = 8 if NO else 4
    QTa = dscr("QTa", [HMAX * 128, NT], BF16)
    QTb = dscr("QTb", [HMAX * 64, NT], BF16)
    KTa_p = dscr("KTa_p", [HMAX * 128, NP], BF16)
    KTb_p = dscr("KTb_p", [HMAX * 64, NP], BF16)
    V_p = dscr("V_p", [HMAX * 128, NBP * 128], BF16)
    KTa_s = dscr("KTa_s", [HMAX * 128, NS], BF16)
    KTb_s = dscr("KTb_s", [HMAX * 64, NS], BF16)
    V_s = dscr("V_s", [HMAX * 128, NBS * 128], BF16)
    KTa_g = dscr("KTa_g", [G * HMAX * 128, NP], BF16)
    KTb_g = dscr("KTb_g", [G * HMAX * 64, NP], BF16)
    V_g = dscr("V_g", [G * HMAX * 128, NBP * 128], BF16)
    AO = dscr("AO", [KC * 128, NT], BF16)
    zP = dscr("zP", [4 * 128, NP + 2], F32)
    zS = dscr("zS", [4 * 128, NS + 2], F32)
    gbD = dscr("gbD", [4 * 128, NT], F32)
    zedge = dscr("zedge", [512, 8], F32)
    zedge_g = dscr("zedge_g", [G * 512, 8], F32)
    B_Q = k.buf("QT")
    B_Kp = k.buf("K_p")
    B_Ks = k.buf("K_s")
    B_Kg = [k.buf(f"K_g{h}") for h in range(HMAX)]
    B_AO = k.buf("AO")
    B_z = k.buf("z")
    B_gb = k.buf("gb")
    B_ze = k.buf("zedge")
    B_zeg = k.buf("zedge_g")
    B_cc = k.buf("cc")
    B_in = k.buf("inputs")

    def sbt(name, shape, dt):
        return nc.alloc_sbuf_tensor("sb_" + name, shape, dt)

    ident = sbt("ident", [128, 128], F32)
    onesf = sbt("onesf", [128, 128], F32)
    rswap = sbt("rswap", [128, 128], F32)
    onesb = sbt("onesb", [128, 128], BF16)
    bd64 = sbt("bd64", [128, 128], BF16)
    gains = sbt("gains", [128, NG], F32)
    gsc = sbt("gsc", [128, NG], F32)
    gs = sbt("gs", [128, NG], F32)
    lamT = sbt("lamT", [128, 4 * NE], F32)
    lamw = sbt("lamw", [128, 8 * NE], F32)
    hmask = sbt("hmask", [128, 2 * G], F32)
    zt = sbt("zt", [128, 64], F32)
    halo = sbt("halo", [128, 8], F32)
    Eg = sbt("Eg", [128, G * 4 * 2], F32)
    B_const = k.buf("const")
    B_halo = k.buf("halo")

    ps = [nc.alloc_psum_tensor(f"ps{i}", [128, 512], F32) for i in range(8)]
    B_ps = [k.buf(f"ps{i}") for i in range(8)]

    arena = Arena(nc, 203 * 1024)

    for dst, src in ((ident, ident_d), (onesf, onesf_d), (rswap, rswap_d), (onesb, onesb_d), (bd64, bd64_d),
                     (gains, gains_d), (gsc, gscale_d), (lamT, lamT_d), (hmask, hmask_d), (zt, zero_d)):
        k.dma('sp', dst[:], src, B_const, writes=[B_const])
    k.op('dve', lambda e: e.tensor_tensor(out=gs[:], in0=gains[:], in1=gsc[:], op=ALU.mult),
         reads=[B_const], writes=[B_const])
    for i in range(NE):
        lw = lamw[:, 8 * i:8 * i + 8]
        lt = lamT[:, 4 * i:4 * i + 4]
        k.op('dve', lambda e, lw=lw, lt=lt: e.tensor_tensor(out=lw[:, 0:1], in0=lt[:, 0:1], in1=lt[:, 1:2], op=ALU.mult),
             reads=[B_const], writes=[B_const])
        k.op('dve', lambda e, lw=lw, lt=lt: e.tensor_tensor(out=lw[:, 1:2], in0=lt[:, 2:3], in1=lt[:, 3:4], op=ALU.mult),
             reads=[B_const], writes=[B_const])
        k.op('pe', lambda e, lw=lw: e.matmul(ps[0][:, 0:2], lhsT=onesf[:], rhs=lw[:, 0:2], start=True, stop=True),
             reads=[B_const], writes=[B_ps[0]])
        k.op('act', lambda e, lw=lw: e.activation(out=lw[:, 2:4], in_=ps[0][:, 0:2], func=AF.Exp),
             reads=[B_ps[0]], writes=[B_const])
        li = lambda_init(2 * i)
        k.op('dve', lambda e, lw=lw, li=li: e.tensor_scalar(out=lw[:, 4:5], in0=lw[:, 2:3], scalar1=lw[:, 3:4], scalar2=li,
                                                         op0=ALU.subtract, op1=ALU.add),
             reads=[B_const], writes=[B_const])
        k.op('dve', lambda e, lw=lw: e.tensor_scalar(out=lw[:, 5:6], in0=lw[:, 4:5], scalar1=-1.0, scalar2=None, op0=ALU.mult),
             reads=[B_const], writes=[B_const])

    def gcol(name, j=0):
        c = gcols[name] + j
        return gs[:, c:c + 1]

    arena.reset()
    CW = 4096
    cst_f = [arena.alloc([CW], F32) for _ in range(3)]
    cst_b = [arena.alloc([CW], BF16) for _ in range(3)]
    B_cf = [k.buf() for _ in range(3)]
    B_cb = [k.buf() for _ in range(3)]
    cast_i = [0]

    def cast_block(src_rows, ncols, dst_fn):
        for c0 in range(0, ncols, CW):
            c1 = min(ncols, c0 + CW)
            i = cast_i[0] % 3
            cast_i[0] += 1
            f, b = cst_f[i], cst_b[i]
            k.dma('sp', f[:, 0:c1 - c0], src_rows[:, c0:c1], B_cf[i], reads=[B_in], writes=[B_cf[i]])
            if cast_i[0] % 2 == 0:
                k.op('dve', lambda e, f=f, b=b, n=c1 - c0: e.tensor_copy(out=b[:, 0:n], in_=f[:, 0:n]),
                     reads=[B_cf[i]], writes=[B_cb[i]])
            else:
                k.op('act', lambda e, f=f, b=b, n=c1 - c0: e.copy(out=b[:, 0:n], in_=f[:, 0:n]),
                     reads=[B_cf[i]], writes=[B_cb[i]])
            res = dst_fn(c0, c1, b)
            if not isinstance(res, list):
                res = [res]
            for dst_ap, src_view in res:
                k.dma('pool', dst_ap, src_view, B_cb[i], reads=[B_cb[i]], writes=[B_W])

    def cast_natural(src2d, dst2d, rows, ncols):
        for r0 in range(0, rows, 128):
            def dst_fn(c0, c1, b, r0=r0):
                return dst2d[r0:r0 + 128, c0:c1], b[:, 0:c1 - c0]
            cast_block(src2d[r0:r0 + 128, :], ncols, dst_fn)

    def cast_win(src2d, dst5):
        for kc in range(8):
            for gu in range(2):
                def dst_fn(c0, c1, b, kc=kc, gu=gu):
                    assert c0 == 0 and c1 == DFF
                    return (dst5[:, :, kc, gu, :].rearrange("u p f -> p u f"),
                            b[:, 0:DFF].rearrange("p (u f) -> p u f", f=256))
                cast_block(src2d[kc * 128:(kc + 1) * 128, gu * DFF:(gu + 1) * DFF], DFF, dst_fn)

    def cast_ukv(src2d, dst2d):
        for r0 in range(0, 256, 128):
            def dst_fn(c0, c1, b, r0=r0):
                bv = b[:, 0:2048].rearrange("p (h s e) -> p h s e", h=8, s=2)
                return [(dst2d[r0:r0 + 128, s_ * 1024:(s_ + 1) * 1024].rearrange("p (h e) -> p h e", h=8), bv[:, :, s_, :])
                        for s_ in range(2)]
            cast_block(src2d[r0:r0 + 128, :], 2048, dst_fn)

    def emit_casts_for_layer(l):
        cast_win(w_f1_in[l], WIN[(l, 1)])
        cast_natural(w_f1_out[l], WOUT[(l, 1)], DFF, D)
        if l % 2 == 0:
            cast_natural(w_e_in[l // 2], EWIN[l // 2], D, 3072)
            cast_natural(w_e_out[l // 2], EWOUT[l // 2], D, D)
        else:
            i = l // 2
            cast_natural(w_m_down[i], MDOWN[i], D, 704)
            cast_natural(w_m_uq[i], MUQ[i], 384, 1536)
            cast_ukv(w_m_ukv[i], MUKV[i])
            cast_natural(w_m_o[i], MWO[i], D, D)
        cast_win(w_f2_in[l], WIN[(l, 2)])
        cast_natural(w_f2_out[l], WOUT[(l, 2)], DFF, D)

    for l in range(L):
        emit_casts_for_layer(l)
    k.barrier()

    psrr = [0]

    def psum_next():
        i = psrr[0] % 8
        psrr[0] += 1
        return ps[i], B_ps[i]

    def row_pass(p):
        arena.reset()
        k.scope('rp')
        xT = [arena.alloc([8, T], F32) for _ in range(2)]
        B_x = [k.buf() for _ in range(2)]
        NSLOT = 3
        wr = [arena.alloc([5632], BF16) for _ in range(NSLOT)]
        B_wr = [k.buf() for _ in range(NSLOT)]
        hT = arena.alloc([8, T], BF16)
        B_h = k.buf("hT")
        AT = arena.alloc([FJ, T], BF16)
        B_AT = k.buf("AT")
        sq = arena.alloc([8, T], BF16)
        B_sq = k.buf("sq")
        rstd = arena.alloc([T], F32)
        B_rstd = k.buf("rstd")
        sg = [arena.alloc([T], F32) for _ in range(2)]
        B_sg = [k.buf() for _ in range(2)]
        r1a = arena.alloc([8 * T], BF16)
        r1b = arena.alloc([4 * (T + 2) + 30], F32)
        r1c = arena.alloc([4 * T], F32)
        B_r1a, B_r1b, B_r1c = k.buf("r1a"), k.buf("r1b"), k.buf("r1c")
        ct = [arena.alloc([T], F32) for _ in range(2)]
        B_ct = [k.buf() for _ in range(2)]
        cosT = arena.alloc([T], F32)
        sinT = arena.alloc([T], F32)
        B_cs = k.buf("cossin")
        p4off = arena.off

        l_out = p - 1
        l_in = p if p < L else None

        def units_for_tile():
            us = []
            if l_out >= 0:
                wsrc = EWOUT[l_out // 2] if l_out % 2 == 0 else MWO[l_out // 2]
                for hh in range(2):
                    us.append(('mo', wsrc.rearrange("(ic p) d -> p ic d", p=128)[:, :, hh * 512:(hh + 1) * 512], [8, 512]))
                us += ffn_units(l_out, 2)
            if l_in is not None:
                us += ffn_units(l_in, 1)
                if l_in % 2 == 0:
                    w = EWIN[l_in // 2].rearrange("(kc p) n -> p kc n", p=128)
                    for u in range(6):
                        us.append(('ein', w[:, :, u * 512:(u + 1) * 512], [8, 512]))
                else:
                    i = l_in // 2
                    us.append(('mdown', MDOWN[i].rearrange("(kc p) n -> p kc n", p=128), [8, 704]))
                    us.append(('muq', MUQ[i].rearrange("(kc p) n -> p kc n", p=128), [3, 1536]))
                    us.append(('mukv', MUKV[i].rearrange("(kc p) n -> p kc n", p=128), [2, 2048]))
            return us

        def ffn_units(l, w):
            us = []
            for u in range(11):
                us.append(('win', WIN[(l, w)][u].rearrange("p kc gu f -> p (kc gu f)"), [4096]))
            wo = WOUT[(l, w)].rearrange("(j p) d -> p j d", p=128)
            for cp in range(4):
                us.append(('wout', wo[:, :, cp * 256:(cp + 1) * 256], [FJ, 256]))
            return us

        tiles = list(range(NT // T))
        tile_units = units_for_tile()
        NU = len(tile_units)
        total_units = NU * len(tiles)
        wstate = {'loaded': 0}

        def wview(slot, shape):
            n = 1
            for s_ in shape:
                n *= s_
            v = wr[slot][:, 0:n]
            if len(shape) == 2:
                v = v.rearrange("p (a b) -> p a b", a=shape[0])
            return v

        def w_ensure(upto):
            while wstate['loaded'] <= min(upto, total_units - 1):
                g = wstate['loaded']
                kind, src, shape = tile_units[g % NU]
                slot = g % NSLOT
                k.dma('sp', wview(slot, shape), src, B_wr[slot], reads=[B_W], writes=[B_wr[slot]])
                wstate['loaded'] += 1

        wcur = {'g': 0}

        def w_next(kind_expect, hold=0):
            g = wcur['g']
            w_ensure(g + NSLOT - 1 - hold)
            kind, src, shape = tile_units[g % NU]
            assert kind == kind_expect, (kind, kind_expect)
            slot = g % NSLOT
            wcur['g'] += 1
            return wview(slot, shape), B_wr[slot]

        def rms_to_h(x, Bx, gname, nchunk=8, n=D, dst=None, Bdst=None, src_list=None):
            dst = hT if dst is None else dst
            Bdst = B_h if Bdst is None else Bdst
            for c in range(nchunk):
                k.op('dve', lambda e, c=c: e.tensor_tensor(out=sq[:, c, :], in0=x[:, c, :], in1=x[:, c, :], op=ALU.mult),
                     reads=[Bx], writes=[B_sq])
            pt, Bp = psum_next()
            for c in range(nchunk):
                k.op('pe', lambda e, c=c, pt=pt: e.matmul(pt[:], lhsT=onesb[:], rhs=sq[:, c, :], start=(c == 0), stop=(c == nchunk - 1)),
                     reads=[B_sq, B_const], writes=[Bp])
            k.op('dve', lambda e, pt=pt: e.tensor_scalar(out=rstd[:], in0=pt[:], scalar1=float(n * EPS), scalar2=None, op0=ALU.add),
                 reads=[Bp], writes=[B_rstd])
            k.op('act', lambda e: e.activation(out=rstd[:], in_=rstd[:], func=AF.Ln), reads=[B_rstd], writes=[B_rstd])
            k.op('act', lambda e: e.activation(out=rstd[:], in_=rstd[:], func=AF.Exp, scale=-0.5), reads=[B_rstd], writes=[B_rstd])
            for c in range(nchunk):
                k.op('dve', lambda e, c=c: e.scalar_tensor_tensor(out=dst[:, c, :], in0=x[:, c, :], scalar=gcol(gname, c), in1=rstd[:],
                                                                op0=ALU.mult, op1=ALU.mult),
                     reads=[Bx, B_rstd, B_const], writes=[Bdst])

        def ffn(x, Bx, l, w):
            rms_to_h(x, Bx, ('f1' if w == 1 else 'f2', l))
            for u in range(11):
                wv, Bw = w_next('win')
                wv = wv.rearrange("p (kc gu f) -> p kc gu f", kc=8, gu=2)
                for jj in range(2):
                    j = 2 * u + jj
                    pg, Bpg = psum_next()
                    pu, Bpu = psum_next()
                    for gu, (pp, Bpp) in enumerate(((pg, Bpg), (pu, Bpu))):
                        for kc in range(8):
                            k.op('pe', lambda e, pp=pp, wv=wv, kc=kc, gu=gu, jj=jj: e.matmul(
                                pp[:], lhsT=wv[:, kc, gu, jj * 128:(jj + 1) * 128], rhs=hT[:, kc, :],
                                start=(kc == 0), stop=(kc == 7)), reads=[Bw, B_h], writes=[Bpp])
                    s_, Bs = sg[j % 2], B_sg[j % 2]
                    k.op('act', lambda e, s_=s_, pg=pg: e.activation(out=s_[:], in_=pg[:], func=AF.Silu),
                         reads=[Bpg], writes=[Bs])
                    k.op('dve', lambda e, s_=s_, pu=pu, j=j: e.tensor_tensor(out=AT[:, j, :], in0=s_[:], in1=pu[:], op=ALU.mult),
                         reads=[Bs, Bpu], writes=[B_AT])
            for cp in range(4):
                wv, Bw = w_next('wout')
                for cc in range(2):
                    c = 2 * cp + cc
                    py, Bpy = psum_next()
                    for j in range(FJ):
                        k.op('pe', lambda e, py=py, wv=wv, j=j, cc=cc: e.matmul(
                            py[:], lhsT=wv[:, j, cc * 128:(cc + 1) * 128], rhs=AT[:, j, :],
                            start=(j == 0), stop=(j == FJ - 1)), reads=[Bw, B_AT], writes=[Bpy])
                    k.op('dve', lambda e, py=py, c=c: e.scalar_tensor_tensor(out=x[:, c, :], in0=py[:], scalar=0.5, in1=x[:, c, :],
                                                                         op0=ALU.mult, op1=ALU.add),
                         reads=[Bpy, Bx], writes=[Bx])

        def mixer_out(x, Bx, l, t0, is_prompt, tloc):
            ao = r1a.rearrange("p (c t) -> p c t", c=8)
            if l % 2 == 0:
                i = l // 2
                k.dma('sp', ao[:, 4:8, :], AO.rearrange("(c p) n -> p c n", p=128)[:, 4:8, t0:t0 + T], B_r1a,
                      reads=[B_AO], writes=[B_r1a])
                zsrc = (zP if is_prompt else zS).rearrange("(c p) n -> p c n", p=128)
                zt_ = r1b[:, 0:4 * (T + 2)].rearrange("p (c t) -> p c t", c=4)
                k.dma('sp', zt_, zsrc[:, :, tloc:tloc + T + 2], B_r1b, reads=[B_z], writes=[B_r1b])
                gbt = r1c.rearrange("p (c t) -> p c t", c=4)
                k.dma('sp', gbt, gbD.rearrange("(c p) n -> p c n", p=128)[:, :, t0:t0 + T], B_r1c, reads=[B_gb], writes=[B_r1c])
                for c in range(4):
                    a, Ba = ct[0], B_ct[0]
                    b, Bb = ct[1], B_ct[1]
                    k.op('dve', lambda e, c=c, a=a: e.tensor_scalar(out=a[:], in0=zt_[:, c, 1:T + 1], scalar1=gcol(('cw', i), 4 + c), scalar2=None, op0=ALU.mult),
                         reads=[B_r1b, B_const], writes=[Ba])
                    k.op('dve', lambda e, c=c, a=a, b=b: e.scalar_tensor_tensor(out=b[:], in0=zt_[:, c, 0:T], scalar=gcol(('cw', i), c), in1=a[:], op0=ALU.mult, op1=ALU.add),
                         reads=[B_r1b, Ba, B_const], writes=[Bb])
                    k.op('dve', lambda e, c=c, a=a, b=b: e.scalar_tensor_tensor(out=a[:], in0=zt_[:, c, 2:T + 2], scalar=gcol(('cw', i), 8 + c), in1=b[:], op0=ALU.mult, op1=ALU.add),
                         reads=[B_r1b, Bb, B_const], writes=[Ba])
                    k.op('dve', lambda e, c=c, a=a: e.tensor_tensor(out=ao[:, c, :], in0=a[:], in1=gbt[:, c, :], op=ALU.mult),
                         reads=[Ba, B_r1c], writes=[B_r1a])
            else:
                k.dma('sp', ao, AO.rearrange("(c p) n -> p c n", p=128)[:, :, t0:t0 + T], B_r1a, reads=[B_AO], writes=[B_r1a])
            for hh in range(2):
                wv, Bw = w_next('mo')
                for cc in range(4):
                    c = hh * 4 + cc
                    py, Bpy = psum_next()
                    for ic in range(8):
                        k.op('pe', lambda e, py=py, wv=wv, ic=ic, cc=cc: e.matmul(
                            py[:], lhsT=wv[:, ic, cc * 128:(cc + 1) * 128], rhs=ao[:, ic, :], start=(ic == 0), stop=(ic == 7)),
                            reads=[Bw, B_r1a], writes=[Bpy])
                    k.op('dve', lambda e, py=py, c=c: e.tensor_tensor(out=x[:, c, :], in0=py[:], in1=x[:, c, :], op=ALU.add),
                         reads=[Bpy, Bx], writes=[Bx])

        def load_cossin(t0):
            k.dma('sp', cosT[:], cos_d[:, t0:t0 + T], B_cs, reads=[B_in], writes=[B_cs])
            k.dma('sp', sinT[:], sin_d[:, t0:t0 + T], B_cs, reads=[B_in], writes=[B_cs])

        if l_in is not None and l_in % 2 == 0:
            arena.off = p4off
            gc_sb = [arena.alloc([T], F32) for _ in range(2)]
            xq = [arena.alloc([T], F32) for _ in range(2)]
            sqb = [arena.alloc([T], BF16) for _ in range(2)]
            rs = [arena.alloc([T], F32) for _ in range(2)]
            t1 = [arena.alloc([T], F32) for _ in range(2)]
            t2 = [arena.alloc([T], F32) for _ in range(2)]
            B_gc = [k.buf() for _ in range(2)]
            B_xq = [k.buf() for _ in range(2)]
            B_sqb = [k.buf() for _ in range(2)]
            B_rs = [k.buf() for _ in range(2)]
            B_t1 = [k.buf() for _ in range(2)]
            B_t2 = [k.buf() for _ in range(2)]
            gb_st = arena.alloc([4, T], F32)
            z_st = arena.alloc([4, T], F32)
            QT_st = arena.alloc([4, T], BF16)
            KT_st = arena.alloc([4, T], BF16)
            V_st = arena.alloc([4, 4, 128], BF16)
            B_gbst, B_zst, B_QTst, B_KTst, B_Vst = (k.buf() for _ in range(5))
        elif l_in is not None:
            arena.off = p4off
            cq = arena.alloc([3, T], F32)
            ckv = arena.alloc([2, T], F32)
            kr = arena.alloc([T], F32)
            krr = arena.alloc([T], F32)
            cqn = arena.alloc([3, T], BF16)
            ckvn = arena.alloc([2, T], BF16)
            sqr_k = arena.alloc([T], BF16)
            B_cq, B_ckv, B_kr, B_krr, B_cqn, B_ckvn, B_sqrk = (k.buf() for _ in range(7))
            qn = [arena.alloc([T], F32) for _ in range(2)]
            qr = [arena.alloc([T], F32) for _ in range(2)]
            sqn = [arena.alloc([T], BF16) for _ in range(2)]
            sqr = [arena.alloc([T], BF16) for _ in range(2)]
            rs = [arena.alloc([T], F32) for _ in range(2)]
            t1 = [arena.alloc([T], F32) for _ in range(2)]
            t2 = [arena.alloc([T], F32) for _ in range(2)]
            B_qn = [k.buf() for _ in range(2)]
            B_qr = [k.buf() for _ in range(2)]
            B_sqn = [k.buf() for _ in range(2)]
            B_sqr = [k.buf() for _ in range(2)]
            B_rs = [k.buf() for _ in range(2)]
            B_t1 = [k.buf() for _ in range(2)]
            B_t2 = [k.buf() for _ in range(2)]
            QTa_st = AT[:, 0:8, :]
            KTa_st = AT[:, 8:16, :]
            QTb_st = r1a.rearrange("p (c t) -> p c t", c=8)
            KTb_st = r1c.bitcast(BF16).rearrange("p (c t) -> p c t", c=8)[:, :, 0:T]
            Vm_st = r1b[:, 0:2048].bitcast(BF16).rearrange("p (h b e) -> p h b e", h=8, b=4)
            B_KTbst, B_Vmst = B_r1c, B_r1b

        def do_proj_even(x, Bx, l, t0, is_prompt, tloc):
            i = l // 2
            rms_to_h(x, Bx, ('mx', l))
            load_cossin(t0)
            wv, Bw = w_next('ein')
            for c in range(4):
                pp, Bp = psum_next()
                for kc in range(8):
                    k.op('pe', lambda e, pp=pp, wv=wv, kc=kc, c=c: e.matmul(pp[:], lhsT=wv[:, kc, c * 128:(c + 1) * 128], rhs=hT[:, kc, :],
                                                                        start=(kc == 0), stop=(kc == 7)), reads=[Bw, B_h], writes=[Bp])
                k.op('act', lambda e, pp=pp, c=c: e.copy(out=gb_st[:, c, :], in_=pp[:]), reads=[Bp], writes=[B_gbst])
            k.dma('pool', gbD.rearrange("(c p) n -> p c n", p=128)[:, :, t0:t0 + T], gb_st, B_gbst, reads=[B_gbst], writes=[B_gb])
            wvc, Bwc = w_next('ein')
            wvu, Bwu = w_next('ein', hold=1)
            for c in range(4):
                pc, Bpc = psum_next()
                pu, Bpu = psum_next()
                for (pp, Bp, wv, Bw) in ((pc, Bpc, wvc, Bwc), (pu, Bpu, wvu, Bwu)):
                    for kc in range(8):
                        k.op('pe', lambda e, pp=pp, wv=wv, kc=kc, c=c: e.matmul(pp[:], lhsT=wv[:, kc, c * 128:(c + 1) * 128], rhs=hT[:, kc, :],
                                                                            start=(kc == 0), stop=(kc == 7)), reads=[Bw, B_h], writes=[Bp])
                g_, Bg = gc_sb[c % 2], B_gc[c % 2]
                k.op('act', lambda e, pc=pc, g_=g_: e.copy(out=g_[:], in_=pc[:]), reads=[Bpc], writes=[Bg])
                k.op('dve', lambda e, pu=pu, g_=g_, c=c: e.tensor_tensor(out=z_st[:, c, :], in0=g_[:], in1=pu[:], op=ALU.mult),
                     reads=[Bg, Bpu], writes=[B_zst])
            zdst = (zP if is_prompt else zS).rearrange("(c p) n -> p c n", p=128)
            k.dma('pool', zdst[:, :, 1 + tloc:1 + tloc + T], z_st, B_zst, reads=[B_zst], writes=[B_z])
            for which, (st, Bst, gA, gB) in enumerate(((QT_st, B_QTst, ('qA', i), ('qB', i)), (KT_st, B_KTst, ('kA', i), ('kB', i)))):
                wv, Bw = w_next('ein')
                for c in range(4):
                    ii = c % 2
                    pp, Bp = psum_next()
                    for kc in range(8):
                        k.op('pe', lambda e, pp=pp, wv=wv, kc=kc, c=c: e.matmul(pp[:], lhsT=wv[:, kc, c * 128:(c + 1) * 128], rhs=hT[:, kc, :],
                                                                            start=(kc == 0), stop=(kc == 7)), reads=[Bw, B_h], writes=[Bp])
                    k.op('act', lambda e, pp=pp, ii=ii: e.copy(out=xq[ii][:], in_=pp[:]), reads=[Bp], writes=[B_xq[ii]])
                    k.op('dve', lambda e, ii=ii: e.tensor_tensor(out=sqb[ii][:], in0=xq[ii][:], in1=xq[ii][:], op=ALU.mult),
                         reads=[B_xq[ii]], writes=[B_sqb[ii]])
                    pss, Bpss = psum_next()
                    k.op('pe', lambda e, pss=pss, ii=ii: e.matmul(pss[:], lhsT=bd64[:], rhs=sqb[ii][:], start=True, stop=True),
                         reads=[B_sqb[ii], B_const], writes=[Bpss])
                    prot, Bprot = psum_next()
                    k.op('pe', lambda e, prot=prot, ii=ii: e.matmul(prot[:], lhsT=rswap[:], rhs=xq[ii][:], start=True, stop=True),
                         reads=[B_xq[ii], B_const], writes=[Bprot])
                    k.op('dve', lambda e, pss=pss, ii=ii: e.tensor_scalar(out=rs[ii][:], in0=pss[:], scalar1=float(64 * EPS), scalar2=None, op0=ALU.add),
                         reads=[Bpss], writes=[B_rs[ii]])
                    k.op('act', lambda e, ii=ii: e.activation(out=rs[ii][:], in_=rs[ii][:], func=AF.Ln), reads=[B_rs[ii]], writes=[B_rs[ii]])
                    k.op('act', lambda e, ii=ii: e.activation(out=rs[ii][:], in_=rs[ii][:], func=AF.Exp, scale=-0.5), reads=[B_rs[ii]], writes=[B_rs[ii]])
                    k.op('dve', lambda e, ii=ii, gA=gA: e.scalar_tensor_tensor(out=t1[ii][:], in0=xq[ii][:], scalar=gcol(gA), in1=cosT[:],
                                                                           op0=ALU.mult, op1=ALU.mult),
                         reads=[B_xq[ii], B_cs, B_const], writes=[B_t1[ii]])
                    k.op('dve', lambda e, ii=ii, gB=gB, prot=prot: e.scalar_tensor_tensor(out=t2[ii][:], in0=prot[:], scalar=gcol(gB), in1=sinT[:],
                                                                                      op0=ALU.mult, op1=ALU.mult),
                         reads=[Bprot, B_cs, B_const], writes=[B_t2[ii]])
                    k.op('dve', lambda e, ii=ii: e.tensor_tensor(out=t1[ii][:], in0=t1[ii][:], in1=t2[ii][:], op=ALU.add),
                         reads=[B_t1[ii], B_t2[ii]], writes=[B_t1[ii]])
                    k.op('dve', lambda e, ii=ii, st=st, c=c: e.tensor_tensor(out=st[:, c, :], in0=t1[ii][:], in1=rs[ii][:], op=ALU.mult),
                         reads=[B_t1[ii], B_rs[ii]], writes=[Bst])
                if which == 0:
                    k.dma('pool', QTa[0:512, :].rearrange("(c p) n -> p c n", p=128)[:, :, t0:t0 + T], st, Bst, reads=[Bst], writes=[B_Q])
                else:
                    dstK = (KTa_p if is_prompt else KTa_s)[0:512, :].rearrange("(c p) n -> p c n", p=128)
                    k.dma('pool', dstK[:, :, tloc:tloc + T], st, Bst, reads=[Bst], writes=[B_Kp if is_prompt else B_Ks])
            wv, Bw = w_next('ein')
            for b in range(4):
                pp, Bp = psum_next()
                for kc in range(8):
                    k.op('pe', lambda e, pp=pp, wv=wv, kc=kc, b=b: e.matmul(pp[:], lhsT=hT[:, kc, b * 128:(b + 1) * 128], rhs=wv[:, kc, :],
                                                                        start=(kc == 0), stop=(kc == 7)), reads=[Bw, B_h], writes=[Bp])
                k.op('act', lambda e, pp=pp, b=b: e.copy(out=V_st[:, :, b, :], in_=pp[:].rearrange("p (h e) -> p h e", h=4)),
                     reads=[Bp], writes=[B_Vst])
            vd = (V_p if is_prompt else V_s)[0:512, :].rearrange("(h p) (b e) -> p h b e", p=128, e=128)
            b0 = tloc // 128
            k.dma('pool', vd[:, :, b0:b0 + 4, :], V_st, B_Vst, reads=[B_Vst], writes=[B_Kp if is_prompt else B_Ks])

        def do_proj_mla(x, Bx, l, t0, is_prompt, tloc):
            i = l // 2
            rms_to_h(x, Bx, ('mx', l))
            load_cossin(t0)
            wv, Bw = w_next('mdown')
            for c in range(6):
                pp, Bp = psum_next()
                m = 128 if c < 5 else 64
                for kc in range(8):
                    k.op('pe', lambda e, pp=pp, wv=wv, kc=kc, c=c, m=m: e.matmul(pp[0:m, :], lhsT=wv[:, kc, c * 128:c * 128 + m], rhs=hT[:, kc, :],
                                                                             start=(kc == 0), stop=(kc == 7)), reads=[Bw, B_h], writes=[Bp])
                if c < 3:
                    k.op('act', lambda e, pp=pp, c=c: e.copy(out=cq[:, c, :], in_=pp[:]), reads=[Bp], writes=[B_cq])
                elif c < 5:
                    k.op('act', lambda e, pp=pp, c=c: e.copy(out=ckv[:, c - 3, :], in_=pp[:]), reads=[Bp], writes=[B_ckv])
                else:
                    k.op('act', lambda e, pp=pp: e.copy(out=kr[0:64, :], in_=pp[0:64, :]), reads=[Bp], writes=[B_kr])
            rms_to_h(cq, B_cq, ('ql', i), nchunk=3, n=384, dst=cqn, Bdst=B_cqn)
            rms_to_h(ckv, B_ckv, ('kvl', i), nchunk=2, n=256, dst=ckvn, Bdst=B_ckvn)
            k.op('dve', lambda e: e.tensor_tensor(out=sqr_k[0:64, :], in0=kr[0:64, :], in1=kr[0:64, :], op=ALU.mult),
                 reads=[B_kr], writes=[B_sqrk])
            prot, Bprot = psum_next()
            k.op('pe', lambda e, prot=prot: e.matmul(prot[0:64, :], lhsT=rswap[0:64, 0:64], rhs=kr[0:64, :], start=True, stop=True),
                 reads=[B_kr, B_const], writes=[Bprot])
            k.op('dve', lambda e: e.scalar_tensor_tensor(out=krr[0:64, :], in0=kr[0:64, :], scalar=gs[0:64, gcols[('krA', i)]:gcols[('krA', i)] + 1],
                                                       in1=cosT[0:64, :], op0=ALU.mult, op1=ALU.mult),
                 reads=[B_kr, B_cs, B_const], writes=[B_krr])
            k.op('dve', lambda e, prot=prot: e.scalar_tensor_tensor(out=t2[0][0:64, :], in0=prot[0:64, :], scalar=gs[0:64, gcols[('krB', i)]:gcols[('krB', i)] + 1],
                                                                  in1=sinT[0:64, :], op0=ALU.mult, op1=ALU.mult),
                 reads=[Bprot, B_cs, B_const], writes=[B_t2[0]])
            k.op('dve', lambda e: e.tensor_tensor(out=krr[0:64, :], in0=krr[0:64, :], in1=t2[0][0:64, :], op=ALU.add),
                 reads=[B_krr, B_t2[0]], writes=[B_krr])
            wv, Bw = w_next('muq')
            for h in range(8):
                ii = h % 2
                pn, Bpn = psum_next()
                pr, Bpr = psum_next()
                for kc in range(3):
                    k.op('pe', lambda e, pn=pn, wv=wv, kc=kc, h=h: e.matmul(pn[:], lhsT=wv[:, kc, h * 192:h * 192 + 128], rhs=cqn[:, kc, :],
                                                                        start=(kc == 0), stop=(kc == 2)), reads=[Bw, B_cqn], writes=[Bpn])
                for kc in range(3):
                    k.op('pe', lambda e, pr=pr, wv=wv, kc=kc, h=h: e.matmul(pr[0:64, :], lhsT=wv[:, kc, h * 192 + 128:h * 192 + 192], rhs=cqn[:, kc, :],
                                                                        start=(kc == 0), stop=(kc == 2)), reads=[Bw, B_cqn], writes=[Bpr])
                k.op('act', lambda e, pn=pn, ii=ii: e.copy(out=qn[ii][:], in_=pn[:]), reads=[Bpn], writes=[B_qn[ii]])
                k.op('act', lambda e, pr=pr, ii=ii: e.copy(out=qr[ii][0:64, :], in_=pr[0:64, :]), reads=[Bpr], writes=[B_qr[ii]])
                k.op('dve', lambda e, ii=ii: e.tensor_tensor(out=sqn[ii][:], in0=qn[ii][:], in1=qn[ii][:], op=ALU.mult),
                     reads=[B_qn[ii]], writes=[B_sqn[ii]])
                k.op('dve', lambda e, ii=ii: e.tensor_tensor(out=sqr[ii][0:64, :], in0=qr[ii][0:64, :], in1=qr[ii][0:64, :], op=ALU.mult),
                     reads=[B_qr[ii]], writes=[B_sqr[ii]])
                pss, Bpss = psum_next()
                k.op('pe', lambda e, pss=pss, ii=ii: e.matmul(pss[:], lhsT=onesb[:], rhs=sqn[ii][:], start=True, stop=False),
                     reads=[B_sqn[ii], B_const], writes=[Bpss])
                k.op('pe', lambda e, pss=pss, ii=ii: e.matmul(pss[:], lhsT=onesb[0:64, :], rhs=sqr[ii][0:64, :], start=False, stop=True),
                     reads=[B_sqr[ii], B_const], writes=[Bpss])
                prot, Bprot = psum_next()
                k.op('pe', lambda e, prot=prot, ii=ii: e.matmul(prot[0:64, :], lhsT=rswap[0:64, 0:64], rhs=qr[ii][0:64, :], start=True, stop=True),
                     reads=[B_qr[ii], B_const], writes=[Bprot])
                k.op('dve', lambda e, pss=pss, ii=ii: e.tensor_scalar(out=rs[ii][:], in0=pss[:], scalar1=float(192 * EPS), scalar2=None, op0=ALU.add),
                     reads=[Bpss], writes=[B_rs[ii]])
                k.op('act', lambda e, ii=ii: e.activation(out=rs[ii][:], in_=rs[ii][:], func=AF.Ln), reads=[B_rs[ii]], writes=[B_rs[ii]])
                k.op('act', lambda e, ii=ii: e.activation(out=rs[ii][:], in_=rs[ii][:], func=AF.Exp, scale=-0.5), reads=[B_rs[ii]], writes=[B_rs[ii]])
                k.op('dve', lambda e, ii=ii, h=h: e.scalar_tensor_tensor(out=QTa_st[:, h, :], in0=qn[ii][:], scalar=gcol(('qn', i)), in1=rs[ii][:],
                                                                     op0=ALU.mult, op1=ALU.mult),
                     reads=[B_qn[ii], B_rs[ii], B_const], writes=[B_AT])
                k.op('dve', lambda e, ii=ii: e.scalar_tensor_tensor(out=t1[ii][0:64, :], in0=qr[ii][0:64, :], scalar=gs[0:64, gcols[('qrA', i)]:gcols[('qrA', i)] + 1],
                                                                  in1=cosT[0:64, :], op0=ALU.mult, op1=ALU.mult),
                     reads=[B_qr[ii], B_cs, B_const], writes=[B_t1[ii]])
                k.op('dve', lambda e, ii=ii, prot=prot: e.scalar_tensor_tensor(out=t2[ii][0:64, :], in0=prot[0:64, :], scalar=gs[0:64, gcols[('qrB', i)]:gcols[('qrB', i)] + 1],
                                                                            in1=sinT[0:64, :], op0=ALU.mult, op1=ALU.mult),
                     reads=[Bprot, B_cs, B_const], writes=[B_t2[ii]])
                k.op('dve', lambda e, ii=ii: e.tensor_tensor(out=t1[ii][0:64, :], in0=t1[ii][0:64, :], in1=t2[ii][0:64, :], op=ALU.add),
                     reads=[B_t1[ii], B_t2[ii]], writes=[B_t1[ii]])
                k.op('dve', lambda e, ii=ii, h=h: e.tensor_tensor(out=QTb_st[0:64, h, :], in0=t1[ii][0:64, :], in1=rs[ii][0:64, :], op=ALU.mult),
                     reads=[B_t1[ii], B_rs[ii]], writes=[B_r1a])
            k.dma('pool', QTa.rearrange("(h p) n -> p h n", p=128)[:, :, t0:t0 + T], QTa_st, B_AT, reads=[B_AT], writes=[B_Q])
            k.dma('pool', QTb.rearrange("(h p) n -> p h n", p=64)[:, :, t0:t0 + T], QTb_st[0:64, :, :], B_r1a, reads=[B_r1a], writes=[B_Q])
            wv, Bw = w_next('mukv')
            for h in range(8):
                ii = h % 2
                pn, Bpn = psum_next()
                for kc in range(2):
                    k.op('pe', lambda e, pn=pn, wv=wv, kc=kc, h=h: e.matmul(pn[:], lhsT=wv[:, kc, h * 128:(h + 1) * 128], rhs=ckvn[:, kc, :],
                                                                        start=(kc == 0), stop=(kc == 1)), reads=[Bw, B_ckvn], writes=[Bpn])
                k.op('act', lambda e, pn=pn, ii=ii: e.copy(out=qn[ii][:], in_=pn[:]), reads=[Bpn], writes=[B_qn[ii]])
                k.op('dve', lambda e, ii=ii: e.tensor_tensor(out=sqn[ii][:], in0=qn[ii][:], in1=qn[ii][:], op=ALU.mult),
                     reads=[B_qn[ii]], writes=[B_sqn[ii]])
                pss, Bpss = psum_next()
                k.op('pe', lambda e, pss=pss, ii=ii: e.matmul(pss[:], lhsT=onesb[:], rhs=sqn[ii][:], start=True, stop=False),
                     reads=[B_sqn[ii], B_const], writes=[Bpss])
                k.op('pe', lambda e, pss=pss: e.matmul(pss[:], lhsT=onesb[0:64, :], rhs=sqr_k[0:64, :], start=False, stop=True),
                     reads=[B_sqrk, B_const], writes=[Bpss])
                k.op('dve', lambda e, pss=pss, ii=ii: e.tensor_scalar(out=rs[ii][:], in0=pss[:], scalar1=float(192 * EPS), scalar2=None, op0=ALU.add),
                     reads=[Bpss], writes=[B_rs[ii]])
                k.op('act', lambda e, ii=ii: e.activation(out=rs[ii][:], in_=rs[ii][:], func=AF.Ln), reads=[B_rs[ii]], writes=[B_rs[ii]])
                k.op('act', lambda e, ii=ii: e.activation(out=rs[ii][:], in_=rs[ii][:], func=AF.Exp, scale=-0.5), reads=[B_rs[ii]], writes=[B_rs[ii]])
                k.op('dve', lambda e, ii=ii, h=h: e.scalar_tensor_tensor(out=KTa_st[:, h, :], in0=qn[ii][:], scalar=gcol(('kn', i)), in1=rs[ii][:],
                                                                     op0=ALU.mult, op1=ALU.mult),
                     reads=[B_qn[ii], B_rs[ii], B_const], writes=[B_AT])
                k.op('dve', lambda e, ii=ii, h=h: e.tensor_tensor(out=KTb_st[0:64, h, :], in0=krr[0:64, :], in1=rs[ii][0:64, :], op=ALU.mult),
                     reads=[B_krr, B_rs[ii]], writes=[B_KTbst])
            dKa = (KTa_p if is_prompt else KTa_s).rearrange("(h p) n -> p h n", p=128)
            dKb = (KTb_p if is_prompt else KTb_s).rearrange("(h p) n -> p h n", p=64)
            BK = B_Kp if is_prompt else B_Ks
            k.dma('pool', dKa[:, :, tloc:tloc + T], KTa_st, B_AT, reads=[B_AT], writes=[BK])
            k.dma('pool', dKb[:, :, tloc:tloc + T], KTb_st[0:64, :, :], B_KTbst, reads=[B_KTbst], writes=[BK])
            for b in range(4):
                for hf in range(2):
                    pp, Bp = psum_next()
                    for kc in range(2):
                        k.op('pe', lambda e, pp=pp, wv=wv, kc=kc, b=b, hf=hf: e.matmul(
                            pp[:], lhsT=ckvn[:, kc, b * 128:(b + 1) * 128], rhs=wv[:, kc, 1024 + hf * 512:1024 + (hf + 1) * 512],
                            start=(kc == 0), stop=(kc == 1)), reads=[Bw, B_ckvn], writes=[Bp])
                    k.op('act', lambda e, pp=pp, b=b, hf=hf: e.copy(out=Vm_st[:, hf * 4:(hf + 1) * 4, b, :], in_=pp[:].rearrange("p (h e) -> p h e", h=4)),
                         reads=[Bp], writes=[B_Vmst])
            vd = (V_p if is_prompt else V_s).rearrange("(h p) (b e) -> p h b e", p=128, e=128)
            b0 = tloc // 128
            k.dma('pool', vd[:, :, b0:b0 + 4, :], Vm_st, B_Vmst, reads=[B_Vmst], writes=[BK])

        ntile = len(tiles)

        def load_x(ti):
            t0 = ti * T
            xb, Bxb = xT[ti % 2], B_x[ti % 2]
            if p == 0:
                return
            k.dma('sp', xb, xres[:, :, t0:t0 + T].rearrange("c p n -> p c n"), Bxb, reads=[B_xres[ti]], writes=[Bxb])

        if p > 0:
            load_x(0)
        for ti in tiles:
            t0 = ti * T
            is_prompt = t0 < NP
            tloc = t0 if is_prompt else t0 - NP
            x, Bx = xT[ti % 2], B_x[ti % 2]
            if p == 0:
                tm = r1a.bitcast(F32)
                for half in range(2):
                    tmv = tm.rearrange("p (b f) -> p b f", b=4)
                    k.dma('sp', tmv, x_in[t0:t0 + T, half * 512:(half + 1) * 512].rearrange("(b p) f -> p b f", p=128), B_r1a,
                          reads=[B_in], writes=[B_r1a])
                    for cc in range(4):
                        c = half * 4 + cc
                        pp, Bp = psum_next()
                        for b in range(4):
                            k.op('pe', lambda e, pp=pp, b=b, cc=cc, tmv=tmv: e.transpose(pp[:, b * 128:(b + 1) * 128], tmv[:, b, cc * 128:(cc + 1) * 128], ident[:]),
                                 reads=[B_r1a, B_const], writes=[Bp])
                        k.op('act', lambda e, pp=pp, c=c, x=x: e.copy(out=x[:, c, :], in_=pp[:]), reads=[Bp], writes=[Bx])
            else:
                if ti + 1 < ntile:
                    load_x(ti + 1)
            if l_out >= 0:
                if 'nomix' in DBG:
                    w_next('mo'); w_next('mo')
                else:
                    mixer_out(x, Bx, l_out, t0, is_prompt, tloc)
                if 'noffn' in DBG:
                    for _ in range(11): w_next('win')
                    for _ in range(4): w_next('wout')
                else:
                    ffn(x, Bx, l_out, 2)
            if l_in is not None:
                if 'noffn' in DBG:
                    for _ in range(11): w_next('win')
                    for _ in range(4): w_next('wout')
                else:
                    ffn(x, Bx, l_in, 1)
                if 'nomix' in DBG:
                    if l_in % 2 == 0:
                        for _ in range(6): w_next('ein')
                    else:
                        w_next('mdown'); w_next('muq'); w_next('mukv')
                elif l_in % 2 == 0:
                    do_proj_even(x, Bx, l_in, t0, is_prompt, tloc)
                else:
                    do_proj_mla(x, Bx, l_in, t0, is_prompt, tloc)
                k.dma('pool', xres[:, :, t0:t0 + T].rearrange("c p n -> p c n"), x, Bx, reads=[Bx], writes=[B_xres[ti]])
            else:
                tm = r1a.bitcast(F32).rearrange("p (b f) -> p b f", b=4)
                for half in range(2):
                    for b in range(4):
                        pp, Bp = psum_next()
                        for cc in range(4):
                            c = half * 4 + cc
                            k.op('pe', lambda e, pp=pp, b=b, cc=cc, c=c, x=x: e.transpose(pp[:, cc * 128:(cc + 1) * 128], x[:, c, b * 128:(b + 1) * 128], ident[:]),
                                 reads=[Bx, B_const], writes=[Bp])
                        k.op('act', lambda e, pp=pp, b=b: e.copy(out=tm[:, b, :], in_=pp[:]), reads=[Bp], writes=[B_r1a])
                    ev = k.dma('pool', y_out[t0:t0 + T, half * 512:(half + 1) * 512].rearrange("(b p) f -> p b f", p=128), tm, B_r1a,
                               reads=[B_r1a], writes=[B_out])
                    final_events.append(ev)
        assert wcur['g'] == total_units, (wcur['g'], total_units)

    B_out = k.buf("out")
    final_events = []

    RG = [[0, 1, 2, 3], [4, 5, 6, 7]]

    def allgather(src, dst, Bsrc, Bdst):
        def fn(e, src=src, dst=dst):
            return e.collective_compute("AllGather", ALU.bypass, replica_groups=RG, ins=[src], outs=[dst])
        if not isinstance(Bdst, list):
            Bdst = [Bdst]
        k.dma('pool', None, None, B_cc, reads=[Bsrc], writes=Bdst, inc=1, fn=fn)

    def exchange(l):
        even = (l % 2 == 0)
        nh = 4 if even else 8
        if even:
            zv = zP.rearrange("r (n o) -> r n o", o=1)
            zev = zedge.rearrange("r (n o) -> r n o", o=1)
            k.dma('pool', zev[:, 0:1, :], zv[:, 1:2, :], B_ze, reads=[B_z], writes=[B_ze], slow=True)
            k.dma('pool', zev[:, 1:2, :], zv[:, NP:NP + 1, :], B_ze, reads=[B_z], writes=[B_ze], slow=True)
            allgather(zedge, zedge_g, B_ze, B_zeg)
        for h in range(nh):
            allgather(KTa_p[h * 128:(h + 1) * 128, :], KTa_g[h * G * 128:(h + 1) * G * 128, :], B_Kp, [B_Kg[h]])
            if not even and h % 2 == 0:
                hp = h // 2
                allgather(KTb_p[hp * 128:(hp + 1) * 128, :], KTb_g[hp * G * 128:(hp + 1) * G * 128, :], B_Kp, [B_Kg[h], B_Kg[h + 1]])
            allgather(V_p[h * 128:(h + 1) * 128, :], V_g[h * G * 128:(h + 1) * G * 128, :], B_Kp, [B_Kg[h]])

    def halo_fix(l):
        Ev = Eg[:].rearrange("p (r c k) -> p r c k", r=G, c=4)
        k.dma('sp', Ev, zedge_g.rearrange("(r c p) k -> p r c k", r=G, c=4)[:, :, :, 0:2], B_halo, reads=[B_zeg], writes=[B_halo], slow=True)
        hv = halo[:].rearrange("p (c s) -> p c s", s=2)
        for side in range(2):
            kk = 1 - side
            for r in range(G):
                if r == 0:
                    k.op('dve', lambda e, side=side, kk=kk, r=r: e.tensor_scalar(out=hv[:, :, side], in0=Ev[:, r, :, kk], scalar1=hmask[:, side * G + r:side * G + r + 1],
                                                                             scalar2=None, op0=ALU.mult), reads=[B_halo, B_const], writes=[B_halo])
                else:
                    k.op('dve', lambda e, side=side, kk=kk, r=r: e.scalar_tensor_tensor(out=hv[:, :, side], in0=Ev[:, r, :, kk], scalar=hmask[:, side * G + r:side * G + r + 1],
                                                                                    in1=hv[:, :, side], op0=ALU.mult, op1=ALU.add),
                         reads=[B_halo, B_const], writes=[B_halo])
        zPv = zP.rearrange("(c p) n -> p c n", p=128)
        zSv = zS.rearrange("(c p) n -> p c n", p=128)
        k.dma('sp', zPv[:, :, 0:1], hv[:, :, 0:1], B_halo, reads=[B_halo], writes=[B_z], slow=True)
        k.dma('sp', zPv[:, :, NP + 1:NP + 2], hv[:, :, 1:2], B_halo, reads=[B_halo], writes=[B_z], slow=True)
        ztv = zt[:, 0:4].rearrange("p (c o) -> p c o", o=1)
        k.dma('sp', zSv[:, :, 0:1], ztv, B_halo, reads=[B_const], writes=[B_z], slow=True)
        k.dma('sp', zSv[:, :, NS + 1:NS + 2], ztv, B_halo, reads=[B_const], writes=[B_z], slow=True)

    def attention(l):
        even = (l % 2 == 0)
        i = l // 2
        nh = 4 if even else 8
        nmap = 2 if even else 1
        scale = (64 ** -0.5) if even else (192 ** -0.5)
        arena.reset()
        k.scope('at')
        NQM = max(NP, NS)
        SEGM = NQM
        qa = [arena.alloc([NQM], BF16) for _ in range(2)]
        B_qa = [k.buf() for _ in range(2)]
        if not even:
            qb = [arena.alloc([NQM], BF16) for _ in range(2)]
            B_qb = [k.buf() for _ in range(2)]
        NKS = 2
        ka = [arena.alloc([SEGM], BF16) for _ in range(NKS)]
        B_ka = [k.buf() for _ in range(NKS)]
        if not even:
            kb_ = [arena.alloc([SEGM], BF16) for _ in range(NKS)]
            B_kb = [k.buf() for _ in range(NKS)]
        vv = [arena.alloc([SEGM // 128, 128], BF16) for _ in range(NKS)]
        B_vv = [k.buf() for _ in range(NKS)]
        NPT = 4
        pt = [arena.alloc([512], BF16) for _ in range(NPT)]
        B_pt = [k.buf() for _ in range(NPT)]
        acc_o = [arena.alloc([NQM], F32) for _ in range(nmap)]
        acc_l = [arena.alloc([NQM], F32) for _ in range(nmap)]
        B_acc = [k.buf() for _ in range(nmap)]
        ost = [arena.alloc([NQM], BF16) for _ in range(2)]
        B_ost = [k.buf() for _ in range(2)]
        f1 = [arena.alloc([512], F32) for _ in range(2)]
        f2 = [arena.alloc([512], F32) for _ in range(2)]
        f3 = [arena.alloc([512], BF16) for _ in range(2)]
        B_f1 = [k.buf() for _ in range(2)]
        B_f2 = [k.buf() for _ in range(2)]
        B_f3 = [k.buf() for _ in range(2)]
        ps_s = [(ps[j], B_ps[j]) for j in range(3)]
        ps_o = [(ps[3 + j], B_ps[3 + j]) for j in range(2)]
        ps_l = [(ps[5 + j], B_ps[5 + j]) for j in range(2)]
        ps_f = (ps[7], B_ps[7])
        cnt = {'s': 0, 'pt': 0, 'ol': 0, 'seg': 0, 'job': 0}

        jobs = []
        for h in range(nh):
            jobs.append(('s', h))
        for h in range(nh):
            jobs.append(('p', h))

        def load_q(ji):
            kind, h = jobs[ji]
            n0, nq = (NP, NS) if kind == 's' else (0, NP)
            s = ji % 2
            k.dma('sp', qa[s][:, 0:nq], QTa[h * 128:(h + 1) * 128, n0:n0 + nq], B_qa[s], reads=[B_Q], writes=[B_qa[s]])
            if not even:
                k.dma('sp', qb[s][0:64, 0:nq], QTb[h * 64:(h + 1) * 64, n0:n0 + nq], B_qb[s], reads=[B_Q], writes=[B_qb[s]])

        segs = []
        for ji, (kind, h) in enumerate(jobs):
            if kind == 's':
                segs.append((ji, 's', h, 0, NS))
            else:
                for r in range(G):
                    segs.append((ji, 'p', h, r, NP))
        seg_loaded = {'n': 0}

        def load_seg(si):
            ji, kind, h, r, nk = segs[si]
            s = si % NKS
            if kind == 's':
                srcKa = KTa_s[h * 128:(h + 1) * 128, :]
                srcV = V_s[h * 128:(h + 1) * 128, :]
                BK = B_Ks
                if not even:
                    srcKb = KTb_s[h * 64:(h + 1) * 64, :]
            else:
                srcKa = KTa_g[(h * G + r) * 128:(h * G + r + 1) * 128, :]
                srcV = V_g[(h * G + r) * 128:(h * G + r + 1) * 128, :]
                BK = B_Kg[h]
                if not even:
                    rb = ((h // 2) * G + r) * 128 + (h % 2) * 64
                    srcKb = KTb_g[rb:rb + 64, :]
            k.dma('sp', ka[s][:, 0:nk], srcKa, B_ka[s], reads=[BK], writes=[B_ka[s]])
            if not even:
                k.dma('sp', kb_[s][0:64, 0:nk], srcKb, B_kb[s], reads=[BK], writes=[B_kb[s]])
            k.dma('sp', vv[s][:, 0:nk // 128, :], srcV.rearrange("p (b e) -> p b e", e=128), B_vv[s], reads=[BK], writes=[B_vv[s]])

        def seg_ensure(upto):
            while seg_loaded['n'] <= min(upto, len(segs) - 1):
                load_seg(seg_loaded['n'])
                seg_loaded['n'] += 1

        load_q(0)
        seg_ensure(0)
        si = 0
        for ji, (kind, h) in enumerate(jobs):
            nq = NS if kind == 's' else NP
            n0 = NP if kind == 's' else 0
            nqc = nq // 512
            if ji + 1 < len(jobs):
                load_q(ji + 1)
            qs = ji % 2
            nseg = 1 if kind == 's' else G
            for sgi in range(nseg):
                seg_ensure(si + 1)
                _, _, _, r, nk = segs[si]
                ks = si % NKS
                nkb = nk // 128
                for qc in range(nqc):
                    for m in range(nmap):
                        po, Bpo = ps_o[cnt['ol'] % 2]
                        pl, Bpl = ps_l[cnt['ol'] % 2]
                        cnt['ol'] += 1
                        for kb in range(nkb):
                            pss, Bpss = ps_s[cnt['s'] % 3]
                            cnt['s'] += 1
                            if even:
                                k.op('pe', lambda e, pss=pss, ks=ks, qs=qs, kb=kb, qc=qc, m=m: e.matmul(
                                    pss[:], lhsT=ka[ks][m * 64:(m + 1) * 64, kb * 128:(kb + 1) * 128],
                                    rhs=qa[qs][m * 64:(m + 1) * 64, qc * 512:(qc + 1) * 512], start=True, stop=True),
                                    reads=[B_ka[ks], B_qa[qs]], writes=[Bpss])
                            else:
                                k.op('pe', lambda e, pss=pss, ks=ks, qs=qs, kb=kb, qc=qc: e.matmul(
                                    pss[:], lhsT=ka[ks][:, kb * 128:(kb + 1) * 128], rhs=qa[qs][:, qc * 512:(qc + 1) * 512],
                                    start=True, stop=False), reads=[B_ka[ks], B_qa[qs]], writes=[Bpss])
                                k.op('pe', lambda e, pss=pss, ks=ks, qs=qs, kb=kb, qc=qc: e.matmul(
                                    pss[:], lhsT=kb_[ks][0:64, kb * 128:(kb + 1) * 128], rhs=qb[qs][0:64, qc * 512:(qc + 1) * 512],
                                    start=False, stop=True), reads=[B_kb[ks], B_qb[qs]], writes=[Bpss])
                            pti = cnt['pt'] % NPT
                            cnt['pt'] += 1
                            k.op('act', lambda e, pss=pss, pti=pti: e.activation(out=pt[pti][:], in_=pss[:], func=AF.Exp, scale=float(scale)),
                                 reads=[Bpss], writes=[B_pt[pti]])
                            k.op('pe', lambda e, po=po, ks=ks, kb=kb, pti=pti, nkb=nkb: e.matmul(
                                po[:], lhsT=vv[ks][:, kb, :], rhs=pt[pti][:], start=(kb == 0), stop=(kb == nkb - 1)),
                                reads=[B_vv[ks], B_pt[pti]], writes=[Bpo])
                            k.op('pe', lambda e, pl=pl, kb=kb, pti=pti, nkb=nkb: e.matmul(
                                pl[:], lhsT=onesb[:], rhs=pt[pti][:], start=(kb == 0), stop=(kb == nkb - 1)),
                                reads=[B_pt[pti], B_const], writes=[Bpl])
                        ao_ = acc_o[m][:, qc * 512:(qc + 1) * 512]
                        al_ = acc_l[m][:, qc * 512:(qc + 1) * 512]
                        if sgi == 0:
                            k.op('dve', lambda e, ao_=ao_, po=po: e.tensor_copy(out=ao_, in_=po[:]), reads=[Bpo], writes=[B_acc[m]])
                            k.op('dve', lambda e, al_=al_, pl=pl: e.tensor_copy(out=al_, in_=pl[:]), reads=[Bpl], writes=[B_acc[m]])
                        else:
                            k.op('dve', lambda e, ao_=ao_, po=po: e.tensor_tensor(out=ao_, in0=po[:], in1=ao_, op=ALU.add),
                                 reads=[Bpo, B_acc[m]], writes=[B_acc[m]])
                            k.op('dve', lambda e, al_=al_, pl=pl: e.tensor_tensor(out=al_, in0=pl[:], in1=al_, op=ALU.add),
                                 reads=[Bpl, B_acc[m]], writes=[B_acc[m]])
                si += 1
            osel = ji % 2
            for qc in range(nqc):
                sl = slice(qc * 512, (qc + 1) * 512)
                fi = qc % 2
                if not even:
                    k.op('dve', lambda e, sl=sl, fi=fi: e.reciprocal(out=f1[fi][:], in_=acc_l[0][:, sl]), reads=[B_acc[0]], writes=[B_f1[fi]])
                    k.op('dve', lambda e, sl=sl, fi=fi, osel=osel: e.tensor_tensor(out=ost[osel][:, sl], in0=acc_o[0][:, sl], in1=f1[fi][:], op=ALU.mult),
                         reads=[B_acc[0], B_f1[fi]], writes=[B_ost[osel]])
                else:
                    k.op('dve', lambda e, sl=sl, fi=fi: e.reciprocal(out=f1[fi][:], in_=acc_l[0][:, sl]), reads=[B_acc[0]], writes=[B_f1[fi]])
                    k.op('dve', lambda e, sl=sl, fi=fi: e.tensor_tensor(out=f1[fi][:], in0=acc_o[0][:, sl], in1=f1[fi][:], op=ALU.mult),
                         reads=[B_acc[0], B_f1[fi]], writes=[B_f1[fi]])
                    k.op('dve', lambda e, sl=sl, fi=fi: e.reciprocal(out=f2[fi][:], in_=acc_l[1][:, sl]), reads=[B_acc[1]], writes=[B_f2[fi]])
                    k.op('dve', lambda e, sl=sl, fi=fi: e.tensor_tensor(out=f2[fi][:], in0=acc_o[1][:, sl], in1=f2[fi][:], op=ALU.mult),
                         reads=[B_acc[1], B_f2[fi]], writes=[B_f2[fi]])
                    k.op('dve', lambda e, fi=fi: e.scalar_tensor_tensor(out=f1[fi][:], in0=f2[fi][:], scalar=lamw[:, 8 * i + 5:8 * i + 6], in1=f1[fi][:],
                                                                      op0=ALU.mult, op1=ALU.add),
                         reads=[B_f1[fi], B_f2[fi], B_const], writes=[B_f1[fi]])
                    k.op('dve', lambda e, fi=fi: e.tensor_tensor(out=f3[fi][:], in0=f1[fi][:], in1=f1[fi][:], op=ALU.mult),
                         reads=[B_f1[fi]], writes=[B_f3[fi]])
                    pf, Bpf = ps_f
                    k.op('pe', lambda e, pf=pf, fi=fi: e.matmul(pf[:], lhsT=onesb[:], rhs=f3[fi][:], start=True, stop=True),
                         reads=[B_f3[fi], B_const], writes=[Bpf])
                    k.op('dve', lambda e, pf=pf, fi=fi: e.tensor_scalar(out=f2[fi][:], in0=pf[:], scalar1=float(128 * EPS), scalar2=None, op0=ALU.add),
                         reads=[Bpf], writes=[B_f2[fi]])
                    k.op('act', lambda e, fi=fi: e.activation(out=f2[fi][:], in_=f2[fi][:], func=AF.Ln), reads=[B_f2[fi]], writes=[B_f2[fi]])
                    k.op('act', lambda e, fi=fi: e.activation(out=f2[fi][:], in_=f2[fi][:], func=AF.Exp, scale=-0.5), reads=[B_f2[fi]], writes=[B_f2[fi]])
                    k.op('dve', lambda e, fi=fi, sl=sl, osel=osel: e.scalar_tensor_tensor(out=ost[osel][:, sl], in0=f1[fi][:], scalar=gcol(('sub', i)), in1=f2[fi][:],
                                                                                      op0=ALU.mult, op1=ALU.mult),
                         reads=[B_f1[fi], B_f2[fi], B_const], writes=[B_ost[osel]])
            chunk = (4 + h) if even else h
            k.dma('pool', AO[chunk * 128:(chunk + 1) * 128, n0:n0 + nq], ost[osel][:, 0:nq], B_ost[osel], reads=[B_ost[osel]], writes=[B_AO])

    for p in range(L + 1):
        row_pass(p)
        if p < L:
            k.barrier()
            if 'nomix' not in DBG:
                exchange(p)
                attention(p)
                if p % 2 == 0:
                    halo_fix(p)
            k.barrier()
    k.finish(final_events)
    k.replay()
    return nc


def _host_consts(cfg):
    NT, NP, NS = cfg.NT, cfg.NP, cfg.NS
    ident = np.eye(128, dtype=np.float32)
    onesf = np.ones((128, 128), np.float32)
    rswap = np.zeros((128, 128), np.float32)
    for p in range(128):
        g, d = p // 64, p % 64
        rswap[g * 64 + (d + 32) % 64, p] = 1.0
    onesb = np.ones((128, 128), ml_dtypes.bfloat16)
    bd = np.zeros((128, 128), np.float32)
    bd[0:64, 0:64] = 1.0
    bd[64:128, 64:128] = 1.0
    bd64 = bd.astype(ml_dtypes.bfloat16)
    return ident, onesf, rswap, onesb, bd64


def _rope_tables(positions):
    d = 64
    inv = (1.0 / (np.float32(ROPE_THETA) ** (np.arange(0, d, 2, dtype=np.float32) / np.float32(d)))).astype(np.float32)
    ang = positions.astype(np.float32)[:, None] * inv[None, :]
    cos = np.cos(ang).astype(np.float32)
    sin = np.sin(ang).astype(np.float32)
    ct = np.zeros((128, len(positions)), np.float32)
    st = np.zeros((128, len(positions)), np.float32)
    for p in range(128):
        dd = p % 64
        j = dd % 32
        ct[p] = cos[:, j]
        st[p] = -sin[:, j] if dd < 32 else sin[:, j]
    return ct, st


_PROG_CACHE = {}


def kernel(**inputs):
    cfg = CFG
    L, NE, NO, NP, NS, NT = cfg.DEPTH, cfg.NE, cfg.NO, cfg.NP, cfg.NS, cfg.NT
    f32 = lambda a: np.ascontiguousarray(np.asarray(a, dtype=np.float32))
    xp = f32(inputs['x_prompt'])
    xs = f32(inputs['x_sample'])
    gcols, NG = gain_layout(cfg)
    gains = np.zeros((128, NG), np.float32)
    gscale = np.ones((128, NG), np.float32)
    P = np.arange(128)

    def put(name, j, vec, sc):
        c = gcols[name] + j
        gains[:, c] = vec
        gscale[:, c] = np.float32(sc)

    for l in range(L):
        for nm, key in (('f1', 'ffn1_norm'), ('mx', 'mix_norm'), ('f2', 'ffn2_norm')):
            g = f32(inputs[key])[l]
            for c in range(8):
                put((nm, l), c, g[c * 128:(c + 1) * 128], math.sqrt(D))
    for i in range(NE):
        cw = f32(inputs['even_conv_w'])[i]
        for kk in range(3):
            for c in range(4):
                put(('cw', i), kk * 4 + c, cw[kk, c * 128:(c + 1) * 128], 1.0)
        qn = f32(inputs['even_q_norm'])[i]
        kn = f32(inputs['even_k_norm'])[i]
        put(('qA', i), 0, qn[P % 64], 8.0)
        put(('qB', i), 0, qn[(P % 64 + 32) % 64], 8.0)
        put(('kA', i), 0, kn[P % 64], 8.0)
        put(('kB', i), 0, kn[(P % 64 + 32) % 64], 8.0)
        put(('sub', i), 0, f32(inputs['even_subln'])[i], math.sqrt(128.0) * (1.0 - lambda_init(2 * i)))
    for i in range(NO):
        ql = f32(inputs['mla_q_lat_norm'])[i]
        kvl = f32(inputs['mla_kv_lat_norm'])[i]
        for c in range(3):
            put(('ql', i), c, ql[c * 128:(c + 1) * 128], math.sqrt(384.0))
        for c in range(2):
            put(('kvl', i), c, kvl[c * 128:(c + 1) * 128], 16.0)
        for pre, key in (('q', 'mla_q_norm'), ('k', 'mla_k_norm')):
            g = f32(inputs[key])[i]
            s = math.sqrt(192.0)
            put((pre + 'n', i), 0, g[0:128], s)
            put((pre + 'rA', i), 0, g[128 + P % 64], s)
            put((pre + 'rB', i), 0, g[128 + (P % 64 + 32) % 64], s)
    lamT = np.zeros((128, 4 * NE), np.float32)
    lv = f32(inputs['even_lambda'])
    for i in range(NE):
        for r in range(4):
            lamT[0:64, 4 * i + r] = lv[i, r]
    ident, onesf, rswap, onesb, bd64 = _host_consts(cfg)
    zeros = np.zeros((128, 64), np.float32)

    common = {
        'ffn1_w_in': f32(inputs['ffn1_w_in']), 'ffn1_w_out': f32(inputs['ffn1_w_out']),
        'ffn2_w_in': f32(inputs['ffn2_w_in']), 'ffn2_w_out': f32(inputs['ffn2_w_out']),
        'even_w_in': f32(inputs['even_w_in']), 'even_w_out': f32(inputs['even_w_out']),
        'gains': gains, 'gscale': gscale, 'lamT': lamT, 'ident': ident, 'onesf': onesf, 'rswap': rswap,
        'onesb': onesb, 'bd64': bd64, 'zeros': zeros,
    }
    if NO:
        common.update({'mla_w_down': f32(inputs['mla_w_down']), 'mla_w_uq': f32(inputs['mla_w_uq']),
                       'mla_w_ukv': f32(inputs['mla_w_ukv']), 'mla_w_o': f32(inputs['mla_w_o'])})
    in_maps = []
    for c in range(NCORES):
        b, r = c // G, c % G
        x_in = np.concatenate([xp[b, r * NP:(r + 1) * NP, :], xs[c]], axis=0)
        pos = np.concatenate([np.arange(r * NP, (r + 1) * NP), np.arange(NS)])
        ct, st = _rope_tables(pos)
        hm = np.zeros((128, 2 * G), np.float32)
        if r > 0:
            hm[:, r - 1] = 1.0
        if r < G - 1:
            hm[:, G + r + 1] = 1.0
        m = dict(common)
        m.update({'x_in': np.ascontiguousarray(x_in), 'cos_t': ct, 'sin_t': st, 'hmask': hm})
        in_maps.append(m)

    key = (cfg.SEQ, cfg.DEC_SEQ, cfg.DEPTH)
    if key not in _PROG_CACHE:
        _PROG_CACHE[key] = build_program(cfg)
    nc = _PROG_CACHE[key]
    res = run_bass_kernel_spmd(nc, in_maps, core_ids=list(range(NCORES)))
    global LAST_RES
    LAST_RES = res
    yp = np.zeros((2, cfg.SEQ, D), np.float32)
    ys = np.zeros((NCORES, NS, D), np.float32)
    for c in range(NCORES):
        y = np.asarray(res.results[c]['y_out'])
        b, r = c // G, c % G
        yp[b, r * NP:(r + 1) * NP, :] = y[0:NP]
        ys[c] = y[NP:NT]
    return yp, ys
```

```python
import math
import numpy as np
import ml_dtypes
import concourse.bass as bass
import concourse.mybir as mybir
from concourse.bass_utils import run_bass_kernel_spmd

F32 = mybir.dt.float32
BF16 = mybir.dt.bfloat16
U8 = mybir.dt.uint8
AF = mybir.ActivationFunctionType
ALU = mybir.AluOpType

D = 1024
KC = 8
DFF = 2816
FJ = 22
T = 512
EPS = 1e-6
ROPE_THETA = 10000.0
NCORES = 8
G = 4


class Cfg:
    def __init__(self, seq=16384, dec_seq=2048, depth=4):
        self.SEQ = seq
        self.DEC_SEQ = dec_seq
        self.DEPTH = depth
        self.NE = (depth + 1) // 2
        self.NO = depth // 2
        self.NP = seq // G
        self.NS = dec_seq
        self.NT = self.NP + self.NS
        assert self.NP % T == 0 and self.NS % T == 0


CFG = Cfg()
DBG = set()


def gain_layout(cfg):
    cols = {}
    n = [0]

    def add(name, k):
        cols[name] = n[0]
        n[0] += k

    for l in range(cfg.DEPTH):
        add(('f1', l), 8)
        add(('mx', l), 8)
        add(('f2', l), 8)
    for i in range(cfg.NE):
        add(('cw', i), 12)
        for nm in ('qA', 'qB', 'kA', 'kB', 'sub'):
            add((nm, i), 1)
    for i in range(cfg.NO):
        add(('ql', i), 3)
        add(('kvl', i), 2)
        for nm in ('qn', 'qrA', 'qrB', 'kn', 'krA', 'krB'):
            add((nm, i), 1)
    return cols, n[0]


ENGS = ('pe', 'act', 'dve', 'pool', 'sp')


class Op:
    __slots__ = ('fn', 'sig', 'idx', 'val', 'dsem', 'dinc')

    def __init__(self, fn):
        self.fn = fn
        self.sig = False
        self.val = 0
        self.dsem = None
        self.dinc = 0


class Buf:
    __slots__ = ('name', 'w', 'r', 'dsem', 'dcnt')

    def __init__(self, name):
        self.name = name
        self.w = {}
        self.r = {}
        self.dsem = None
        self.dcnt = 0


class K:
    def __init__(self, nc):
        self.nc = nc
        self.q = {e: [] for e in ENGS}
        self.waited = {e: {} for e in ENGS}
        self.esem = {e: nc.alloc_semaphore(name=f"es_{e}") for e in ENGS}
        self.dsems = []
        self.bufs = {}
        self.scope_name = 'g'
        self.scope_cnt = 0

    def scope(self, name):
        self.scope_name = name
        self.scope_cnt = 0

    def buf(self, name=None):
        if name is None:
            self.scope_cnt += 1
            name = f"{self.scope_name}_{self.scope_cnt}"
        if name not in self.bufs:
            self.bufs[name] = Buf(name)
        return self.bufs[name]

    def _wait(self, eng, key, ev):
        o = ev[2]
        if self.waited[eng].get(key, -1) >= o:
            return
        self.waited[eng][key] = o
        if ev[0] == 'e':
            ev[3].sig = True
        self.q[eng].append(('w', ev))

    def _deps(self, eng, reads, writes):
        deps = {}
        for b in reads:
            for key, ev in b.w.items():
                if key not in deps or ev[2] > deps[key][2]:
                    deps[key] = ev
        for b in writes:
            for dct in (b.w, b.r):
                for key, ev in dct.items():
                    if key == eng:
                        continue
                    if key not in deps or ev[2] > deps[key][2]:
                        deps[key] = ev
        for key, ev in deps.items():
            self._wait(eng, key, ev)

    def op(self, eng, fn, reads=(), writes=()):
        self._deps(eng, reads, writes)
        o = Op(fn)
        o.idx = len(self.q[eng])
        self.q[eng].append(o)
        ev = ('e', eng, o.idx, o)
        for b in reads:
            b.r[eng] = ev
        for b in writes:
            b.w[eng] = ev
        return o

    def dma(self, eng, out, in_, sb, reads=(), writes=(), inc=16, fn=None, slow=False):
        if sb.dsem is None:
            sb.dsem = self.nc.alloc_semaphore(name=f"ds_{len(self.dsems)}")
            self.dsems.append(sb)
        self._deps(eng, reads, writes)
        sb.dcnt += inc
        if fn is None:
            def fn(e, out=out, in_=in_, slow=slow):
                if slow:
                    return e.dma_start(out=out, in_=in_, allow_slow_non_contiguous=True)
                return e.dma_start(out=out, in_=in_)
        o = Op(fn)
        o.idx = len(self.q[eng])
        o.dsem = sb.dsem
        o.dinc = inc
        self.q[eng].append(o)
        key = ('d', id(sb))
        ev = ('d', key, sb.dcnt, sb.dsem)
        for b in reads:
            b.r[key] = ev
        for b in writes:
            b.w[key] = ev
        return ev

    def barrier(self):
        lasts = {}
        for e in ENGS:
            for it in reversed(self.q[e]):
                if isinstance(it, Op) and it.dsem is None:
                    lasts[e] = ('e', e, it.idx, it)
                    break
        for e in ENGS:
            for f, ev in lasts.items():
                if f != e:
                    self._wait(e, f, ev)
            for sb in self.dsems:
                if sb.dcnt > 0:
                    self._wait(e, ('d', id(sb)), ('d', ('d', id(sb)), sb.dcnt, sb.dsem))

    def finish(self, final_events):
        for ev in final_events:
            self._wait('sp', ev[1], ev)

    def replay(self):
        for e in ENGS:
            c = 0
            for it in self.q[e]:
                if isinstance(it, Op) and it.sig:
                    c += 1
                    it.val = c
        nc = self.nc
        K_ = self

        def run(ename, eng):
            sem = K_.esem[ename]
            for it in K_.q[ename]:
                if isinstance(it, Op):
                    ins = it.fn(eng)
                    if it.dsem is not None:
                        ins.then_inc(it.dsem, it.dinc)
                    elif it.sig:
                        ins.then_inc(sem, 1)
                else:
                    ev = it[1]
                    if ev[0] == 'e':
                        eng.wait_ge(K_.esem[ev[1]], ev[3].val)
                    else:
                        eng.wait_ge(ev[3], ev[2])

        with nc.Block() as block:
            @block.tensor
            def _(e):
                run('pe', e)

            @block.scalar
            def _(e):
                run('act', e)

            @block.vector
            def _(e):
                run('dve', e)

            @block.gpsimd
            def _(e):
                run('pool', e)

            @block.sync
            def _(e):
                run('sp', e)


class Arena:
    def __init__(self, nc, nbytes):
        self.t = nc.alloc_sbuf_tensor("arena", [128, nbytes], U8)
        self.nbytes = nbytes
        self.off = 0

    def reset(self):
        self.off = 0

    def alloc(self, shape, dtype):
        esz = 4 if dtype == F32 else 2
        n = 1
        for s in shape:
            n *= s
        nb = (n * esz + 63) // 64 * 64
        assert self.off + nb <= self.nbytes, f"arena overflow {self.off + nb} > {self.nbytes}"
        ap = self.t[:, self.off:self.off + n * esz].bitcast(dtype)
        self.off += nb
        if len(shape) == 2:
            ap = ap.rearrange("p (a b) -> p a b", a=shape[0])
        elif len(shape) == 3:
            ap = ap.rearrange("p (a b c) -> p a b c", a=shape[0], b=shape[1])
        return ap


def lambda_init(l):
    return 0.8 - 0.6 * math.exp(-0.3 * l)


def build_program(cfg):
    nc = bass.Bass("TRN2", target_bir_lowering=False)
    k = K(nc)
    L, NE, NO, NP, NS, NT = cfg.DEPTH, cfg.NE, cfg.NO, cfg.NP, cfg.NS, cfg.NT
    gcols, NG = gain_layout(cfg)
    NBP = NP // 128
    NBS = NS // 128

    def din(name, shape, dt=F32):
        return nc.dram_tensor(name, list(shape), dt, kind="ExternalInput").ap()

    def dscr(name, shape, dt):
        if 'dump' in DBG and name in ('AO', 'QTa', 'zP', 'gbD', 'KTa_s', 'V_s', 'zS', 'xres', 'QTb'):
            return nc.dram_tensor(name, list(shape), dt, kind="ExternalOutput").ap()
        return nc.dram_tensor(name, list(shape), dt, kind="Internal").ap()

    x_in = din("x_in", [NT, D])
    y_out = nc.dram_tensor("y_out", [NT, D], F32, kind="ExternalOutput").ap()
    w_f1_in = din("ffn1_w_in", [L, D, 2 * DFF])
    w_f1_out = din("ffn1_w_out", [L, DFF, D])
    w_f2_in = din("ffn2_w_in", [L, D, 2 * DFF])
    w_f2_out = din("ffn2_w_out", [L, DFF, D])
    w_e_in = din("even_w_in", [NE, D, 3072])
    w_e_out = din("even_w_out", [NE, D, D])
    if NO:
        w_m_down = din("mla_w_down", [NO, D, 704])
        w_m_uq = din("mla_w_uq", [NO, 384, 1536])
        w_m_ukv = din("mla_w_ukv", [NO, 256, 2048])
        w_m_o = din("mla_w_o", [NO, D, D])
    gains_d = din("gains", [128, NG])
    gscale_d = din("gscale", [128, NG])
    lamT_d = din("lamT", [128, 4 * NE])
    ident_d = din("ident", [128, 128])
    onesf_d = din("onesf", [128, 128])
    rswap_d = din("rswap", [128, 128])
    onesb_d = din("onesb", [128, 128], BF16)
    bd64_d = din("bd64", [128, 128], BF16)
    cos_d = din("cos_t", [128, NT])
    sin_d = din("sin_t", [128, NT])
    hmask_d = din("hmask", [128, 2 * G])
    zero_d = din("zeros", [128, 64])

    xres = dscr("xres", [KC, 128, NT], F32)
    B_xres = [k.buf(f"xres{i}") for i in range(NT // T)]
    WIN = {}
    WOUT = {}
    for l in range(L):
        for w in (1, 2):
            WIN[(l, w)] = dscr(f"win_{l}_{w}", [11, 128, 8, 2, 256], BF16)
            WOUT[(l, w)] = dscr(f"wout_{l}_{w}", [DFF, D], BF16)
    EWIN = [dscr(f"ewin{i}", [D, 3072], BF16) for i in range(NE)]
    EWOUT = [dscr(f"ewout{i}", [D, D], BF16) for i in range(NE)]
    MDOWN = [dscr(f"mdown{i}", [D, 704], BF16) for i in range(NO)]
    MUQ = [dscr(f"muq{i}", [384, 1536], BF16) for i in range(NO)]
    MUKV = [dscr(f"mukv{i}", [256, 2048], BF16) for i in range(NO)]
    MWO = [dscr(f"mwo{i}", [D, D], BF16) for i in range(NO)]
    B_W = k.buf("weights_bf16")

    HMAX = 8 if NO else 4
    QTa = dscr("QTa", [HMAX * 128, NT], BF16)
    QTb = dscr("QTb", [HMAX * 64, NT], BF16)
    KTa_p = dscr("KTa_p", [HMAX * 128, NP], BF16)
    KTb_p = dscr("KTb_p", [HMAX * 64, NP], BF16)
    V_p = dscr("V_p", [HMAX * 128, NBP * 128], BF16)
    KTa_s = dscr("KTa_s", [HMAX * 128, NS], BF16)
    KTb_s = dscr("KTb_s", [HMAX * 64, NS], BF16)
    V_s = dscr("V_s", [HMAX * 128, NBS * 128], BF16)
    KTa_g = dscr("KTa_g", [G * HMAX * 128, NP], BF16)
    KTb_g = dscr("KTb_g", [G * HMAX * 64, NP], BF16)
    V_g = dscr("V_g", [G * HMAX * 128, NBP * 128], BF16)
    AO = dscr("AO", [KC * 128, NT], BF16)
    zP = dscr("zP", [4 * 128, NP + 2], F32)
    zS = dscr("zS", [4 * 128, NS + 2], F32)
    gbD = dscr("gbD", [4 * 128, NT], F32)
    zedge = dscr("zedge", [512, 8], F32)
    zedge_g = dscr("zedge_g", [G * 512, 8], F32)
    B_Q = k.buf("QT")
    B_Kp = k.buf("K_p")
    B_Ks = k.buf("K_s")
    B_Kg = [k.buf(f"K_g{h}") for h in range(HMAX)]
    B_AO = k.buf("AO")
    B_z = k.buf("z")
    B_gb = k.buf("gb")
    B_ze = k.buf("zedge")
    B_zeg = k.buf("zedge_g")
    B_cc = k.buf("cc")
    B_in = k.buf("inputs")

    def sbt(name, shape, dt):
        return nc.alloc_sbuf_tensor("sb_" + name, shape, dt)

    ident = sbt("ident", [128, 128], F32)
    onesf = sbt("onesf", [128, 128], F32)
    rswap = sbt("rswap", [128, 128], F32)
    onesb = sbt("onesb", [128, 128], BF16)
    bd64 = sbt("bd64", [128, 128], BF16)
    gains = sbt("gains", [128, NG], F32)
    gsc = sbt("gsc", [128, NG], F32)
    gs = sbt("gs", [128, NG], F32)
    lamT = sbt("lamT", [128, 4 * NE], F32)
    lamw = sbt("lamw", [128, 8 * NE], F32)
    hmask = sbt("hmask", [128, 2 * G], F32)
    zt = sbt("zt", [128, 64], F32)
    halo = sbt("halo", [128, 8], F32)
    Eg = sbt("Eg", [128, G * 4 * 2], F32)
    B_const = k.buf("const")
    B_halo = k.buf("halo")

    ps = [nc.alloc_psum_tensor(f"ps{i}", [128, 512], F32) for i in range(8)]
    B_ps = [k.buf(f"ps{i}") for i in range(8)]

    arena = Arena(nc, 203 * 1024)

    for dst, src in ((ident, ident_d), (onesf, onesf_d), (rswap, rswap_d), (onesb, onesb_d), (bd64, bd64_d),
                     (gains, gains_d), (gsc, gscale_d), (lamT, lamT_d), (hmask, hmask_d), (zt, zero_d)):
        k.dma('sp', dst[:], src, B_const, writes=[B_const])
    k.op('dve', lambda e: e.tensor_tensor(out=gs[:], in0=gains[:], in1=gsc[:], op=ALU.mult),
         reads=[B_const], writes=[B_const])
    for i in range(NE):
        lw = lamw[:, 8 * i:8 * i + 8]
        lt = lamT[:, 4 * i:4 * i + 4]
        k.op('dve', lambda e, lw=lw, lt=lt: e.tensor_tensor(out=lw[:, 0:1], in0=lt[:, 0:1], in1=lt[:, 1:2], op=ALU.mult),
             reads=[B_const], writes=[B_const])
        k.op('dve', lambda e, lw=lw, lt=lt: e.tensor_tensor(out=lw[:, 1:2], in0=lt[:, 2:3], in1=lt[:, 3:4], op=ALU.mult),
             reads=[B_const], writes=[B_const])
        k.op('pe', lambda e, lw=lw: e.matmul(ps[0][:, 0:2], lhsT=onesf[:], rhs=lw[:, 0:2], start=True, stop=True),
             reads=[B_const], writes=[B_ps[0]])
        k.op('act', lambda e, lw=lw: e.activation(out=lw[:, 2:4], in_=ps[0][:, 0:2], func=AF.Exp),
             reads=[B_ps[0]], writes=[B_const])
        li = lambda_init(2 * i)
        k.op('dve', lambda e, lw=lw, li=li: e.tensor_scalar(out=lw[:, 4:5], in0=lw[:, 2:3], scalar1=lw[:, 3:4], scalar2=li,
                                                         op0=ALU.subtract, op1=ALU.add),
             reads=[B_const], writes=[B_const])
        k.op('dve', lambda e, lw=lw: e.tensor_scalar(out=lw[:, 5:6], in0=lw[:, 4:5], scalar1=-1.0, scalar2=None, op0=ALU.mult),
             reads=[B_const], writes=[B_const])

    def gcol(name, j=0):
        c = gcols[name] + j
        return gs[:, c:c + 1]

    arena.reset()
    CW = 4096
    cst_f = [arena.alloc([CW], F32) for _ in range(3)]
    cst_b = [arena.alloc([CW], BF16) for _ in range(3)]
    B_cf = [k.buf() for _ in range(3)]
    B_cb = [k.buf() for _ in range(3)]
    cast_i = [0]

    def cast_block(src_rows, ncols, dst_fn):
        for c0 in range(0, ncols, CW):
            c1 = min(ncols, c0 + CW)
            i = cast_i[0] % 3
            cast_i[0] += 1
            f, b = cst_f[i], cst_b[i]
            k.dma('sp', f[:, 0:c1 - c0], src_rows[:, c0:c1], B_cf[i], reads=[B_in], writes=[B_cf[i]])
            if cast_i[0] % 2 == 0:
                k.op('dve', lambda e, f=f, b=b, n=c1 - c0: e.tensor_copy(out=b[:, 0:n], in_=f[:, 0:n]),
                     reads=[B_cf[i]], writes=[B_cb[i]])
            else:
                k.op('act', lambda e, f=f, b=b, n=c1 - c0: e.copy(out=b[:, 0:n], in_=f[:, 0:n]),
                     reads=[B_cf[i]], writes=[B_cb[i]])
            res = dst_fn(c0, c1, b)
            if not isinstance(res, list):
                res = [res]
            for dst_ap, src_view in res:
                k.dma('pool', dst_ap, src_view, B_cb[i], reads=[B_cb[i]], writes=[B_W])

    def cast_natural(src2d, dst2d, rows, ncols):
        for r0 in range(0, rows, 128):
            def dst_fn(c0, c1, b, r0=r0):
                return dst2d[r0:r0 + 128, c0:c1], b[:, 0:c1 - c0]
            cast_block(src2d[r0:r0 + 128, :], ncols, dst_fn)

    def cast_win(src2d, dst5):
        for kc in range(8):
            for gu in range(2):
                def dst_fn(c0, c1, b, kc=kc, gu=gu):
                    assert c0 == 0 and c1 == DFF
                    return (dst5[:, :, kc, gu, :].rearrange("u p f -> p u f"),
                            b[:, 0:DFF].rearrange("p (u f) -> p u f", f=256))
                cast_block(src2d[kc * 128:(kc + 1) * 128, gu * DFF:(gu + 1) * DFF], DFF, dst_fn)

    def cast_ukv(src2d, dst2d):
        for r0 in range(0, 256, 128):
            def dst_fn(c0, c1, b, r0=r0):
                bv = b[:, 0:2048].rearrange("p (h s e) -> p h s e", h=8, s=2)
                return [(dst2d[r0:r0 + 128, s_ * 1024:(s_ + 1) * 1024].rearrange("p (h e) -> p h e", h=8), bv[:, :, s_, :])
                        for s_ in range(2)]
            cast_block(src2d[r0:r0 + 128, :], 2048, dst_fn)

    def emit_casts_for_layer(l):
        cast_win(w_f1_in[l], WIN[(l, 1)])
        cast_natural(w_f1_out[l], WOUT[(l, 1)], DFF, D)
        if l % 2 == 0:
            cast_natural(w_e_in[l // 2], EWIN[l // 2], D, 3072)
            cast_natural(w_e_out[l // 2], EWOUT[l // 2], D, D)
        else:
            i = l // 2
            cast_natural(w_m_down[i], MDOWN[i], D, 704)
            cast_natural(w_m_uq[i], MUQ[i], 384, 1536)
            cast_ukv(w_m_ukv[i], MUKV[i])
            cast_natural(w_m_o[i], MWO[i], D, D)
        cast_win(w_f2_in[l], WIN[(l, 2)])
        cast_natural(w_f2_out[l], WOUT[(l, 2)], DFF, D)

    for l in range(L):
        emit_casts_for_layer(l)
    k.barrier()

    psrr = [0]

    def psum_next():
        i = psrr[0] % 8
        psrr[0] += 1
        return ps[i], B_ps[i]

    def row_pass(p):
        arena.reset()
        k.scope('rp')
        xT = [arena.alloc([8, T], F32) for _ in range(2)]
        B_x = [k.buf() for _ in range(2)]
        NSLOT = 3
        wr = [arena.alloc([5632], BF16) for _ in range(NSLOT)]
        B_wr = [k.buf() for _ in range(NSLOT)]
        hT = arena.alloc([8, T], BF16)
        B_h = k.buf("hT")
        AT = arena.alloc([FJ, T], BF16)
        B_AT = k.buf("AT")
        sq = arena.alloc([8, T], BF16)
        B_sq = k.buf("sq")
        rstd = arena.alloc([T], F32)
        B_rstd = k.buf("rstd")
        sg = [arena.alloc([T], F32) for _ in range(2)]
        B_sg = [k.buf() for _ in range(2)]
        r1a = arena.alloc([8 * T], BF16)
        r1b = arena.alloc([4 * (T + 2) + 30], F32)
        r1c = arena.alloc([4 * T], F32)
        B_r1a, B_r1b, B_r1c = k.buf("r1a"), k.buf("r1b"), k.buf("r1c")
        ct = [arena.alloc([T], F32) for _ in range(2)]
        B_ct = [k.buf() for _ in range(2)]
        cosT = arena.alloc([T], F32)
        sinT = arena.alloc([T], F32)
        B_cs = k.buf("cossin")
        p4off = arena.off

        l_out = p - 1
        l_in = p if p < L else None

        def units_for_tile():
            us = []
            if l_out >= 0:
                wsrc = EWOUT[l_out // 2] if l_out % 2 == 0 else MWO[l_out // 2]
                for hh in range(2):
                    us.append(('mo', wsrc.rearrange("(ic p) d -> p ic d", p=128)[:, :, hh * 512:(hh + 1) * 512], [8, 512]))
                us += ffn_units(l_out, 2)
            if l_in is not None:
                us += ffn_units(l_in, 1)
                if l_in % 2 == 0:
                    w = EWIN[l_in // 2].rearrange("(kc p) n -> p kc n", p=128)
                    for u in range(6):
                        us.append(('ein', w[:, :, u * 512:(u + 1) * 512], [8, 512]))
                else:
                    i = l_in // 2
                    us.append(('mdown', MDOWN[i].rearrange("(kc p) n -> p kc n", p=128), [8, 704]))
                    us.append(('muq', MUQ[i].rearrange("(kc p) n -> p kc n", p=128), [3, 1536]))
                    us.append(('mukv', MUKV[i].rearrange("(kc p) n -> p kc n", p=128), [2, 2048]))
            return us

        def ffn_units(l, w):
            us = []
            for u in range(11):
                us.append(('win', WIN[(l, w)][u].rearrange("p kc gu f -> p (kc gu f)"), [4096]))
            wo = WOUT[(l, w)].rearrange("(j p) d -> p j d", p=128)
            for cp in range(4):
                us.append(('wout', wo[:, :, cp * 256:(cp + 1) * 256], [FJ, 256]))
            return us

        tiles = list(range(NT // T))
        tile_units = units_for_tile()
        NU = len(tile_units)
        total_units = NU * len(tiles)
        wstate = {'loaded': 0}

        def wview(slot, shape):
            n = 1
            for s_ in shape:
                n *= s_
            v = wr[slot][:, 0:n]
            if len(shape) == 2:
                v = v.rearrange("p (a b) -> p a b", a=shape[0])
            return v

        def w_ensure(upto):
            while wstate['loaded'] <= min(upto, total_units - 1):
                g = wstate['loaded']
                kind, src, shape = tile_units[g % NU]
                slot = g % NSLOT
                k.dma('sp', wview(slot, shape), src, B_wr[slot], reads=[B_W], writes=[B_wr[slot]])
                wstate['loaded'] += 1

        wcur = {'g': 0}

        def w_next(kind_expect, hold=0):
            g = wcur['g']
            w_ensure(g + NSLOT - 1 - hold)
            kind, src, shape = tile_units[g % NU]
            assert kind == kind_expect, (kind, kind_expect)
            slot = g % NSLOT
            wcur['g'] += 1
            return wview(slot, shape), B_wr[slot]

        def rms_to_h(x, Bx, gname, nchunk=8, n=D, dst=None, Bdst=None, src_list=None):
            dst = hT if dst is None else dst
            Bdst = B_h if Bdst is None else Bdst
            for c in range(nchunk):
                k.op('dve', lambda e, c=c: e.tensor_tensor(out=sq[:, c, :], in0=x[:, c, :], in1=x[:, c, :], op=ALU.mult),
                     reads=[Bx], writes=[B_sq])
            pt, Bp = psum_next()
            for c in range(nchunk):
                k.op('pe', lambda e, c=c, pt=pt: e.matmul(pt[:], lhsT=onesb[:], rhs=sq[:, c, :], start=(c == 0), stop=(c == nchunk - 1)),
                     reads=[B_sq, B_const], writes=[Bp])
            k.op('dve', lambda e, pt=pt: e.tensor_scalar(out=rstd[:], in0=pt[:], scalar1=float(n * EPS), scalar2=None, op0=ALU.add),
                 reads=[Bp], writes=[B_rstd])
            k.op('act', lambda e: e.activation(out=rstd[:], in_=rstd[:], func=AF.Ln), reads=[B_rstd], writes=[B_rstd])
            k.op('act', lambda e: e.activation(out=rstd[:], in_=rstd[:], func=AF.Exp, scale=-0.5), reads=[B_rstd], writes=[B_rstd])
            for c in range(nchunk):
                k.op('dve', lambda e, c=c: e.scalar_tensor_tensor(out=dst[:, c, :], in0=x[:, c, :], scalar=gcol(gname, c), in1=rstd[:],
                                                                op0=ALU.mult, op1=ALU.mult),
                     reads=[Bx, B_rstd, B_const], writes=[Bdst])

        def ffn(x, Bx, l, w):
            rms_to_h(x, Bx, ('f1' if w == 1 else 'f2', l))
            for u in range(11):
                wv, Bw = w_next('win')
                wv = wv.rearrange("p (kc gu f) -> p kc gu f", kc=8, gu=2)
                for jj in range(2):
                    j = 2 * u + jj
                    pg, Bpg = psum_next()
                    pu, Bpu = psum_next()
                    for gu, (pp, Bpp) in enumerate(((pg, Bpg), (pu, Bpu))):
                        for kc in range(8):
                            k.op('pe', lambda e, pp=pp, wv=wv, kc=kc, gu=gu, jj=jj: e.matmul(
                                pp[:], lhsT=wv[:, kc, gu, jj * 128:(jj + 1) * 128], rhs=hT[:, kc, :],
                                start=(kc == 0), stop=(kc == 7)), reads=[Bw, B_h], writes=[Bpp])
                    s_, Bs = sg[j % 2], B_sg[j % 2]
                    k.op('act', lambda e, s_=s_, pg=pg: e.activation(out=s_[:], in_=pg[:], func=AF.Silu),
                         reads=[Bpg], writes=[Bs])
                    k.op('dve', lambda e, s_=s_, pu=pu, j=j: e.tensor_tensor(out=AT[:, j, :], in0=s_[:], in1=pu[:], op=ALU.mult),
                         reads=[Bs, Bpu], writes=[B_AT])
            for cp in range(4):
                wv, Bw = w_next('wout')
                for cc in range(2):
                    c = 2 * cp + cc
                    py, Bpy = psum_next()
                    for j in range(FJ):
                        k.op('pe', lambda e, py=py, wv=wv, j=j, cc=cc: e.matmul(
                            py[:], lhsT=wv[:, j, cc * 128:(cc + 1) * 128], rhs=AT[:, j, :],
                            start=(j == 0), stop=(j == FJ - 1)), reads=[Bw, B_AT], writes=[Bpy])
                    k.op('dve', lambda e, py=py, c=c: e.scalar_tensor_tensor(out=x[:, c, :], in0=py[:], scalar=0.5, in1=x[:, c, :],
                                                                         op0=ALU.mult, op1=ALU.add),
                         reads=[Bpy, Bx], writes=[Bx])

        def mixer_out(x, Bx, l, t0, is_prompt, tloc):
            ao = r1a.rearrange("p (c t) -> p c t", c=8)
            if l % 2 == 0:
                i = l // 2
                k.dma('sp', ao[:, 4:8, :], AO.rearrange("(c p) n -> p c n", p=128)[:, 4:8, t0:t0 + T], B_r1a,
                      reads=[B_AO], writes=[B_r1a])
                zsrc = (zP if is_prompt else zS).rearrange("(c p) n -> p c n", p=128)
                zt_ = r1b[:, 0:4 * (T + 2)].rearrange("p (c t) -> p c t", c=4)
                k.dma('sp', zt_, zsrc[:, :, tloc:tloc + T + 2], B_r1b, reads=[B_z], writes=[B_r1b])
                gbt = r1c.rearrange("p (c t) -> p c t", c=4)
                k.dma('sp', gbt, gbD.rearrange("(c p) n -> p c n", p=128)[:, :, t0:t0 + T], B_r1c, reads=[B_gb], writes=[B_r1c])
                for c in range(4):
                    a, Ba = ct[0], B_ct[0]
                    b, Bb = ct[1], B_ct[1]
                    k.op('dve', lambda e, c=c, a=a: e.tensor_scalar(out=a[:], in0=zt_[:, c, 1:T + 1], scalar1=gcol(('cw', i), 4 + c), scalar2=None, op0=ALU.mult),
                         reads=[B_r1b, B_const], writes=[Ba])
                    k.op('dve', lambda e, c=c, a=a, b=b: e.scalar_tensor_tensor(out=b[:], in0=zt_[:, c, 0:T], scalar=gcol(('cw', i), c), in1=a[:], op0=ALU.mult, op1=ALU.add),
                         reads=[B_r1b, Ba, B_const], writes=[Bb])
                    k.op('dve', lambda e, c=c, a=a, b=b: e.scalar_tensor_tensor(out=a[:], in0=zt_[:, c, 2:T + 2], scalar=gcol(('cw', i), 8 + c), in1=b[:], op0=ALU.mult, op1=ALU.add),
                         reads=[B_r1b, Bb, B_const], writes=[Ba])
                    k.op('dve', lambda e, c=c, a=a: e.tensor_tensor(out=ao[:, c, :], in0=a[:], in1=gbt[:, c, :], op=ALU.mult),
                         reads=[Ba, B_r1c], writes=[B_r1a])
            else:
                k.dma('sp', ao, AO.rearrange("(c p) n -> p c n", p=128)[:, :, t0:t0 + T], B_r1a, reads=[B_AO], writes=[B_r1a])
            for hh in range(2):
                wv, Bw = w_next('mo')
                for cc in range(4):
                    c = hh * 4 + cc
                    py, Bpy = psum_next()
                    for ic in range(8):
                        k.op('pe', lambda e, py=py, wv=wv, ic=ic, cc=cc: e.matmul(
                            py[:], lhsT=wv[:, ic, cc * 128:(cc + 1) * 128], rhs=ao[:, ic, :], start=(ic == 0), stop=(ic == 7)),
                            reads=[Bw, B_r1a], writes=[Bpy])
                    k.op('dve', lambda e, py=py, c=c: e.tensor_tensor(out=x[:, c, :], in0=py[:], in1=x[:, c, :], op=ALU.add),
                         reads=[Bpy, Bx], writes=[Bx])

        def load_cossin(t0):
            k.dma('sp', cosT[:], cos_d[:, t0:t0 + T], B_cs, reads=[B_in], writes=[B_cs])
            k.dma('sp', sinT[:], sin_d[:, t0:t0 + T], B_cs, reads=[B_in], writes=[B_cs])

        if l_in is not None and l_in % 2 == 0:
            arena.off = p4off
            gc_sb = [arena.alloc([T], F32) for _ in range(2)]
            xq = [arena.alloc([T], F32) for _ in range(2)]
            sqb = [arena.alloc([T], BF16) for _ in range(2)]
            rs = [arena.alloc([T], F32) for _ in range(2)]
            t1 = [arena.alloc([T], F32) for _ in range(2)]
            t2 = [arena.alloc([T], F32) for _ in range(2)]
            B_gc = [k.buf() for _ in range(2)]
            B_xq = [k.buf() for _ in range(2)]
            B_sqb = [k.buf() for _ in range(2)]
            B_rs = [k.buf() for _ in range(2)]
            B_t1 = [k.buf() for _ in range(2)]
            B_t2 = [k.buf() for _ in range(2)]
            gb_st = arena.alloc([4, T], F32)
            z_st = arena.alloc([4, T], F32)
            QT_st = arena.alloc([4, T], BF16)
            KT_st = arena.alloc([4, T], BF16)
            V_st = arena.alloc([4, 4, 128], BF16)
            B_gbst, B_zst, B_QTst, B_KTst, B_Vst = (k.buf() for _ in range(5))
        elif l_in is not None:
            arena.off = p4off
            cq = arena.alloc([3, T], F32)
            ckv = arena.alloc([2, T], F32)
            kr = arena.alloc([T], F32)
            krr = arena.alloc([T], F32)
            cqn = arena.alloc([3, T], BF16)
            ckvn = arena.alloc([2, T], BF16)
            sqr_k = arena.alloc([T], BF16)
            B_cq, B_ckv, B_kr, B_krr, B_cqn, B_ckvn, B_sqrk = (k.buf() for _ in range(7))
            qn = [arena.alloc([T], F32) for _ in range(2)]
            qr = [arena.alloc([T], F32) for _ in range(2)]
            sqn = [arena.alloc([T], BF16) for _ in range(2)]
            sqr = [arena.alloc([T], BF16) for _ in range(2)]
            rs = [arena.alloc([T], F32) for _ in range(2)]
            t1 = [arena.alloc([T], F32) for _ in range(2)]
            t2 = [arena.alloc([T], F32) for _ in range(2)]
            B_qn = [k.buf() for _ in range(2)]
            B_qr = [k.buf() for _ in range(2)]
            B_sqn = [k.buf() for _ in range(2)]
            B_sqr = [k.buf() for _ in range(2)]
            B_rs = [k.buf() for _ in range(2)]
            B_t1 = [k.buf() for _ in range(2)]
            B_t2 = [k.buf() for _ in range(2)]
            QTa_st = AT[:, 0:8, :]
            KTa_st = AT[:, 8:16, :]
            QTb_st = r1a.rearrange("p (c t) -> p c t", c=8)
            KTb_st = r1c.bitcast(BF16).rearrange("p (c t) -> p c t", c=8)[:, :, 0:T]
            Vm_st = r1b[:, 0:2048].bitcast(BF16).rearrange("p (h b e) -> p h b e", h=8, b=4)
            B_KTbst, B_Vmst = B_r1c, B_r1b

        def do_proj_even(x, Bx, l, t0, is_prompt, tloc):
            i = l // 2
            rms_to_h(x, Bx, ('mx', l))
            load_cossin(t0)
            wv, Bw = w_next('ein')
            for c in range(4):
                pp, Bp = psum_next()
                for kc in range(8):
                    k.op('pe', lambda e, pp=pp, wv=wv, kc=kc, c=c: e.matmul(pp[:], lhsT=wv[:, kc, c * 128:(c + 1) * 128], rhs=hT[:, kc, :],
                                                                        start=(kc == 0), stop=(kc == 7)), reads=[Bw, B_h], writes=[Bp])
                k.op('act', lambda e, pp=pp, c=c: e.copy(out=gb_st[:, c, :], in_=pp[:]), reads=[Bp], writes=[B_gbst])
            k.dma('pool', gbD.rearrange("(c p) n -> p c n", p=128)[:, :, t0:t0 + T], gb_st, B_gbst, reads=[B_gbst], writes=[B_gb])
            wvc, Bwc = w_next('ein')
            wvu, Bwu = w_next('ein', hold=1)
            for c in range(4):
                pc, Bpc = psum_next()
                pu, Bpu = psum_next()
                for (pp, Bp, wv, Bw) in ((pc, Bpc, wvc, Bwc), (pu, Bpu, wvu, Bwu)):
                    for kc in range(8):
                        k.op('pe', lambda e, pp=pp, wv=wv, kc=kc, c=c: e.matmul(pp[:], lhsT=wv[:, kc, c * 128:(c + 1) * 128], rhs=hT[:, kc, :],
                                                                            start=(kc == 0), stop=(kc == 7)), reads=[Bw, B_h], writes=[Bp])
                g_, Bg = gc_sb[c % 2], B_gc[c % 2]
                k.op('act', lambda e, pc=pc, g_=g_: e.copy(out=g_[:], in_=pc[:]), reads=[Bpc], writes=[Bg])
                k.op('dve', lambda e, pu=pu, g_=g_, c=c: e.tensor_tensor(out=z_st[:, c, :], in0=g_[:], in1=pu[:], op=ALU.mult),
                     reads=[Bg, Bpu], writes=[B_zst])
            zdst = (zP if is_prompt else zS).rearrange("(c p) n -> p c n", p=128)
            k.dma('pool', zdst[:, :, 1 + tloc:1 + tloc + T], z_st, B_zst, reads=[B_zst], writes=[B_z])
            for which, (st, Bst, gA, gB) in enumerate(((QT_st, B_QTst, ('qA', i), ('qB', i)), (KT_st, B_KTst, ('kA', i), ('kB', i)))):
                wv, Bw = w_next('ein')
                for c in range(4):
                    ii = c % 2
                    pp, Bp = psum_next()
                    for kc in range(8):
                        k.op('pe', lambda e, pp=pp, wv=wv, kc=kc, c=c: e.matmul(pp[:], lhsT=wv[:, kc, c * 128:(c + 1) * 128], rhs=hT[:, kc, :],
                                                                            start=(kc == 0), stop=(kc == 7)), reads=[Bw, B_h], writes=[Bp])
                    k.op('act', lambda e, pp=pp, ii=ii: e.copy(out=xq[ii][:], in_=pp[:]), reads=[Bp], writes=[B_xq[ii]])
                    k.op('dve', lambda e, ii=ii: e.tensor_tensor(out=sqb[ii][:], in0=xq[ii][:], in1=xq[ii][:], op=ALU.mult),
                         reads=[B_xq[ii]], writes=[B_sqb[ii]])
                    pss, Bpss = psum_next()
                    k.op('pe', lambda e, pss=pss, ii=ii: e.matmul(pss[:], lhsT=bd64[:], rhs=sqb[ii][:], start=True, stop=True),
                         reads=[B_sqb[ii], B_const], writes=[Bpss])
                    prot, Bprot = psum_next()
                    k.op('pe', lambda e, prot=prot, ii=ii: e.matmul(prot[:], lhsT=rswap[:], rhs=xq[ii][:], start=True, stop=True),
                         reads=[B_xq[ii], B_const], writes=[Bprot])
                    k.op('dve', lambda e, pss=pss, ii=ii: e.tensor_scalar(out=rs[ii][:], in0=pss[:], scalar1=float(64 * EPS), scalar2=None, op0=ALU.add),
                         reads=[Bpss], writes=[B_rs[ii]])
                    k.op('act', lambda e, ii=ii: e.activation(out=rs[ii][:], in_=rs[ii][:], func=AF.Ln), reads=[B_rs[ii]], writes=[B_rs[ii]])
                    k.op('act', lambda e, ii=ii: e.activation(out=rs[ii][:], in_=rs[ii][:], func=AF.Exp, scale=-0.5), reads=[B_rs[ii]], writes=[B_rs[ii]])
                    k.op('dve', lambda e, ii=ii, gA=gA: e.scalar_tensor_tensor(out=t1[ii][:], in0=xq[ii][:], scalar=gcol(gA), in1=cosT[:],
                                                                           op0=ALU.mult, op1=ALU.mult),
                         reads=[B_xq[ii], B_cs, B_const], writes=[B_t1[ii]])
                    k.op('dve', lambda e, ii=ii, gB=gB, prot=prot: e.scalar_tensor_tensor(out=t2[ii][:], in0=prot[:], scalar=gcol(gB), in1=sinT[:],
                                                                                      op0=ALU.mult, op1=ALU.mult),
                         reads=[Bprot, B_cs, B_const], writes=[B_t2[ii]])
                    k.op('dve', lambda e, ii=ii: e.tensor_tensor(out=t1[ii][:], in0=t1[ii][:], in1=t2[ii][:], op=ALU.add),
                         reads=[B_t1[ii], B_t2[ii]], writes=[B_t1[ii]])
                    k.op('dve', lambda e, ii=ii, st=st, c=c: e.tensor_tensor(out=st[:, c, :], in0=t1[ii][:], in1=rs[ii][:], op=ALU.mult),
                         reads=[B_t1[ii], B_rs[ii]], writes=[Bst])
                if which == 0:
                    k.dma('pool', QTa[0:512, :].rearrange("(c p) n -> p c n", p=128)[:, :, t0:t0 + T], st, Bst, reads=[Bst], writes=[B_Q])
                else:
                    dstK = (KTa_p if is_prompt else KTa_s)[0:512, :].rearrange("(c p) n -> p c n", p=128)
                    k.dma('pool', dstK[:, :, tloc:tloc + T], st, Bst, reads=[Bst], writes=[B_Kp if is_prompt else B_Ks])
            wv, Bw = w_next('ein')
            for b in range(4):
                pp, Bp = psum_next()
                for kc in range(8):
                    k.op('pe', lambda e, pp=pp, wv=wv, kc=kc, b=b: e.matmul(pp[:], lhsT=hT[:, kc, b * 128:(b + 1) * 128], rhs=wv[:, kc, :],
                                                                        start=(kc == 0), stop=(kc == 7)), reads=[Bw, B_h], writes=[Bp])
                k.op('act', lambda e, pp=pp, b=b: e.copy(out=V_st[:, :, b, :], in_=pp[:].rearrange("p (h e) -> p h e", h=4)),
                     reads=[Bp], writes=[B_Vst])
            vd = (V_p if is_prompt else V_s)[0:512, :].rearrange("(h p) (b e) -> p h b e", p=128, e=128)
            b0 = tloc // 128
            k.dma('pool', vd[:, :, b0:b0 + 4, :], V_st, B_Vst, reads=[B_Vst], writes=[B_Kp if is_prompt else B_Ks])

        def do_proj_mla(x, Bx, l, t0, is_prompt, tloc):
            i = l // 2
            rms_to_h(x, Bx, ('mx', l))
            load_cossin(t0)
            wv, Bw = w_next('mdown')
            for c in range(6):
                pp, Bp = psum_next()
                m = 128 if c < 5 else 64
                for kc in range(8):
                    k.op('pe', lambda e, pp=pp, wv=wv, kc=kc, c=c, m=m: e.matmul(pp[0:m, :], lhsT=wv[:, kc, c * 128:c * 128 + m], rhs=hT[:, kc, :],
                                                                             start=(kc == 0), stop=(kc == 7)), reads=[Bw, B_h], writes=[Bp])
                if c < 3:
                    k.op('act', lambda e, pp=pp, c=c: e.copy(out=cq[:, c, :], in_=pp[:]), reads=[Bp], writes=[B_cq])
                elif c < 5:
                    k.op('act', lambda e, pp=pp, c=c: e.copy(out=ckv[:, c - 3, :], in_=pp[:]), reads=[Bp], writes=[B_ckv])
                else:
                    k.op('act', lambda e, pp=pp: e.copy(out=kr[0:64, :], in_=pp[0:64, :]), reads=[Bp], writes=[B_kr])
            rms_to_h(cq, B_cq, ('ql', i), nchunk=3, n=384, dst=cqn, Bdst=B_cqn)
            rms_to_h(ckv, B_ckv, ('kvl', i), nchunk=2, n=256, dst=ckvn, Bdst=B_ckvn)
            k.op('dve', lambda e: e.tensor_tensor(out=sqr_k[0:64, :], in0=kr[0:64, :], in1=kr[0:64, :], op=ALU.mult),
                 reads=[B_kr], writes=[B_sqrk])
            prot, Bprot = psum_next()
            k.op('pe', lambda e, prot=prot: e.matmul(prot[0:64, :], lhsT=rswap[0:64, 0:64], rhs=kr[0:64, :], start=True, stop=True),
                 reads=[B_kr, B_const], writes=[Bprot])
            k.op('dve', lambda e: e.scalar_tensor_tensor(out=krr[0:64, :], in0=kr[0:64, :], scalar=gs[0:64, gcols[('krA', i)]:gcols[('krA', i)] + 1],
                                                       in1=cosT[0:64, :], op0=ALU.mult, op1=ALU.mult),
                 reads=[B_kr, B_cs, B_const], writes=[B_krr])
            k.op('dve', lambda e, prot=prot: e.scalar_tensor_tensor(out=t2[0][0:64, :], in0=prot[0:64, :], scalar=gs[0:64, gcols[('krB', i)]:gcols[('krB', i)] + 1],
                                                                  in1=sinT[0:64, :], op0=ALU.mult, op1=ALU.mult),
                 reads=[Bprot, B_cs, B_const], writes=[B_t2[0]])
            k.op('dve', lambda e: e.tensor_tensor(out=krr[0:64, :], in0=krr[0:64, :], in1=t2[0][0:64, :], op=ALU.add),
                 reads=[B_krr, B_t2[0]], writes=[B_krr])
            wv, Bw = w_next('muq')
            for h in range(8):
                ii = h % 2
                pn, Bpn = psum_next()
                pr, Bpr = psum_next()
                for kc in range(3):
                    k.op('pe', lambda e, pn=pn, wv=wv, kc=kc, h=h: e.matmul(pn[:], lhsT=wv[:, kc, h * 192:h * 192 + 128], rhs=cqn[:, kc, :],
                                                                        start=(kc == 0), stop=(kc == 2)), reads=[Bw, B_cqn], writes=[Bpn])
                for kc in range(3):
                    k.op('pe', lambda e, pr=pr, wv=wv, kc=kc, h=h: e.matmul(pr[0:64, :], lhsT=wv[:, kc, h * 192 + 128:h * 192 + 192], rhs=cqn[:, kc, :],
                                                                        start=(kc == 0), stop=(kc == 2)), reads=[Bw, B_cqn], writes=[Bpr])
                k.op('act', lambda e, pn=pn, ii=ii: e.copy(out=qn[ii][:], in_=pn[:]), reads=[Bpn], writes=[B_qn[ii]])
                k.op('act', lambda e, pr=pr, ii=ii: e.copy(out=qr[ii][0:64, :], in_=pr[0:64, :]), reads=[Bpr], writes=[B_qr[ii]])
                k.op('dve', lambda e, ii=ii: e.tensor_tensor(out=sqn[ii][:], in0=qn[ii][:], in1=qn[ii][:], op=ALU.mult),
                     reads=[B_qn[ii]], writes=[B_sqn[ii]])
                k.op('dve', lambda e, ii=ii: e.tensor_tensor(out=sqr[ii][0:64, :], in0=qr[ii][0:64, :], in1=qr[ii][0:64, :], op=ALU.mult),
                     reads=[B_qr[ii]], writes=[B_sqr[ii]])
                pss, Bpss = psum_next()
                k.op('pe', lambda e, pss=pss, ii=ii: e.matmul(pss[:], lhsT=onesb[:], rhs=sqn[ii][:], start=True, stop=False),
                     reads=[B_sqn[ii], B_const], writes=[Bpss])
                k.op('pe', lambda e, pss=pss, ii=ii: e.matmul(pss[:], lhsT=onesb[0:64, :], rhs=sqr[ii][0:64, :], start=False, stop=True),
                     reads=[B_sqr[ii], B_const], writes=[Bpss])
                prot, Bprot = psum_next()
                k.op('pe', lambda e, prot=prot, ii=ii: e.matmul(prot[0:64, :], lhsT=rswap[0:64, 0:64], rhs=qr[ii][0:64, :], start=True, stop=True),
                     reads=[B_qr[ii], B_const], writes=[Bprot])
                k.op('dve', lambda e, pss=pss, ii=ii: e.tensor_scalar(out=rs[ii][:], in0=pss[:], scalar1=float(192 * EPS), scalar2=None, op0=ALU.add),
                     reads=[Bpss], writes=[B_rs[ii]])
                k.op('act', lambda e, ii=ii: e.activation(out=rs[ii][:], in_=rs[ii][:], func=AF.Ln), reads=[B_rs[ii]], writes=[B_rs[ii]])
                k.op('act', lambda e, ii=ii: e.activation(out=rs[ii][:], in_=rs[ii][:], func=AF.Exp, scale=-0.5), reads=[B_rs[ii]], writes=[B_rs[ii]])
                k.op('dve', lambda e, ii=ii, h=h: e.scalar_tensor_tensor(out=QTa_st[:, h, :], in0=qn[ii][:], scalar=gcol(('qn', i)), in1=rs[ii][:],
                                                                     op0=ALU.mult, op1=ALU.mult),
                     reads=[B_qn[ii], B_rs[ii], B_const], writes=[B_AT])
                k.op('dve', lambda e, ii=ii: e.scalar_tensor_tensor(out=t1[ii][0:64, :], in0=qr[ii][0:64, :], scalar=gs[0:64, gcols[('qrA', i)]:gcols[('qrA', i)] + 1],
                                                                  in1=cosT[0:64, :], op0=ALU.mult, op1=ALU.mult),
                     reads=[B_qr[ii], B_cs, B_const], writes=[B_t1[ii]])
                k.op('dve', lambda e, ii=ii, prot=prot: e.scalar_tensor_tensor(out=t2[ii][0:64, :], in0=prot[0:64, :], scalar=gs[0:64, gcols[('qrB', i)]:gcols[('qrB', i)] + 1],
                                                                            in1=sinT[0:64, :], op0=ALU.mult, op1=ALU.mult),
                     reads=[Bprot, B_cs, B_const], writes=[B_t2[ii]])
                k.op('dve', lambda e, ii=ii: e.tensor_tensor(out=t1[ii][0:64, :], in0=t1[ii][0:64, :], in1=t2[ii][0:64, :], op=ALU.add),
                     reads=[B_t1[ii], B_t2[ii]], writes=[B_t1[ii]])
                k.op('dve', lambda e, ii=ii, h=h: e.tensor_tensor(out=QTb_st[0:64, h, :], in0=t1[ii][0:64, :], in1=rs[ii][0:64, :], op=ALU.mult),
                     reads=[B_t1[ii], B_rs[ii]], writes=[B_r1a])
            k.dma('pool', QTa.rearrange("(h p) n -> p h n", p=128)[:, :, t0:t0 + T], QTa_st, B_AT, reads=[B_AT], writes=[B_Q])
            k.dma('pool', QTb.rearrange("(h p) n -> p h n", p=64)[:, :, t0:t0 + T], QTb_st[0:64, :, :], B_r1a, reads=[B_r1a], writes=[B_Q])
            wv, Bw = w_next('mukv')
            for h in range(8):
                ii = h % 2
                pn, Bpn = psum_next()
                for kc in range(2):
                    k.op('pe', lambda e, pn=pn, wv=wv, kc=kc, h=h: e.matmul(pn[:], lhsT=wv[:, kc, h * 128:(h + 1) * 128], rhs=ckvn[:, kc, :],
                                                                        start=(kc == 0), stop=(kc == 1)), reads=[Bw, B_ckvn], writes=[Bpn])
                k.op('act', lambda e, pn=pn, ii=ii: e.copy(out=qn[ii][:], in_=pn[:]), reads=[Bpn], writes=[B_qn[ii]])
                k.op('dve', lambda e, ii=ii: e.tensor_tensor(out=sqn[ii][:], in0=qn[ii][:], in1=qn[ii][:], op=ALU.mult),
                     reads=[B_qn[ii]], writes=[B_sqn[ii]])
                pss, Bpss = psum_next()
                k.op('pe', lambda e, pss=pss, ii=ii: e.matmul(pss[:], lhsT=onesb[:], rhs=sqn[ii][:], start=True, stop=False),
                     reads=[B_sqn[ii], B_const], writes=[Bpss])
                k.op('pe', lambda e, pss=pss: e.matmul(pss[:], lhsT=onesb[0:64, :], rhs=sqr_k[0:64, :], start=False, stop=True),
                     reads=[B_sqrk, B_const], writes=[Bpss])
                k.op('dve', lambda e, pss=pss, ii=ii: e.tensor_scalar(out=rs[ii][:], in0=pss[:], scalar1=float(192 * EPS), scalar2=None, op0=ALU.add),
                     reads=[Bpss], writes=[B_rs[ii]])
                k.op('act', lambda e, ii=ii: e.activation(out=rs[ii][:], in_=rs[ii][:], func=AF.Ln), reads=[B_rs[ii]], writes=[B_rs[ii]])
                k.op('act', lambda e, ii=ii: e.activation(out=rs[ii][:], in_=rs[ii][:], func=AF.Exp, scale=-0.5), reads=[B_rs[ii]], writes=[B_rs[ii]])
                k.op('dve', lambda e, ii=ii, h=h: e.scalar_tensor_tensor(out=KTa_st[:, h, :], in0=qn[ii][:], scalar=gcol(('kn', i)), in1=rs[ii][:],
                                                                     op0=ALU.mult, op1=ALU.mult),
                     reads=[B_qn[ii], B_rs[ii], B_const], writes=[B_AT])
                k.op('dve', lambda e, ii=ii, h=h: e.tensor_tensor(out=KTb_st[0:64, h, :], in0=krr[0:64, :], in1=rs[ii][0:64, :], op=ALU.mult),
                     reads=[B_krr, B_rs[ii]], writes=[B_KTbst])
            dKa = (KTa_p if is_prompt else KTa_s).rearrange("(h p) n -> p h n", p=128)
            dKb = (KTb_p if is_prompt else KTb_s).rearrange("(h p) n -> p h n", p=64)
            BK = B_Kp if is_prompt else B_Ks
            k.dma('pool', dKa[:, :, tloc:tloc + T], KTa_st, B_AT, reads=[B_AT], writes=[BK])
            k.dma('pool', dKb[:, :, tloc:tloc + T], KTb_st[0:64, :, :], B_KTbst, reads=[B_KTbst], writes=[BK])
            for b in range(4):
                for hf in range(2):
                    pp, Bp = psum_next()
                    for kc in range(2):
                        k.op('pe', lambda e, pp=pp, wv=wv, kc=kc, b=b, hf=hf: e.matmul(
                            pp[:], lhsT=ckvn[:, kc, b * 128:(b + 1) * 128], rhs=wv[:, kc, 1024 + hf * 512:1024 + (hf + 1) * 512],
                            start=(kc == 0), stop=(kc == 1)), reads=[Bw, B_ckvn], writes=[Bp])
                    k.op('act', lambda e, pp=pp, b=b, hf=hf: e.copy(out=Vm_st[:, hf * 4:(hf + 1) * 4, b, :], in_=pp[:].rearrange("p (h e) -> p h e", h=4)),
                         reads=[Bp], writes=[B_Vmst])
            vd = (V_p if is_prompt else V_s).rearrange("(h p) (b e) -> p h b e", p=128, e=128)
            b0 = tloc // 128
            k.dma('pool', vd[:, :, b0:b0 + 4, :], Vm_st, B_Vmst, reads=[B_Vmst], writes=[BK])

        ntile = len(tiles)

        def load_x(ti):
            t0 = ti * T
            xb, Bxb = xT[ti % 2], B_x[ti % 2]
            if p == 0:
                return
            k.dma('sp', xb, xres[:, :, t0:t0 + T].rearrange("c p n -> p c n"), Bxb, reads=[B_xres[ti]], writes=[Bxb])

        if p > 0:
            load_x(0)
        for ti in tiles:
            t0 = ti * T
            is_prompt = t0 < NP
            tloc = t0 if is_prompt else t0 - NP
            x, Bx = xT[ti % 2], B_x[ti % 2]
            if p == 0:
                tm = r1a.bitcast(F32)
                for half in range(2):
                    tmv = tm.rearrange("p (b f) -> p b f", b=4)
                    k.dma('sp', tmv, x_in[t0:t0 + T, half * 512:(half + 1) * 512].rearrange("(b p) f -> p b f", p=128), B_r1a,
                          reads=[B_in], writes=[B_r1a])
                    for cc in range(4):
                        c = half * 4 + cc
                        pp, Bp = psum_next()
                        for b in range(4):
                            k.op('pe', lambda e, pp=pp, b=b, cc=cc, tmv=tmv: e.transpose(pp[:, b * 128:(b + 1) * 128], tmv[:, b, cc * 128:(cc + 1) * 128], ident[:]),
                                 reads=[B_r1a, B_const], writes=[Bp])
                        k.op('act', lambda e, pp=pp, c=c, x=x: e.copy(out=x[:, c, :], in_=pp[:]), reads=[Bp], writes=[Bx])
            else:
                if ti + 1 < ntile:
                    load_x(ti + 1)
            if l_out >= 0:
                if 'nomix' in DBG:
                    w_next('mo'); w_next('mo')
                else:
                    mixer_out(x, Bx, l_out, t0, is_prompt, tloc)
                if 'noffn' in DBG:
                    for _ in range(11): w_next('win')
                    for _ in range(4): w_next('wout')
                else:
                    ffn(x, Bx, l_out, 2)
            if l_in is not None:
                if 'noffn' in DBG:
                    for _ in range(11): w_next('win')
                    for _ in range(4): w_next('wout')
                else:
                    ffn(x, Bx, l_in, 1)
                if 'nomix' in DBG:
                    if l_in % 2 == 0:
                        for _ in range(6): w_next('ein')
                    else:
                        w_next('mdown'); w_next('muq'); w_next('mukv')
                elif l_in % 2 == 0:
                    do_proj_even(x, Bx, l_in, t0, is_prompt, tloc)
                else:
                    do_proj_mla(x, Bx, l_in, t0, is_prompt, tloc)
                k.dma('pool', xres[:, :, t0:t0 + T].rearrange("c p n -> p c n"), x, Bx, reads=[Bx], writes=[B_xres[ti]])
            else:
                tm = r1a.bitcast(F32).rearrange("p (b f) -> p b f", b=4)
                for half in range(2):
                    for b in range(4):
                        pp, Bp = psum_next()
                        for cc in range(4):
                            c = half * 4 + cc
                            k.op('pe', lambda e, pp=pp, b=b, cc=cc, c=c, x=x: e.transpose(pp[:, cc * 128:(cc + 1) * 128], x[:, c, b * 128:(b + 1) * 128], ident[:]),
                                 reads=[Bx, B_const], writes=[Bp])
                        k.op('act', lambda e, pp=pp, b=b: e.copy(out=tm[:, b, :], in_=pp[:]), reads=[Bp], writes=[B_r1a])
                    ev = k.dma('pool', y_out[t0:t0 + T, half * 512:(half + 1) * 512].rearrange("(b p) f -> p b f", p=128), tm, B_r1a,
                               reads=[B_r1a], writes=[B_out])
                    final_events.append(ev)
        assert wcur['g'] == total_units, (wcur['g'], total_units)

    B_out = k.buf("out")
    final_events = []

    RG = [[0, 1, 2, 3], [4, 5, 6, 7]]

    def allgather(src, dst, Bsrc, Bdst):
        def fn(e, src=src, dst=dst):
            return e.collective_compute("AllGather", ALU.bypass, replica_groups=RG, ins=[src], outs=[dst])
        if not isinstance(Bdst, list):
            Bdst = [Bdst]
        k.dma('pool', None, None, B_cc, reads=[Bsrc], writes=Bdst, inc=1, fn=fn)

    def exchange(l):
        even = (l % 2 == 0)
        nh = 4 if even else 8
        if even:
            zv = zP.rearrange("r (n o) -> r n o", o=1)
            zev = zedge.rearrange("r (n o) -> r n o", o=1)
            k.dma('pool', zev[:, 0:1, :], zv[:, 1:2, :], B_ze, reads=[B_z], writes=[B_ze], slow=True)
            k.dma('pool', zev[:, 1:2, :], zv[:, NP:NP + 1, :], B_ze, reads=[B_z], writes=[B_ze], slow=True)
            allgather(zedge, zedge_g, B_ze, B_zeg)
        for h in range(nh):
            allgather(KTa_p[h * 128:(h + 1) * 128, :], KTa_g[h * G * 128:(h + 1) * G * 128, :], B_Kp, [B_Kg[h]])
            if not even and h % 2 == 0:
                hp = h // 2
                allgather(KTb_p[hp * 128:(hp + 1) * 128, :], KTb_g[hp * G * 128:(hp + 1) * G * 128, :], B_Kp, [B_Kg[h], B_Kg[h + 1]])
            allgather(V_p[h * 128:(h + 1) * 128, :], V_g[h * G * 128:(h + 1) * G * 128, :], B_Kp, [B_Kg[h]])

    def halo_fix(l):
        Ev = Eg[:].rearrange("p (r c k) -> p r c k", r=G, c=4)
        k.dma('sp', Ev, zedge_g.rearrange("(r c p) k -> p r c k", r=G, c=4)[:, :, :, 0:2], B_halo, reads=[B_zeg], writes=[B_halo], slow=True)
        hv = halo[:].rearrange("p (c s) -> p c s", s=2)
        for side in range(2):
            kk = 1 - side
            for r in range(G):
                if r == 0:
                    k.op('dve', lambda e, side=side, kk=kk, r=r: e.tensor_scalar(out=hv[:, :, side], in0=Ev[:, r, :, kk], scalar1=hmask[:, side * G + r:side * G + r + 1],
                                                                             scalar2=None, op0=ALU.mult), reads=[B_halo, B_const], writes=[B_halo])
                else:
                    k.op('dve', lambda e, side=side, kk=kk, r=r: e.scalar_tensor_tensor(out=hv[:, :, side], in0=Ev[:, r, :, kk], scalar=hmask[:, side * G + r:side * G + r + 1],
                                                                                    in1=hv[:, :, side], op0=ALU.mult, op1=ALU.add),
                         reads=[B_halo, B_const], writes=[B_halo])
        zPv = zP.rearrange("(c p) n -> p c n", p=128)
        zSv = zS.rearrange("(c p) n -> p c n", p=128)
        k.dma('sp', zPv[:, :, 0:1], hv[:, :, 0:1], B_halo, reads=[B_halo], writes=[B_z], slow=True)
        k.dma('sp', zPv[:, :, NP + 1:NP + 2], hv[:, :, 1:2], B_halo, reads=[B_halo], writes=[B_z], slow=True)
        ztv = zt[:, 0:4].rearrange("p (c o) -> p c o", o=1)
        k.dma('sp', zSv[:, :, 0:1], ztv, B_halo, reads=[B_const], writes=[B_z], slow=True)
        k.dma('sp', zSv[:, :, NS + 1:NS + 2], ztv, B_halo, reads=[B_const], writes=[B_z], slow=True)

    def attention(l):
        even = (l % 2 == 0)
        i = l // 2
        nh = 4 if even else 8
        nmap = 2 if even else 1
        scale = (64 ** -0.5) if even else (192 ** -0.5)
        arena.reset()
        k.scope('at')
        NQM = max(NP, NS)
        SEGM = NQM
        qa = [arena.alloc([NQM], BF16) for _ in range(2)]
        B_qa = [k.buf() for _ in range(2)]
        if not even:
            qb = [arena.alloc([NQM], BF16) for _ in range(2)]
            B_qb = [k.buf() for _ in range(2)]
        NKS = 2
        ka = [arena.alloc([SEGM], BF16) for _ in range(NKS)]
        B_ka = [k.buf() for _ in range(NKS)]
        if not even:
            kb_ = [arena.alloc([SEGM], BF16) for _ in range(NKS)]
            B_kb = [k.buf() for _ in range(NKS)]
        vv = [arena.alloc([SEGM // 128, 128], BF16) for _ in range(NKS)]
        B_vv = [k.buf() for _ in range(NKS)]
        NPT = 4
        pt = [arena.alloc([512], BF16) for _ in range(NPT)]
        B_pt = [k.buf() for _ in range(NPT)]
        acc_o = [arena.alloc([NQM], F32) for _ in range(nmap)]
        acc_l = [arena.alloc([NQM], F32) for _ in range(nmap)]
        B_acc = [k.buf() for _ in range(nmap)]
        ost = [arena.alloc([NQM], BF16) for _ in range(2)]
        B_ost = [k.buf() for _ in range(2)]
        f1 = [arena.alloc([512], F32) for _ in range(2)]
        f2 = [arena.alloc([512], F32) for _ in range(2)]
        f3 = [arena.alloc([512], BF16) for _ in range(2)]
        B_f1 = [k.buf() for _ in range(2)]
        B_f2 = [k.buf() for _ in range(2)]
        B_f3 = [k.buf() for _ in range(2)]
        ps_s = [(ps[j], B_ps[j]) for j in range(3)]
        ps_o = [(ps[3 + j], B_ps[3 + j]) for j in range(2)]
        ps_l = [(ps[5 + j], B_ps[5 + j]) for j in range(2)]
        ps_f = (ps[7], B_ps[7])
        cnt = {'s': 0, 'pt': 0, 'ol': 0, 'seg': 0, 'job': 0}

        jobs = []
        for h in range(nh):
            jobs.append(('s', h))
        for h in range(nh):
            jobs.append(('p', h))

        def load_q(ji):
            kind, h = jobs[ji]
            n0, nq = (NP, NS) if kind == 's' else (0, NP)
            s = ji % 2
            k.dma('sp', qa[s][:, 0:nq], QTa[h * 128:(h + 1) * 128, n0:n0 + nq], B_qa[s], reads=[B_Q], writes=[B_qa[s]])
            if not even:
                k.dma('sp', qb[s][0:64, 0:nq], QTb[h * 64:(h + 1) * 64, n0:n0 + nq], B_qb[s], reads=[B_Q], writes=[B_qb[s]])

        segs = []
        for ji, (kind, h) in enumerate(jobs):
            if kind == 's':
                segs.append((ji, 's', h, 0, NS))
            else:
                for r in range(G):
                    segs.append((ji, 'p', h, r, NP))
        seg_loaded = {'n': 0}

        def load_seg(si):
            ji, kind, h, r, nk = segs[si]
            s = si % NKS
            if kind == 's':
                srcKa = KTa_s[h * 128:(h + 1) * 128, :]
                srcV = V_s[h * 128:(h + 1) * 128, :]
                BK = B_Ks
                if not even:
                    srcKb = KTb_s[h * 64:(h + 1) * 64, :]
            else:
                srcKa = KTa_g[(h * G + r) * 128:(h * G + r + 1) * 128, :]
                srcV = V_g[(h * G + r) * 128:(h * G + r + 1) * 128, :]
                BK = B_Kg[h]
                if not even:
                    rb = ((h // 2) * G + r) * 128 + (h % 2) * 64
                    srcKb = KTb_g[rb:rb + 64, :]
            k.dma('sp', ka[s][:, 0:nk], srcKa, B_ka[s], reads=[BK], writes=[B_ka[s]])
            if not even:
                k.dma('sp', kb_[s][0:64, 0:nk], srcKb, B_kb[s], reads=[BK], writes=[B_kb[s]])
            k.dma('sp', vv[s][:, 0:nk // 128, :], srcV.rearrange("p (b e) -> p b e", e=128), B_vv[s], reads=[BK], writes=[B_vv[s]])

        def seg_ensure(upto):
            while seg_loaded['n'] <= min(upto, len(segs) - 1):
                load_seg(seg_loaded['n'])
                seg_loaded['n'] += 1

        def finalize(ji, h, nq, n0, nqc):
            osel = ji % 2
            for qc in range(nqc):
                sl = slice(qc * 512, (qc + 1) * 512)
                fi = qc % 2
                if not even:
                    k.op('dve', lambda e, sl=sl, fi=fi: e.reciprocal(out=f1[fi][:], in_=acc_l[0][:, sl]), reads=[B_acc[0]], writes=[B_f1[fi]])
                    k.op('dve', lambda e, sl=sl, fi=fi, osel=osel: e.tensor_tensor(out=ost[osel][:, sl], in0=acc_o[0][:, sl], in1=f1[fi][:], op=ALU.mult),
                         reads=[B_acc[0], B_f1[fi]], writes=[B_ost[osel]])
                else:
                    k.op('dve', lambda e, sl=sl, fi=fi: e.reciprocal(out=f1[fi][:], in_=acc_l[0][:, sl]), reads=[B_acc[0]], writes=[B_f1[fi]])
                    k.op('dve', lambda e, sl=sl, fi=fi: e.tensor_tensor(out=f1[fi][:], in0=acc_o[0][:, sl], in1=f1[fi][:], op=ALU.mult),
                         reads=[B_acc[0], B_f1[fi]], writes=[B_f1[fi]])
                    k.op('dve', lambda e, sl=sl, fi=fi: e.reciprocal(out=f2[fi][:], in_=acc_l[1][:, sl]), reads=[B_acc[1]], writes=[B_f2[fi]])
                    k.op('dve', lambda e, sl=sl, fi=fi: e.tensor_tensor(out=f2[fi][:], in0=acc_o[1][:, sl], in1=f2[fi][:], op=ALU.mult),
                         reads=[B_acc[1], B_f2[fi]], writes=[B_f2[fi]])
                    k.op('dve', lambda e, fi=fi: e.scalar_tensor_tensor(out=f1[fi][:], in0=f2[fi][:], scalar=lamw[:, 8 * i + 5:8 * i + 6], in1=f1[fi][:],
                                                                      op0=ALU.mult, op1=ALU.add),
                         reads=[B_f1[fi], B_f2[fi], B_const], writes=[B_f1[fi]])
                    k.op('dve', lambda e, fi=fi: e.tensor_tensor(out=f3[fi][:], in0=f1[fi][:], in1=f1[fi][:], op=ALU.mult),
                         reads=[B_f1[fi]], writes=[B_f3[fi]])
                    pf, Bpf = ps_f
                    k.op('pe', lambda e, pf=pf, fi=fi: e.matmul(pf[:], lhsT=onesb[:], rhs=f3[fi][:], start=True, stop=True),
                         reads=[B_f3[fi], B_const], writes=[Bpf])
                    k.op('dve', lambda e, pf=pf, fi=fi: e.tensor_scalar(out=f2[fi][:], in0=pf[:], scalar1=float(128 * EPS), scalar2=None, op0=ALU.add),
                         reads=[Bpf], writes=[B_f2[fi]])
                    k.op('act', lambda e, fi=fi: e.activation(out=f2[fi][:], in_=f2[fi][:], func=AF.Ln), reads=[B_f2[fi]], writes=[B_f2[fi]])
                    k.op('act', lambda e, fi=fi: e.activation(out=f2[fi][:], in_=f2[fi][:], func=AF.Exp, scale=-0.5), reads=[B_f2[fi]], writes=[B_f2[fi]])
                    k.op('dve', lambda e, fi=fi, sl=sl, osel=osel: e.scalar_tensor_tensor(out=ost[osel][:, sl], in0=f1[fi][:], scalar=gcol(('sub', i)), in1=f2[fi][:],
                                                                                      op0=ALU.mult, op1=ALU.mult),
                         reads=[B_f1[fi], B_f2[fi], B_const], writes=[B_ost[osel]])
            chunk = (4 + h) if even else h
            k.dma('pool', AO[chunk * 128:(chunk + 1) * 128, n0:n0 + nq], ost[osel][:, 0:nq], B_ost[osel], reads=[B_ost[osel]], writes=[B_AO])

        LAG = 2
        tasks = []

        def mk_qk(pss, Bpss, ks, qs, kb, qc, m, pti):
            def fn():
                if even:
                    k.op('pe', lambda e: e.matmul(
                        pss[:], lhsT=ka[ks][m * 64:(m + 1) * 64, kb * 128:(kb + 1) * 128],
                        rhs=qa[qs][m * 64:(m + 1) * 64, qc * 512:(qc + 1) * 512], start=True, stop=True),
                        reads=[B_ka[ks], B_qa[qs]], writes=[Bpss])
                else:
                    k.op('pe', lambda e: e.matmul(
                        pss[:], lhsT=ka[ks][:, kb * 128:(kb + 1) * 128], rhs=qa[qs][:, qc * 512:(qc + 1) * 512],
                        start=True, stop=False), reads=[B_ka[ks], B_qa[qs]], writes=[Bpss])
                    k.op('pe', lambda e: e.matmul(
                        pss[:], lhsT=kb_[ks][0:64, kb * 128:(kb + 1) * 128], rhs=qb[qs][0:64, qc * 512:(qc + 1) * 512],
                        start=False, stop=True), reads=[B_kb[ks], B_qb[qs]], writes=[Bpss])
                k.op('act', lambda e: e.activation(out=pt[pti][:], in_=pss[:], func=AF.Exp, scale=float(scale)),
                     reads=[Bpss], writes=[B_pt[pti]])
            return fn

        def mk_pv(po, Bpo, pl, Bpl, ks, kb, pti, nkb):
            def fn():
                k.op('pe', lambda e: e.matmul(
                    po[:], lhsT=vv[ks][:, kb, :], rhs=pt[pti][:], start=(kb == 0), stop=(kb == nkb - 1)),
                    reads=[B_vv[ks], B_pt[pti]], writes=[Bpo])
                k.op('pe', lambda e: e.matmul(
                    pl[:], lhsT=onesb[:], rhs=pt[pti][:], start=(kb == 0), stop=(kb == nkb - 1)),
                    reads=[B_pt[pti], B_const], writes=[Bpl])
            return fn

        def mk_evac(po, Bpo, pl, Bpl, m, qc, sgi):
            def fn():
                ao_ = acc_o[m][:, qc * 512:(qc + 1) * 512]
                al_ = acc_l[m][:, qc * 512:(qc + 1) * 512]
                if sgi == 0:
                    k.op('dve', lambda e: e.tensor_copy(out=ao_, in_=po[:]), reads=[Bpo], writes=[B_acc[m]])
                    k.op('dve', lambda e: e.tensor_copy(out=al_, in_=pl[:]), reads=[Bpl], writes=[B_acc[m]])
                else:
                    k.op('dve', lambda e: e.tensor_tensor(out=ao_, in0=po[:], in1=ao_, op=ALU.add),
                         reads=[Bpo, B_acc[m]], writes=[B_acc[m]])
                    k.op('dve', lambda e: e.tensor_tensor(out=al_, in0=pl[:], in1=al_, op=ALU.add),
                         reads=[Bpl, B_acc[m]], writes=[B_acc[m]])
            return fn

        load_q(0)
        seg_ensure(1)
        si = 0
        for ji, (kind, h) in enumerate(jobs):
            nq = NS if kind == 's' else NP
            n0 = NP if kind == 's' else 0
            nqc = nq // 512
            qs = ji % 2
            nseg = 1 if kind == 's' else G
            for sgi in range(nseg):
                _, _, _, r, nk = segs[si]
                ks = si % NKS
                nkb = nk // 128
                for qc in range(nqc):
                    for m in range(nmap):
                        po, Bpo = ps_o[cnt['ol'] % 2]
                        pl, Bpl = ps_l[cnt['ol'] % 2]
                        cnt['ol'] += 1
                        for kb in range(nkb):
                            pss, Bpss = ps_s[cnt['s'] % 3]
                            cnt['s'] += 1
                            pti = cnt['pt'] % NPT
                            cnt['pt'] += 1
                            pre = []
                            post = []
                            if qc == 0 and m == 0 and kb == 0:
                                if sgi == 0 and ji + 1 < len(jobs):
                                    pre.append(lambda ji=ji: load_q(ji + 1))
                            if kb == nkb - 1:
                                post.append(mk_evac(po, Bpo, pl, Bpl, m, qc, sgi))
                                if qc == nqc - 1 and m == nmap - 1:
                                    post.append(lambda si=si: seg_ensure(si + 2))
                                if sgi == nseg - 1 and qc == nqc - 1 and m == nmap - 1:
                                    post.append(lambda ji=ji, h=h, nq=nq, n0=n0, nqc=nqc: finalize(ji, h, nq, n0, nqc))
                            tasks.append((pre, mk_qk(pss, Bpss, ks, qs, kb, qc, m, pti), mk_pv(po, Bpo, pl, Bpl, ks, kb, pti, nkb), post))
                si += 1
        nt_ = len(tasks)
        for it in range(nt_ + LAG):
            if it < nt_:
                for f in tasks[it][0]:
                    f()
                tasks[it][1]()
            if it >= LAG:
                tasks[it - LAG][2]()
                for f in tasks[it - LAG][3]:
                    f()

    for p in range(L + 1):
        row_pass(p)
        if p < L:
            k.barrier()
            if 'nomix' not in DBG:
                exchange(p)
                attention(p)
                if p % 2 == 0:
                    halo_fix(p)
            k.barrier()
    k.finish(final_events)
    k.replay()
    return nc


def _host_consts(cfg):
    NT, NP, NS = cfg.NT, cfg.NP, cfg.NS
    ident = np.eye(128, dtype=np.float32)
    onesf = np.ones((128, 128), np.float32)
    rswap = np.zeros((128, 128), np.float32)
    for p in range(128):
        g, d = p // 64, p % 64
        rswap[g * 64 + (d + 32) % 64, p] = 1.0
    onesb = np.ones((128, 128), ml_dtypes.bfloat16)
    bd = np.zeros((128, 128), np.float32)
    bd[0:64, 0:64] = 1.0
    bd[64:128, 64:128] = 1.0
    bd64 = bd.astype(ml_dtypes.bfloat16)
    return ident, onesf, rswap, onesb, bd64


def _rope_tables(positions):
    d = 64
    inv = (1.0 / (np.float32(ROPE_THETA) ** (np.arange(0, d, 2, dtype=np.float32) / np.float32(d)))).astype(np.float32)
    ang = positions.astype(np.float32)[:, None] * inv[None, :]
    cos = np.cos(ang).astype(np.float32)
    sin = np.sin(ang).astype(np.float32)
    ct = np.zeros((128, len(positions)), np.float32)
    st = np.zeros((128, len(positions)), np.float32)
    for p in range(128):
        dd = p % 64
        j = dd % 32
        ct[p] = cos[:, j]
        st[p] = -sin[:, j] if dd < 32 else sin[:, j]
    return ct, st


_PROG_CACHE = {}


def kernel(**inputs):
    cfg = CFG
    L, NE, NO, NP, NS, NT = cfg.DEPTH, cfg.NE, cfg.NO, cfg.NP, cfg.NS, cfg.NT
    f32 = lambda a: np.ascontiguousarray(np.asarray(a, dtype=np.float32))
    xp = f32(inputs['x_prompt'])
    xs = f32(inputs['x_sample'])
    gcols, NG = gain_layout(cfg)
    gains = np.zeros((128, NG), np.float32)
    gscale = np.ones((128, NG), np.float32)
    P = np.arange(128)

    def put(name, j, vec, sc):
        c = gcols[name] + j
        gains[:, c] = vec
        gscale[:, c] = np.float32(sc)

    for l in range(L):
        for nm, key in (('f1', 'ffn1_norm'), ('mx', 'mix_norm'), ('f2', 'ffn2_norm')):
            g = f32(inputs[key])[l]
            for c in range(8):
                put((nm, l), c, g[c * 128:(c + 1) * 128], math.sqrt(D))
    for i in range(NE):
        cw = f32(inputs['even_conv_w'])[i]
        for kk in range(3):
            for c in range(4):
                put(('cw', i), kk * 4 + c, cw[kk, c * 128:(c + 1) * 128], 1.0)
        qn = f32(inputs['even_q_norm'])[i]
        kn = f32(inputs['even_k_norm'])[i]
        put(('qA', i), 0, qn[P % 64], 8.0)
        put(('qB', i), 0, qn[(P % 64 + 32) % 64], 8.0)
        put(('kA', i), 0, kn[P % 64], 8.0)
        put(('kB', i), 0, kn[(P % 64 + 32) % 64], 8.0)
        put(('sub', i), 0, f32(inputs['even_subln'])[i], math.sqrt(128.0) * (1.0 - lambda_init(2 * i)))
    for i in range(NO):
        ql = f32(inputs['mla_q_lat_norm'])[i]
        kvl = f32(inputs['mla_kv_lat_norm'])[i]
        for c in range(3):
            put(('ql', i), c, ql[c * 128:(c + 1) * 128], math.sqrt(384.0))
        for c in range(2):
            put(('kvl', i), c, kvl[c * 128:(c + 1) * 128], 16.0)
        for pre, key in (('q', 'mla_q_norm'), ('k', 'mla_k_norm')):
            g = f32(inputs[key])[i]
            s = math.sqrt(192.0)
            put((pre + 'n', i), 0, g[0:128], s)
            put((pre + 'rA', i), 0, g[128 + P % 64], s)
            put((pre + 'rB', i), 0, g[128 + (P % 64 + 32) % 64], s)
    lamT = np.zeros((128, 4 * NE), np.float32)
    lv = f32(inputs['even_lambda'])
    for i in range(NE):
        for r in range(4):
            lamT[0:64, 4 * i + r] = lv[i, r]
    ident, onesf, rswap, onesb, bd64 = _host_consts(cfg)
    zeros = np.zeros((128, 64), np.float32)

    common = {
        'ffn1_w_in': f32(inputs['ffn1_w_in']), 'ffn1_w_out': f32(inputs['ffn1_w_out']),
        'ffn2_w_in': f32(inputs['ffn2_w_in']), 'ffn2_w_out': f32(inputs['ffn2_w_out']),
        'even_w_in': f32(inputs['even_w_in']), 'even_w_out': f32(inputs['even_w_out']),
        'gains': gains, 'gscale': gscale, 'lamT': lamT, 'ident': ident, 'onesf': onesf, 'rswap': rswap,
        'onesb': onesb, 'bd64': bd64, 'zeros': zeros,
    }
    if NO:
        common.update({'mla_w_down': f32(inputs['mla_w_down']), 'mla_w_uq': f32(inputs['mla_w_uq']),
                       'mla_w_ukv': f32(inputs['mla_w_ukv']), 'mla_w_o': f32(inputs['mla_w_o'])})
    in_maps = []
    for c in range(NCORES):
        b, r = c // G, c % G
        x_in = np.concatenate([xp[b, r * NP:(r + 1) * NP, :], xs[c]], axis=0)
        pos = np.concatenate([np.arange(r * NP, (r + 1) * NP), np.arange(NS)])
        ct, st = _rope_tables(pos)
        hm = np.zeros((128, 2 * G), np.float32)
        if r > 0:
            hm[:, r - 1] = 1.0
        if r < G - 1:
            hm[:, G + r + 1] = 1.0
        m = dict(common)
        m.update({'x_in': np.ascontiguousarray(x_in), 'cos_t': ct, 'sin_t': st, 'hmask': hm})
        in_maps.append(m)

    key = (cfg.SEQ, cfg.DEC_SEQ, cfg.DEPTH)
    if key not in _PROG_CACHE:
        _PROG_CACHE[key] = build_program(cfg)
    nc = _PROG_CACHE[key]
    res = run_bass_kernel_spmd(nc, in_maps, core_ids=list(range(NCORES)))
    global LAST_RES
    LAST_RES = res
    yp = np.zeros((2, cfg.SEQ, D), np.float32)
    ys = np.zeros((NCORES, NS, D), np.float32)
    for c in range(NCORES):
        y = np.asarray(res.results[c]['y_out'])
        b, r = c // G, c % G
        yp[b, r * NP:(r + 1) * NP, :] = y[0:NP]
        ys[c] = y[NP:NT]
    return yp, ys
```

```python
import math
import numpy as np
import ml_dtypes
import concourse.bass as bass
import concourse.mybir as mybir
from concourse.bass_utils import run_bass_kernel_spmd

F32 = mybir.dt.float32
BF16 = mybir.dt.bfloat16
U8 = mybir.dt.uint8
AF = mybir.ActivationFunctionType
ALU = mybir.AluOpType

D = 1024
KC = 8
DFF = 2816
FJ = 22
T = 512
EPS = 1e-6
ROPE_THETA = 10000.0
NCORES = 8
G = 4


class Cfg:
    def __init__(self, seq=16384, dec_seq=2048, depth=4):
        self.SEQ = seq
        self.DEC_SEQ = dec_seq
        self.DEPTH = depth
        self.NE = (depth + 1) // 2
        self.NO = depth // 2
        self.NP = seq // G
        self.NS = dec_seq
        self.NT = self.NP + self.NS
        assert self.NP % T == 0 and self.NS % T == 0


CFG = Cfg()
DBG = set()


def gain_layout(cfg):
    cols = {}
    n = [0]

    def add(name, k):
        cols[name] = n[0]
        n[0] += k

    for l in range(cfg.DEPTH):
        add(('f1', l), 8)
        add(('mx', l), 8)
        add(('f2', l), 8)
    for i in range(cfg.NE):
        add(('cw', i), 12)
        for nm in ('qA', 'qB', 'kA', 'kB', 'sub'):
            add((nm, i), 1)
    for i in range(cfg.NO):
        add(('ql', i), 3)
        add(('kvl', i), 2)
        for nm in ('qn', 'qrA', 'qrB', 'kn', 'krA', 'krB'):
            add((nm, i), 1)
    return cols, n[0]


ENGS = ('pe', 'act', 'dve', 'pool', 'sp')


class Op:
    __slots__ = ('fn', 'sig', 'idx', 'val', 'dsem', 'dinc')

    def __init__(self, fn):
        self.fn = fn
        self.sig = False
        self.val = 0
        self.dsem = None
        self.dinc = 0


class Buf:
    __slots__ = ('name', 'w', 'r', 'dsem', 'dcnt')

    def __init__(self, name):
        self.name = name
        self.w = {}
        self.r = {}
        self.dsem = None
        self.dcnt = 0


class K:
    def __init__(self, nc):
        self.nc = nc
        self.q = {e: [] for e in ENGS}
        self.waited = {e: {} for e in ENGS}
        self.esem = {e: nc.alloc_semaphore(name=f"es_{e}") for e in ENGS}
        self.dsems = []
        self.bufs = {}
        self.scope_name = 'g'
        self.scope_cnt = 0

    def scope(self, name):
        self.scope_name = name
        self.scope_cnt = 0

    def buf(self, name=None):
        if name is None:
            self.scope_cnt += 1
            name = f"{self.scope_name}_{self.scope_cnt}"
        if name not in self.bufs:
            self.bufs[name] = Buf(name)
        return self.bufs[name]

    def _wait(self, eng, key, ev):
        o = ev[2]
        if self.waited[eng].get(key, -1) >= o:
            return
        self.waited[eng][key] = o
        if ev[0] == 'e':
            ev[3].sig = True
        self.q[eng].append(('w', ev))

    def _deps(self, eng, reads, writes):
        deps = {}
        for b in reads:
            for key, ev in b.w.items():
                if key not in deps or ev[2] > deps[key][2]:
                    deps[key] = ev
        for b in writes:
            for dct in (b.w, b.r):
                for key, ev in dct.items():
                    if key == eng:
                        continue
                    if key not in deps or ev[2] > deps[key][2]:
                        deps[key] = ev
        for key, ev in deps.items():
            self._wait(eng, key, ev)

    def op(self, eng, fn, reads=(), writes=()):
        self._deps(eng, reads, writes)
        o = Op(fn)
        o.idx = len(self.q[eng])
        self.q[eng].append(o)
        ev = ('e', eng, o.idx, o)
        for b in reads:
            b.r[eng] = ev
        for b in writes:
            b.w[eng] = ev
        return o

    def dma(self, eng, out, in_, sb, reads=(), writes=(), inc=16, fn=None, slow=False):
        if sb.dsem is None:
            sb.dsem = self.nc.alloc_semaphore(name=f"ds_{len(self.dsems)}")
            self.dsems.append(sb)
        self._deps(eng, reads, writes)
        sb.dcnt += inc
        if fn is None:
            def fn(e, out=out, in_=in_, slow=slow):
                if slow:
                    return e.dma_start(out=out, in_=in_, allow_slow_non_contiguous=True)
                return e.dma_start(out=out, in_=in_)
        o = Op(fn)
        o.idx = len(self.q[eng])
        o.dsem = sb.dsem
        o.dinc = inc
        self.q[eng].append(o)
        key = ('d', id(sb))
        ev = ('d', key, sb.dcnt, sb.dsem)
        for b in reads:
            b.r[key] = ev
        for b in writes:
            b.w[key] = ev
        return ev

    def barrier(self):
        lasts = {}
        for e in ENGS:
            for it in reversed(self.q[e]):
                if isinstance(it, Op) and it.dsem is None:
                    lasts[e] = ('e', e, it.idx, it)
                    break
        for e in ENGS:
            for f, ev in lasts.items():
                if f != e:
                    self._wait(e, f, ev)
            for sb in self.dsems:
                if sb.dcnt > 0:
                    self._wait(e, ('d', id(sb)), ('d', ('d', id(sb)), sb.dcnt, sb.dsem))

    def finish(self, final_events):
        for ev in final_events:
            self._wait('sp', ev[1], ev)

    def replay(self):
        for e in ENGS:
            c = 0
            for it in self.q[e]:
                if isinstance(it, Op) and it.sig:
                    c += 1
                    it.val = c
        nc = self.nc
        K_ = self

        def run(ename, eng):
            sem = K_.esem[ename]
            for it in K_.q[ename]:
                if isinstance(it, Op):
                    ins = it.fn(eng)
                    if it.dsem is not None:
                        ins.then_inc(it.dsem, it.dinc)
                    elif it.sig:
                        ins.then_inc(sem, 1)
                else:
                    ev = it[1]
                    if ev[0] == 'e':
                        eng.wait_ge(K_.esem[ev[1]], ev[3].val)
                    else:
                        eng.wait_ge(ev[3], ev[2])

        with nc.Block() as block:
            @block.tensor
            def _(e):
                run('pe', e)

            @block.scalar
            def _(e):
                run('act', e)

            @block.vector
            def _(e):
                run('dve', e)

            @block.gpsimd
            def _(e):
                run('pool', e)

            @block.sync
            def _(e):
                run('sp', e)


class Arena:
    def __init__(self, nc, nbytes):
        self.t = nc.alloc_sbuf_tensor("arena", [128, nbytes], U8)
        self.nbytes = nbytes
        self.off = 0

    def reset(self):
        self.off = 0

    def alloc(self, shape, dtype):
        esz = 4 if dtype == F32 else 2
        n = 1
        for s in shape:
            n *= s
        nb = (n * esz + 63) // 64 * 64
        assert self.off + nb <= self.nbytes, f"arena overflow {self.off + nb} > {self.nbytes}"
        ap = self.t[:, self.off:self.off + n * esz].bitcast(dtype)
        self.off += nb
        if len(shape) == 2:
            ap = ap.rearrange("p (a b) -> p a b", a=shape[0])
        elif len(shape) == 3:
            ap = ap.rearrange("p (a b c) -> p a b c", a=shape[0], b=shape[1])
        return ap


def lambda_init(l):
    return 0.8 - 0.6 * math.exp(-0.3 * l)


def build_program(cfg):
    nc = bass.Bass("TRN2", target_bir_lowering=False)
    k = K(nc)
    L, NE, NO, NP, NS, NT = cfg.DEPTH, cfg.NE, cfg.NO, cfg.NP, cfg.NS, cfg.NT
    gcols, NG = gain_layout(cfg)
    NBP = NP // 128
    NBS = NS // 128

    def din(name, shape, dt=F32):
        return nc.dram_tensor(name, list(shape), dt, kind="ExternalInput").ap()

    def dscr(name, shape, dt):
        if 'dump' in DBG and name in ('AO', 'QTa', 'zP', 'gbD', 'KTa_s', 'V_s', 'zS', 'xres', 'QTb'):
            return nc.dram_tensor(name, list(shape), dt, kind="ExternalOutput").ap()
        return nc.dram_tensor(name, list(shape), dt, kind="Internal").ap()

    x_in = din("x_in", [NT, D])
    y_out = nc.dram_tensor("y_out", [NT, D], F32, kind="ExternalOutput").ap()
    w_f1_in = din("ffn1_w_in", [L, D, 2 * DFF])
    w_f1_out = din("ffn1_w_out", [L, DFF, D])
    w_f2_in = din("ffn2_w_in", [L, D, 2 * DFF])
    w_f2_out = din("ffn2_w_out", [L, DFF, D])
    w_e_in = din("even_w_in", [NE, D, 3072])
    w_e_out = din("even_w_out", [NE, D, D])
    if NO:
        w_m_down = din("mla_w_down", [NO, D, 704])
        w_m_uq = din("mla_w_uq", [NO, 384, 1536])
        w_m_ukv = din("mla_w_ukv", [NO, 256, 2048])
        w_m_o = din("mla_w_o", [NO, D, D])
    gains_d = din("gains", [128, NG])
    gscale_d = din("gscale", [128, NG])
    lamT_d = din("lamT", [128, 4 * NE])
    ident_d = din("ident", [128, 128])
    onesf_d = din("onesf", [128, 128])
    rswap_d = din("rswap", [128, 128])
    onesb_d = din("onesb", [128, 128], BF16)
    bd64_d = din("bd64", [128, 128], BF16)
    cos_d = din("cos_t", [128, NT])
    sin_d = din("sin_t", [128, NT])
    hmask_d = din("hmask", [128, 2 * G])
    zero_d = din("zeros", [128, 64])

    xres = dscr("xres", [KC, 128, NT], F32)
    B_xres = [k.buf(f"xres{i}") for i in range(NT // T)]
    WIN = {}
    WOUT = {}
    for l in range(L):
        for w in (1, 2):
            WIN[(l, w)] = dscr(f"win_{l}_{w}", [11, 128, 8, 2, 256], BF16)
            WOUT[(l, w)] = dscr(f"wout_{l}_{w}", [DFF, D], BF16)
    EWIN = [dscr(f"ewin{i}", [D, 3072], BF16) for i in range(NE)]
    EWOUT = [dscr(f"ewout{i}", [D, D], BF16) for i in range(NE)]
    MDOWN = [dscr(f"mdown{i}", [D, 704], BF16) for i in range(NO)]
    MUQ = [dscr(f"muq{i}", [384, 1536], BF16) for i in range(NO)]
    MUKV = [dscr(f"mukv{i}", [256, 2048], BF16) for i in range(NO)]
    MWO = [dscr(f"mwo{i}", [D, D], BF16) for i in range(NO)]
    B_W = k.buf("weights_bf16")

    HMAX = 8 if NO else 4
    QTa = dscr("QTa", [HMAX * 128, NT], BF16)
    QTb = dscr("QTb", [HMAX * 64, NT], BF16)
    KTa_p = dscr("KTa_p", [HMAX * 128, NP], BF16)
    KTb_p = dscr("KTb_p", [HMAX * 64, NP], BF16)
    V_p = dscr("V_p", [HMAX * 128, NBP * 128], BF16)
    KTa_s = dscr("KTa_s", [HMAX * 128, NS], BF16)
    KTb_s = dscr("KTb_s", [HMAX * 64, NS], BF16)
    V_s = dscr("V_s", [HMAX * 128, NBS * 128], BF16)
    KTa_g = dscr("KTa_g", [G * HMAX * 128, NP], BF16)
    KTb_g = dscr("KTb_g", [G * HMAX * 64, NP], BF16)
    V_g = dscr("V_g", [G * HMAX * 128, NBP * 128], BF16)
    AO = dscr("AO", [KC * 128, NT], BF16)
    zP = dscr("zP", [4 * 128, NP + 2], F32)
    zS = dscr("zS", [4 * 128, NS + 2], F32)
    gbD = dscr("gbD", [4 * 128, NT], F32)
    zedge = dscr("zedge", [512, 8], F32)
    zedge_g = dscr("zedge_g", [G * 512, 8], F32)
    B_Q = k.buf("QT")
    B_Kp = k.buf("K_p")
    B_Ks = k.buf("K_s")
    B_Kg = [k.buf(f"K_g{h}") for h in range(HMAX)]
    B_AO = k.buf("AO")
    B_z = k.buf("z")
    B_gb = k.buf("gb")
    B_ze = k.buf("zedge")
    B_zeg = k.buf("zedge_g")
    B_cc = k.buf("cc")
    B_in = k.buf("inputs")

    def sbt(name, shape, dt):
        return nc.alloc_sbuf_tensor("sb_" + name, shape, dt)

    ident = sbt("ident", [128, 128], F32)
    onesf = sbt("onesf", [128, 128], F32)
    rswap = sbt("rswap", [128, 128], F32)
    onesb = sbt("onesb", [128, 128], BF16)
    bd64 = sbt("bd64", [128, 128], BF16)
    gains = sbt("gains", [128, NG], F32)
    gsc = sbt("gsc", [128, NG], F32)
    gs = sbt("gs", [128, NG], F32)
    lamT = sbt("lamT", [128, 4 * NE], F32)
    lamw = sbt("lamw", [128, 8 * NE], F32)
    hmask = sbt("hmask", [128, 2 * G], F32)
    zt = sbt("zt", [128, 64], F32)
    halo = sbt("halo", [128, 8], F32)
    Eg = sbt("Eg", [128, G * 4 * 2], F32)
    B_const = k.buf("const")
    B_halo = k.buf("halo")

    ps = [nc.alloc_psum_tensor(f"ps{i}", [128, 512], F32) for i in range(8)]
    B_ps = [k.buf(f"ps{i}") for i in range(8)]

    arena = Arena(nc, 203 * 1024)

    for dst, src in ((ident, ident_d), (onesf, onesf_d), (rswap, rswap_d), (onesb, onesb_d), (bd64, bd64_d),
                     (gains, gains_d), (gsc, gscale_d), (lamT, lamT_d), (hmask, hmask_d), (zt, zero_d)):
        k.dma('sp', dst[:], src, B_const, writes=[B_const])
    k.op('dve', lambda e: e.tensor_tensor(out=gs[:], in0=gains[:], in1=gsc[:], op=ALU.mult),
         reads=[B_const], writes=[B_const])
    for i in range(NE):
        lw = lamw[:, 8 * i:8 * i + 8]
        lt = lamT[:, 4 * i:4 * i + 4]
        k.op('dve', lambda e, lw=lw, lt=lt: e.tensor_tensor(out=lw[:, 0:1], in0=lt[:, 0:1], in1=lt[:, 1:2], op=ALU.mult),
             reads=[B_const], writes=[B_const])
        k.op('dve', lambda e, lw=lw, lt=lt: e.tensor_tensor(out=lw[:, 1:2], in0=lt[:, 2:3], in1=lt[:, 3:4], op=ALU.mult),
             reads=[B_const], writes=[B_const])
        k.op('pe', lambda e, lw=lw: e.matmul(ps[0][:, 0:2], lhsT=onesf[:], rhs=lw[:, 0:2], start=True, stop=True),
             reads=[B_const], writes=[B_ps[0]])
        k.op('act', lambda e, lw=lw: e.activation(out=lw[:, 2:4], in_=ps[0][:, 0:2], func=AF.Exp),
             reads=[B_ps[0]], writes=[B_const])
        li = lambda_init(2 * i)
        k.op('dve', lambda e, lw=lw, li=li: e.tensor_scalar(out=lw[:, 4:5], in0=lw[:, 2:3], scalar1=lw[:, 3:4], scalar2=li,
                                                         op0=ALU.subtract, op1=ALU.add),
             reads=[B_const], writes=[B_const])
        k.op('dve', lambda e, lw=lw: e.tensor_scalar(out=lw[:, 5:6], in0=lw[:, 4:5], scalar1=-1.0, scalar2=None, op0=ALU.mult),
             reads=[B_const], writes=[B_const])

    def gcol(name, j=0):
        c = gcols[name] + j
        return gs[:, c:c + 1]

    arena.reset()
    CW = 4096
    cst_f = [arena.alloc([CW], F32) for _ in range(3)]
    cst_b = [arena.alloc([CW], BF16) for _ in range(3)]
    B_cf = [k.buf() for _ in range(3)]
    B_cb = [k.buf() for _ in range(3)]
    cast_i = [0]

    def cast_block(src_rows, ncols, dst_fn):
        for c0 in range(0, ncols, CW):
            c1 = min(ncols, c0 + CW)
            i = cast_i[0] % 3
            cast_i[0] += 1
            f, b = cst_f[i], cst_b[i]
            k.dma('sp', f[:, 0:c1 - c0], src_rows[:, c0:c1], B_cf[i], reads=[B_in], writes=[B_cf[i]])
            if cast_i[0] % 2 == 0:
                k.op('dve', lambda e, f=f, b=b, n=c1 - c0: e.tensor_copy(out=b[:, 0:n], in_=f[:, 0:n]),
                     reads=[B_cf[i]], writes=[B_cb[i]])
            else:
                k.op('act', lambda e, f=f, b=b, n=c1 - c0: e.copy(out=b[:, 0:n], in_=f[:, 0:n]),
                     reads=[B_cf[i]], writes=[B_cb[i]])
            res = dst_fn(c0, c1, b)
            if not isinstance(res, list):
                res = [res]
            for dst_ap, src_view in res:
                k.dma('pool', dst_ap, src_view, B_cb[i], reads=[B_cb[i]], writes=[B_W])

    def cast_natural(src2d, dst2d, rows, ncols):
        for r0 in range(0, rows, 128):
            def dst_fn(c0, c1, b, r0=r0):
                return dst2d[r0:r0 + 128, c0:c1], b[:, 0:c1 - c0]
            cast_block(src2d[r0:r0 + 128, :], ncols, dst_fn)

    def cast_win(src2d, dst5):
        for kc in range(8):
            for gu in range(2):
                def dst_fn(c0, c1, b, kc=kc, gu=gu):
                    assert c0 == 0 and c1 == DFF
                    return (dst5[:, :, kc, gu, :].rearrange("u p f -> p u f"),
                            b[:, 0:DFF].rearrange("p (u f) -> p u f", f=256))
                cast_block(src2d[kc * 128:(kc + 1) * 128, gu * DFF:(gu + 1) * DFF], DFF, dst_fn)

    def cast_ukv(src2d, dst2d):
        for r0 in range(0, 256, 128):
            def dst_fn(c0, c1, b, r0=r0):
                bv = b[:, 0:2048].rearrange("p (h s e) -> p h s e", h=8, s=2)
                return [(dst2d[r0:r0 + 128, s_ * 1024:(s_ + 1) * 1024].rearrange("p (h e) -> p h e", h=8), bv[:, :, s_, :])
                        for s_ in range(2)]
            cast_block(src2d[r0:r0 + 128, :], 2048, dst_fn)

    def emit_casts_for_layer(l):
        cast_win(w_f1_in[l], WIN[(l, 1)])
        cast_natural(w_f1_out[l], WOUT[(l, 1)], DFF, D)
        if l % 2 == 0:
            cast_natural(w_e_in[l // 2], EWIN[l // 2], D, 3072)
            cast_natural(w_e_out[l // 2], EWOUT[l // 2], D, D)
        else:
            i = l // 2
            cast_natural(w_m_down[i], MDOWN[i], D, 704)
            cast_natural(w_m_uq[i], MUQ[i], 384, 1536)
            cast_ukv(w_m_ukv[i], MUKV[i])
            cast_natural(w_m_o[i], MWO[i], D, D)
        cast_win(w_f2_in[l], WIN[(l, 2)])
        cast_natural(w_f2_out[l], WOUT[(l, 2)], DFF, D)

    for l in range(L):
        emit_casts_for_layer(l)
    k.barrier()

    psrr = [0]

    def psum_next():
        i = psrr[0] % 8
        psrr[0] += 1
        return ps[i], B_ps[i]

    def row_pass(p):
        arena.reset()
        k.scope('rp')
        xT = [arena.alloc([8, T], F32) for _ in range(2)]
        B_x = [k.buf() for _ in range(2)]
        NSLOT = 3
        wr = [arena.alloc([5632], BF16) for _ in range(NSLOT)]
        B_wr = [k.buf() for _ in range(NSLOT)]
        hT = arena.alloc([8, T], BF16)
        B_h = k.buf("hT")
        AT = arena.alloc([FJ, T], BF16)
        B_AT = k.buf("AT")
        sq = arena.alloc([8, T], BF16)
        B_sq = k.buf("sq")
        rstd = arena.alloc([T], F32)
        B_rstd = k.buf("rstd")
        sg = [arena.alloc([T], F32) for _ in range(2)]
        B_sg = [k.buf() for _ in range(2)]
        r1a = arena.alloc([8 * T], BF16)
        r1b = arena.alloc([4 * (T + 2) + 30], F32)
        r1c = arena.alloc([4 * T], F32)
        B_r1a, B_r1b, B_r1c = k.buf("r1a"), k.buf("r1b"), k.buf("r1c")
        ct = [arena.alloc([T], F32) for _ in range(2)]
        B_ct = [k.buf() for _ in range(2)]
        cosT = arena.alloc([T], F32)
        sinT = arena.alloc([T], F32)
        B_cs = k.buf("cossin")
        p4off = arena.off

        l_out = p - 1
        l_in = p if p < L else None

        def units_for_tile():
            us = []
            if l_out >= 0:
                wsrc = EWOUT[l_out // 2] if l_out % 2 == 0 else MWO[l_out // 2]
                for hh in range(2):
                    us.append(('mo', wsrc.rearrange("(ic p) d -> p ic d", p=128)[:, :, hh * 512:(hh + 1) * 512], [8, 512]))
                us += ffn_units(l_out, 2)
            if l_in is not None:
                us += ffn_units(l_in, 1)
                if l_in % 2 == 0:
                    w = EWIN[l_in // 2].rearrange("(kc p) n -> p kc n", p=128)
                    for u in range(6):
                        us.append(('ein', w[:, :, u * 512:(u + 1) * 512], [8, 512]))
                else:
                    i = l_in // 2
                    us.append(('mdown', MDOWN[i].rearrange("(kc p) n -> p kc n", p=128), [8, 704]))
                    us.append(('muq', MUQ[i].rearrange("(kc p) n -> p kc n", p=128), [3, 1536]))
                    us.append(('mukv', MUKV[i].rearrange("(kc p) n -> p kc n", p=128), [2, 2048]))
            return us

        def ffn_units(l, w):
            us = []
            for u in range(11):
                us.append(('win', WIN[(l, w)][u].rearrange("p kc gu f -> p (kc gu f)"), [4096]))
            wo = WOUT[(l, w)].rearrange("(j p) d -> p j d", p=128)
            for cp in range(4):
                us.append(('wout', wo[:, :, cp * 256:(cp + 1) * 256], [FJ, 256]))
            return us

        tiles = list(range(NT // T))
        tile_units = units_for_tile()
        NU = len(tile_units)
        total_units = NU * len(tiles)
        wstate = {'loaded': 0}

        def wview(slot, shape):
            n = 1
            for s_ in shape:
                n *= s_
            v = wr[slot][:, 0:n]
            if len(shape) == 2:
                v = v.rearrange("p (a b) -> p a b", a=shape[0])
            return v

        def w_ensure(upto):
            while wstate['loaded'] <= min(upto, total_units - 1):
                g = wstate['loaded']
                kind, src, shape = tile_units[g % NU]
                slot = g % NSLOT
                k.dma('sp', wview(slot, shape), src, B_wr[slot], reads=[B_W], writes=[B_wr[slot]])
                wstate['loaded'] += 1

        wcur = {'g': 0}

        def w_next(kind_expect, hold=0):
            g = wcur['g']
            w_ensure(g + NSLOT - 1 - hold)
            kind, src, shape = tile_units[g % NU]
            assert kind == kind_expect, (kind, kind_expect)
            slot = g % NSLOT
            wcur['g'] += 1
            return wview(slot, shape), B_wr[slot]

        def rms_to_h(x, Bx, gname, nchunk=8, n=D, dst=None, Bdst=None, src_list=None):
            dst = hT if dst is None else dst
            Bdst = B_h if Bdst is None else Bdst
            for c in range(nchunk):
                k.op('dve', lambda e, c=c: e.tensor_tensor(out=sq[:, c, :], in0=x[:, c, :], in1=x[:, c, :], op=ALU.mult),
                     reads=[Bx], writes=[B_sq])
            pt, Bp = psum_next()
            for c in range(nchunk):
                k.op('pe', lambda e, c=c, pt=pt: e.matmul(pt[:], lhsT=onesb[:], rhs=sq[:, c, :], start=(c == 0), stop=(c == nchunk - 1)),
                     reads=[B_sq, B_const], writes=[Bp])
            k.op('dve', lambda e, pt=pt: e.tensor_scalar(out=rstd[:], in0=pt[:], scalar1=float(n * EPS), scalar2=None, op0=ALU.add),
                 reads=[Bp], writes=[B_rstd])
            k.op('act', lambda e: e.activation(out=rstd[:], in_=rstd[:], func=AF.Ln), reads=[B_rstd], writes=[B_rstd])
            k.op('act', lambda e: e.activation(out=rstd[:], in_=rstd[:], func=AF.Exp, scale=-0.5), reads=[B_rstd], writes=[B_rstd])
            for c in range(nchunk):
                k.op('dve', lambda e, c=c: e.scalar_tensor_tensor(out=dst[:, c, :], in0=x[:, c, :], scalar=gcol(gname, c), in1=rstd[:],
                                                                op0=ALU.mult, op1=ALU.mult),
                     reads=[Bx, B_rstd, B_const], writes=[Bdst])

        def ffn(x, Bx, l, w):
            rms_to_h(x, Bx, ('f1' if w == 1 else 'f2', l))
            for u in range(11):
                wv, Bw = w_next('win')
                wv = wv.rearrange("p (kc gu f) -> p kc gu f", kc=8, gu=2)
                for jj in range(2):
                    j = 2 * u + jj
                    pg, Bpg = psum_next()
                    pu, Bpu = psum_next()
                    for gu, (pp, Bpp) in enumerate(((pg, Bpg), (pu, Bpu))):
                        for kc in range(8):
                            k.op('pe', lambda e, pp=pp, wv=wv, kc=kc, gu=gu, jj=jj: e.matmul(
                                pp[:], lhsT=wv[:, kc, gu, jj * 128:(jj + 1) * 128], rhs=hT[:, kc, :],
                                start=(kc == 0), stop=(kc == 7)), reads=[Bw, B_h], writes=[Bpp])
                    s_, Bs = sg[j % 2], B_sg[j % 2]
                    k.op('act', lambda e, s_=s_, pg=pg: e.activation(out=s_[:], in_=pg[:], func=AF.Silu),
                         reads=[Bpg], writes=[Bs])
                    k.op('dve', lambda e, s_=s_, pu=pu, j=j: e.tensor_tensor(out=AT[:, j, :], in0=s_[:], in1=pu[:], op=ALU.mult),
                         reads=[Bs, Bpu], writes=[B_AT])
            for cp in range(4):
                wv, Bw = w_next('wout')
                for cc in range(2):
                    c = 2 * cp + cc
                    py, Bpy = psum_next()
                    for j in range(FJ):
                        k.op('pe', lambda e, py=py, wv=wv, j=j, cc=cc: e.matmul(
                            py[:], lhsT=wv[:, j, cc * 128:(cc + 1) * 128], rhs=AT[:, j, :],
                            start=(j == 0), stop=(j == FJ - 1)), reads=[Bw, B_AT], writes=[Bpy])
                    k.op('dve', lambda e, py=py, c=c: e.scalar_tensor_tensor(out=x[:, c, :], in0=py[:], scalar=0.5, in1=x[:, c, :],
                                                                         op0=ALU.mult, op1=ALU.add),
                         reads=[Bpy, Bx], writes=[Bx])

        def mixer_out(x, Bx, l, t0, is_prompt, tloc):
            ao = r1a.rearrange("p (c t) -> p c t", c=8)
            if l % 2 == 0:
                i = l // 2
                k.dma('sp', ao[:, 4:8, :], AO.rearrange("(c p) n -> p c n", p=128)[:, 4:8, t0:t0 + T], B_r1a,
                      reads=[B_AO], writes=[B_r1a])
                zsrc = (zP if is_prompt else zS).rearrange("(c p) n -> p c n", p=128)
                zt_ = r1b[:, 0:4 * (T + 2)].rearrange("p (c t) -> p c t", c=4)
                k.dma('sp', zt_, zsrc[:, :, tloc:tloc + T + 2], B_r1b, reads=[B_z], writes=[B_r1b])
                gbt = r1c.rearrange("p (c t) -> p c t", c=4)
                k.dma('sp', gbt, gbD.rearrange("(c p) n -> p c n", p=128)[:, :, t0:t0 + T], B_r1c, reads=[B_gb], writes=[B_r1c])
                for c in range(4):
                    a, Ba = ct[0], B_ct[0]
                    b, Bb = ct[1], B_ct[1]
                    k.op('dve', lambda e, c=c, a=a: e.tensor_scalar(out=a[:], in0=zt_[:, c, 1:T + 1], scalar1=gcol(('cw', i), 4 + c), scalar2=None, op0=ALU.mult),
                         reads=[B_r1b, B_const], writes=[Ba])
                    k.op('dve', lambda e, c=c, a=a, b=b: e.scalar_tensor_tensor(out=b[:], in0=zt_[:, c, 0:T], scalar=gcol(('cw', i), c), in1=a[:], op0=ALU.mult, op1=ALU.add),
                         reads=[B_r1b, Ba, B_const], writes=[Bb])
                    k.op('dve', lambda e, c=c, a=a, b=b: e.scalar_tensor_tensor(out=a[:], in0=zt_[:, c, 2:T + 2], scalar=gcol(('cw', i), 8 + c), in1=b[:], op0=ALU.mult, op1=ALU.add),
                         reads=[B_r1b, Bb, B_const], writes=[Ba])
                    k.op('dve', lambda e, c=c, a=a: e.tensor_tensor(out=ao[:, c, :], in0=a[:], in1=gbt[:, c, :], op=ALU.mult),
                         reads=[Ba, B_r1c], writes=[B_r1a])
            else:
                k.dma('sp', ao, AO.rearrange("(c p) n -> p c n", p=128)[:, :, t0:t0 + T], B_r1a, reads=[B_AO], writes=[B_r1a])
            for hh in range(2):
                wv, Bw = w_next('mo')
                for cc in range(4):
                    c = hh * 4 + cc
                    py, Bpy = psum_next()
                    for ic in range(8):
                        k.op('pe', lambda e, py=py, wv=wv, ic=ic, cc=cc: e.matmul(
                            py[:], lhsT=wv[:, ic, cc * 128:(cc + 1) * 128], rhs=ao[:, ic, :], start=(ic == 0), stop=(ic == 7)),
                            reads=[Bw, B_r1a], writes=[Bpy])
                    k.op('dve', lambda e, py=py, c=c: e.tensor_tensor(out=x[:, c, :], in0=py[:], in1=x[:, c, :], op=ALU.add),
                         reads=[Bpy, Bx], writes=[Bx])

        def load_cossin(t0):
            k.dma('sp', cosT[:], cos_d[:, t0:t0 + T], B_cs, reads=[B_in], writes=[B_cs])
            k.dma('sp', sinT[:], sin_d[:, t0:t0 + T], B_cs, reads=[B_in], writes=[B_cs])

        if l_in is not None and l_in % 2 == 0:
            arena.off = p4off
            gc_sb = [arena.alloc([T], F32) for _ in range(2)]
            xq = [arena.alloc([T], F32) for _ in range(2)]
            sqb = [arena.alloc([T], BF16) for _ in range(2)]
            rs = [arena.alloc([T], F32) for _ in range(2)]
            t1 = [arena.alloc([T], F32) for _ in range(2)]
            t2 = [arena.alloc([T], F32) for _ in range(2)]
            B_gc = [k.buf() for _ in range(2)]
            B_xq = [k.buf() for _ in range(2)]
            B_sqb = [k.buf() for _ in range(2)]
            B_rs = [k.buf() for _ in range(2)]
            B_t1 = [k.buf() for _ in range(2)]
            B_t2 = [k.buf() for _ in range(2)]
            gb_st = arena.alloc([4, T], F32)
            z_st = arena.alloc([4, T], F32)
            QT_st = arena.alloc([4, T], BF16)
            KT_st = arena.alloc([4, T], BF16)
            V_st = arena.alloc([4, 4, 128], BF16)
            B_gbst, B_zst, B_QTst, B_KTst, B_Vst = (k.buf() for _ in range(5))
        elif l_in is not None:
            arena.off = p4off
            cq = arena.alloc([3, T], F32)
            ckv = arena.alloc([2, T], F32)
            kr = arena.alloc([T], F32)
            krr = arena.alloc([T], F32)
            cqn = arena.alloc([3, T], BF16)
            ckvn = arena.alloc([2, T], BF16)
            sqr_k = arena.alloc([T], BF16)
            B_cq, B_ckv, B_kr, B_krr, B_cqn, B_ckvn, B_sqrk = (k.buf() for _ in range(7))
            qn = [arena.alloc([T], F32) for _ in range(2)]
            qr = [arena.alloc([T], F32) for _ in range(2)]
            sqn = [arena.alloc([T], BF16) for _ in range(2)]
            sqr = [arena.alloc([T], BF16) for _ in range(2)]
            rs = [arena.alloc([T], F32) for _ in range(2)]
            t1 = [arena.alloc([T], F32) for _ in range(2)]
            t2 = [arena.alloc([T], F32) for _ in range(2)]
            B_qn = [k.buf() for _ in range(2)]
            B_qr = [k.buf() for _ in range(2)]
            B_sqn = [k.buf() for _ in range(2)]
            B_sqr = [k.buf() for _ in range(2)]
            B_rs = [k.buf() for _ in range(2)]
            B_t1 = [k.buf() for _ in range(2)]
            B_t2 = [k.buf() for _ in range(2)]
            QTa_st = AT[:, 0:8, :]
            KTa_st = AT[:, 8:16, :]
            QTb_st = r1a.rearrange("p (c t) -> p c t", c=8)
            KTb_st = r1c.bitcast(BF16).rearrange("p (c t) -> p c t", c=8)[:, :, 0:T]
            Vm_st = r1b[:, 0:2048].bitcast(BF16).rearrange("p (h b e) -> p h b e", h=8, b=4)
            B_KTbst, B_Vmst = B_r1c, B_r1b

        def do_proj_even(x, Bx, l, t0, is_prompt, tloc):
            i = l // 2
            rms_to_h(x, Bx, ('mx', l))
            load_cossin(t0)
            wv, Bw = w_next('ein')
            for c in range(4):
                pp, Bp = psum_next()
                for kc in range(8):
                    k.op('pe', lambda e, pp=pp, wv=wv, kc=kc, c=c: e.matmul(pp[:], lhsT=wv[:, kc, c * 128:(c + 1) * 128], rhs=hT[:, kc, :],
                                                                        start=(kc == 0), stop=(kc == 7)), reads=[Bw, B_h], writes=[Bp])
                k.op('act', lambda e, pp=pp, c=c: e.copy(out=gb_st[:, c, :], in_=pp[:]), reads=[Bp], writes=[B_gbst])
            k.dma('pool', gbD.rearrange("(c p) n -> p c n", p=128)[:, :, t0:t0 + T], gb_st, B_gbst, reads=[B_gbst], writes=[B_gb])
            wvc, Bwc = w_next('ein')
            wvu, Bwu = w_next('ein', hold=1)
            for c in range(4):
                pc, Bpc = psum_next()
                pu, Bpu = psum_next()
                for (pp, Bp, wv, Bw) in ((pc, Bpc, wvc, Bwc), (pu, Bpu, wvu, Bwu)):
                    for kc in range(8):
                        k.op('pe', lambda e, pp=pp, wv=wv, kc=kc, c=c: e.matmul(pp[:], lhsT=wv[:, kc, c * 128:(c + 1) * 128], rhs=hT[:, kc, :],
                                                                            start=(kc == 0), stop=(kc == 7)), reads=[Bw, B_h], writes=[Bp])
                g_, Bg = gc_sb[c % 2], B_gc[c % 2]
                k.op('act', lambda e, pc=pc, g_=g_: e.copy(out=g_[:], in_=pc[:]), reads=[Bpc], writes=[Bg])
                k.op('dve', lambda e, pu=pu, g_=g_, c=c: e.tensor_tensor(out=z_st[:, c, :], in0=g_[:], in1=pu[:], op=ALU.mult),
                     reads=[Bg, Bpu], writes=[B_zst])
            zdst = (zP if is_prompt else zS).rearrange("(c p) n -> p c n", p=128)
            k.dma('pool', zdst[:, :, 1 + tloc:1 + tloc + T], z_st, B_zst, reads=[B_zst], writes=[B_z])
            for which, (st, Bst, gA, gB) in enumerate(((QT_st, B_QTst, ('qA', i), ('qB', i)), (KT_st, B_KTst, ('kA', i), ('kB', i)))):
                wv, Bw = w_next('ein')
                for c in range(4):
                    ii = c % 2
                    pp, Bp = psum_next()
                    for kc in range(8):
                        k.op('pe', lambda e, pp=pp, wv=wv, kc=kc, c=c: e.matmul(pp[:], lhsT=wv[:, kc, c * 128:(c + 1) * 128], rhs=hT[:, kc, :],
                                                                            start=(kc == 0), stop=(kc == 7)), reads=[Bw, B_h], writes=[Bp])
                    k.op('act', lambda e, pp=pp, ii=ii: e.copy(out=xq[ii][:], in_=pp[:]), reads=[Bp], writes=[B_xq[ii]])
                    k.op('dve', lambda e, ii=ii: e.tensor_tensor(out=sqb[ii][:], in0=xq[ii][:], in1=xq[ii][:], op=ALU.mult),
                         reads=[B_xq[ii]], writes=[B_sqb[ii]])
                    pss, Bpss = psum_next()
                    k.op('pe', lambda e, pss=pss, ii=ii: e.matmul(pss[:], lhsT=bd64[:], rhs=sqb[ii][:], start=True, stop=True),
                         reads=[B_sqb[ii], B_const], writes=[Bpss])
                    prot, Bprot = psum_next()
                    k.op('pe', lambda e, prot=prot, ii=ii: e.matmul(prot[:], lhsT=rswap[:], rhs=xq[ii][:], start=True, stop=True),
                         reads=[B_xq[ii], B_const], writes=[Bprot])
                    k.op('dve', lambda e, pss=pss, ii=ii: e.tensor_scalar(out=rs[ii][:], in0=pss[:], scalar1=float(64 * EPS), scalar2=None, op0=ALU.add),
                         reads=[Bpss], writes=[B_rs[ii]])
                    k.op('act', lambda e, ii=ii: e.activation(out=rs[ii][:], in_=rs[ii][:], func=AF.Ln), reads=[B_rs[ii]], writes=[B_rs[ii]])
                    k.op('act', lambda e, ii=ii: e.activation(out=rs[ii][:], in_=rs[ii][:], func=AF.Exp, scale=-0.5), reads=[B_rs[ii]], writes=[B_rs[ii]])
                    k.op('dve', lambda e, ii=ii, gA=gA: e.scalar_tensor_tensor(out=t1[ii][:], in0=xq[ii][:], scalar=gcol(gA), in1=cosT[:],
                                                                           op0=ALU.mult, op1=ALU.mult),
                         reads=[B_xq[ii], B_cs, B_const], writes=[B_t1[ii]])
                    k.op('dve', lambda e, ii=ii, gB=gB, prot=prot: e.scalar_tensor_tensor(out=t2[ii][:], in0=prot[:], scalar=gcol(gB), in1=sinT[:],
                                                                                      op0=ALU.mult, op1=ALU.mult),
                         reads=[Bprot, B_cs, B_const], writes=[B_t2[ii]])
                    k.op('dve', lambda e, ii=ii: e.tensor_tensor(out=t1[ii][:], in0=t1[ii][:], in1=t2[ii][:], op=ALU.add),
                         reads=[B_t1[ii], B_t2[ii]], writes=[B_t1[ii]])
                    k.op('dve', lambda e, ii=ii, st=st, c=c: e.tensor_tensor(out=st[:, c, :], in0=t1[ii][:], in1=rs[ii][:], op=ALU.mult),
                         reads=[B_t1[ii], B_rs[ii]], writes=[Bst])
                if which == 0:
                    k.dma('pool', QTa[0:512, :].rearrange("(c p) n -> p c n", p=128)[:, :, t0:t0 + T], st, Bst, reads=[Bst], writes=[B_Q])
                else:
                    dstK = (KTa_p if is_prompt else KTa_s)[0:512, :].rearrange("(c p) n -> p c n", p=128)
                    k.dma('pool', dstK[:, :, tloc:tloc + T], st, Bst, reads=[Bst], writes=[B_Kp if is_prompt else B_Ks])
            wv, Bw = w_next('ein')
            for b in range(4):
                pp, Bp = psum_next()
                for kc in range(8):
                    k.op('pe', lambda e, pp=pp, wv=wv, kc=kc, b=b: e.matmul(pp[:], lhsT=hT[:, kc, b * 128:(b + 1) * 128], rhs=wv[:, kc, :],
                                                                        start=(kc == 0), stop=(kc == 7)), reads=[Bw, B_h], writes=[Bp])
                k.op('act', lambda e, pp=pp, b=b: e.copy(out=V_st[:, :, b, :], in_=pp[:].rearrange("p (h e) -> p h e", h=4)),
                     reads=[Bp], writes=[B_Vst])
            vd = (V_p if is_prompt else V_s)[0:512, :].rearrange("(h p) (b e) -> p h b e", p=128, e=128)
            b0 = tloc // 128
            k.dma('pool', vd[:, :, b0:b0 + 4, :], V_st, B_Vst, reads=[B_Vst], writes=[B_Kp if is_prompt else B_Ks])

        def do_proj_mla(x, Bx, l, t0, is_prompt, tloc):
            i = l // 2
            rms_to_h(x, Bx, ('mx', l))
            load_cossin(t0)
            wv, Bw = w_next('mdown')
            for c in range(6):
                pp, Bp = psum_next()
                m = 128 if c < 5 else 64
                for kc in range(8):
                    k.op('pe', lambda e, pp=pp, wv=wv, kc=kc, c=c, m=m: e.matmul(pp[0:m, :], lhsT=wv[:, kc, c * 128:c * 128 + m], rhs=hT[:, kc, :],
                                                                             start=(kc == 0), stop=(kc == 7)), reads=[Bw, B_h], writes=[Bp])
                if c < 3:
                    k.op('act', lambda e, pp=pp, c=c: e.copy(out=cq[:, c, :], in_=pp[:]), reads=[Bp], writes=[B_cq])
                elif c < 5:
                    k.op('act', lambda e, pp=pp, c=c: e.copy(out=ckv[:, c - 3, :], in_=pp[:]), reads=[Bp], writes=[B_ckv])
                else:
                    k.op('act', lambda e, pp=pp: e.copy(out=kr[0:64, :], in_=pp[0:64, :]), reads=[Bp], writes=[B_kr])
            rms_to_h(cq, B_cq, ('ql', i), nchunk=3, n=384, dst=cqn, Bdst=B_cqn)
            rms_to_h(ckv, B_ckv, ('kvl', i), nchunk=2, n=256, dst=ckvn, Bdst=B_ckvn)
            k.op('dve', lambda e: e.tensor_tensor(out=sqr_k[0:64, :], in0=kr[0:64, :], in1=kr[0:64, :], op=ALU.mult),
                 reads=[B_kr], writes=[B_sqrk])
            prot, Bprot = psum_next()
            k.op('pe', lambda e, prot=prot: e.matmul(prot[0:64, :], lhsT=rswap[0:64, 0:64], rhs=kr[0:64, :], start=True, stop=True),
                 reads=[B_kr, B_const], writes=[Bprot])
            k.op('dve', lambda e: e.scalar_tensor_tensor(out=krr[0:64, :], in0=kr[0:64, :], scalar=gs[0:64, gcols[('krA', i)]:gcols[('krA', i)] + 1],
                                                       in1=cosT[0:64, :], op0=ALU.mult, op1=ALU.mult),
                 reads=[B_kr, B_cs, B_const], writes=[B_krr])
            k.op('dve', lambda e, prot=prot: e.scalar_tensor_tensor(out=t2[0][0:64, :], in0=prot[0:64, :], scalar=gs[0:64, gcols[('krB', i)]:gcols[('krB', i)] + 1],
                                                                  in1=sinT[0:64, :], op0=ALU.mult, op1=ALU.mult),
                 reads=[Bprot, B_cs, B_const], writes=[B_t2[0]])
            k.op('dve', lambda e: e.tensor_tensor(out=krr[0:64, :], in0=krr[0:64, :], in1=t2[0][0:64, :], op=ALU.add),
                 reads=[B_krr, B_t2[0]], writes=[B_krr])
            wv, Bw = w_next('muq')
            for h in range(8):
                ii = h % 2
                pn, Bpn = psum_next()
                pr, Bpr = psum_next()
                for kc in range(3):
                    k.op('pe', lambda e, pn=pn, wv=wv, kc=kc, h=h: e.matmul(pn[:], lhsT=wv[:, kc, h * 192:h * 192 + 128], rhs=cqn[:, kc, :],
                                                                        start=(kc == 0), stop=(kc == 2)), reads=[Bw, B_cqn], writes=[Bpn])
                for kc in range(3):
                    k.op('pe', lambda e, pr=pr, wv=wv, kc=kc, h=h: e.matmul(pr[0:64, :], lhsT=wv[:, kc, h * 192 + 128:h * 192 + 192], rhs=cqn[:, kc, :],
                                                                        start=(kc == 0), stop=(kc == 2)), reads=[Bw, B_cqn], writes=[Bpr])
                k.op('act', lambda e, pn=pn, ii=ii: e.copy(out=qn[ii][:], in_=pn[:]), reads=[Bpn], writes=[B_qn[ii]])
                k.op('act', lambda e, pr=pr, ii=ii: e.copy(out=qr[ii][0:64, :], in_=pr[0:64, :]), reads=[Bpr], writes=[B_qr[ii]])
                k.op('dve', lambda e, ii=ii: e.tensor_tensor(out=sqn[ii][:], in0=qn[ii][:], in1=qn[ii][:], op=ALU.mult),
                     reads=[B_qn[ii]], writes=[B_sqn[ii]])
                k.op('dve', lambda e, ii=ii: e.tensor_tensor(out=sqr[ii][0:64, :], in0=qr[ii][0:64, :], in1=qr[ii][0:64, :], op=ALU.mult),
                     reads=[B_qr[ii]], writes=[B_sqr[ii]])
                pss, Bpss = psum_next()
                k.op('pe', lambda e, pss=pss, ii=ii: e.matmul(pss[:], lhsT=onesb[:], rhs=sqn[ii][:], start=True, stop=False),
                     reads=[B_sqn[ii], B_const], writes=[Bpss])
                k.op('pe', lambda e, pss=pss, ii=ii: e.matmul(pss[:], lhsT=onesb[0:64, :], rhs=sqr[ii][0:64, :], start=False, stop=True),
                     reads=[B_sqr[ii], B_const], writes=[Bpss])
                prot, Bprot = psum_next()
                k.op('pe', lambda e, prot=prot, ii=ii: e.matmul(prot[0:64, :], lhsT=rswap[0:64, 0:64], rhs=qr[ii][0:64, :], start=True, stop=True),
                     reads=[B_qr[ii], B_const], writes=[Bprot])
                k.op('dve', lambda e, pss=pss, ii=ii: e.tensor_scalar(out=rs[ii][:], in0=pss[:], scalar1=float(192 * EPS), scalar2=None, op0=ALU.add),
                     reads=[Bpss], writes=[B_rs[ii]])
                k.op('act', lambda e, ii=ii: e.activation(out=rs[ii][:], in_=rs[ii][:], func=AF.Ln), reads=[B_rs[ii]], writes=[B_rs[ii]])
                k.op('act', lambda e, ii=ii: e.activation(out=rs[ii][:], in_=rs[ii][:], func=AF.Exp, scale=-0.5), reads=[B_rs[ii]], writes=[B_rs[ii]])
                k.op('dve', lambda e, ii=ii, h=h: e.scalar_tensor_tensor(out=QTa_st[:, h, :], in0=qn[ii][:], scalar=gcol(('qn', i)), in1=rs[ii][:],
                                                                     op0=ALU.mult, op1=ALU.mult),
                     reads=[B_qn[ii], B_rs[ii], B_const], writes=[B_AT])
                k.op('dve', lambda e, ii=ii: e.scalar_tensor_tensor(out=t1[ii][0:64, :], in0=qr[ii][0:64, :], scalar=gs[0:64, gcols[('qrA', i)]:gcols[('qrA', i)] + 1],
                                                                  in1=cosT[0:64, :], op0=ALU.mult, op1=ALU.mult),
                     reads=[B_qr[ii], B_cs, B_const], writes=[B_t1[ii]])
                k.op('dve', lambda e, ii=ii, prot=prot: e.scalar_tensor_tensor(out=t2[ii][0:64, :], in0=prot[0:64, :], scalar=gs[0:64, gcols[('qrB', i)]:gcols[('qrB', i)] + 1],
                                                                            in1=sinT[0:64, :], op0=ALU.mult, op1=ALU.mult),
                     reads=[Bprot, B_cs, B_const], writes=[B_t2[ii]])
                k.op('dve', lambda e, ii=ii: e.tensor_tensor(out=t1[ii][0:64, :], in0=t1[ii][0:64, :], in1=t2[ii][0:64, :], op=ALU.add),
                     reads=[B_t1[ii], B_t2[ii]], writes=[B_t1[ii]])
                k.op('dve', lambda e, ii=ii, h=h: e.tensor_tensor(out=QTb_st[0:64, h, :], in0=t1[ii][0:64, :], in1=rs[ii][0:64, :], op=ALU.mult),
                     reads=[B_t1[ii], B_rs[ii]], writes=[B_r1a])
            k.dma('pool', QTa.rearrange("(h p) n -> p h n", p=128)[:, :, t0:t0 + T], QTa_st, B_AT, reads=[B_AT], writes=[B_Q])
            k.dma('pool', QTb.rearrange("(h p) n -> p h n", p=64)[:, :, t0:t0 + T], QTb_st[0:64, :, :], B_r1a, reads=[B_r1a], writes=[B_Q])
            wv, Bw = w_next('mukv')
            for h in range(8):
                ii = h % 2
                pn, Bpn = psum_next()
                for kc in range(2):
                    k.op('pe', lambda e, pn=pn, wv=wv, kc=kc, h=h: e.matmul(pn[:], lhsT=wv[:, kc, h * 128:(h + 1) * 128], rhs=ckvn[:, kc, :],
                                                                        start=(kc == 0), stop=(kc == 1)), reads=[Bw, B_ckvn], writes=[Bpn])
                k.op('act', lambda e, pn=pn, ii=ii: e.copy(out=qn[ii][:], in_=pn[:]), reads=[Bpn], writes=[B_qn[ii]])
                k.op('dve', lambda e, ii=ii: e.tensor_tensor(out=sqn[ii][:], in0=qn[ii][:], in1=qn[ii][:], op=ALU.mult),
                     reads=[B_qn[ii]], writes=[B_sqn[ii]])
                pss, Bpss = psum_next()
                k.op('pe', lambda e, pss=pss, ii=ii: e.matmul(pss[:], lhsT=onesb[:], rhs=sqn[ii][:], start=True, stop=False),
                     reads=[B_sqn[ii], B_const], writes=[Bpss])
                k.op('pe', lambda e, pss=pss: e.matmul(pss[:], lhsT=onesb[0:64, :], rhs=sqr_k[0:64, :], start=False, stop=True),
                     reads=[B_sqrk, B_const], writes=[Bpss])
                k.op('dve', lambda e, pss=pss, ii=ii: e.tensor_scalar(out=rs[ii][:], in0=pss[:], scalar1=float(192 * EPS), scalar2=None, op0=ALU.add),
                     reads=[Bpss], writes=[B_rs[ii]])
                k.op('act', lambda e, ii=ii: e.activation(out=rs[ii][:], in_=rs[ii][:], func=AF.Ln), reads=[B_rs[ii]], writes=[B_rs[ii]])
                k.op('act', lambda e, ii=ii: e.activation(out=rs[ii][:], in_=rs[ii][:], func=AF.Exp, scale=-0.5), reads=[B_rs[ii]], writes=[B_rs[ii]])
                k.op('dve', lambda e, ii=ii, h=h: e.scalar_tensor_tensor(out=KTa_st[:, h, :], in0=qn[ii][:], scalar=gcol(('kn', i)), in1=rs[ii][:],
                                                                     op0=ALU.mult, op1=ALU.mult),
                     reads=[B_qn[ii], B_rs[ii], B_const], writes=[B_AT])
                k.op('dve', lambda e, ii=ii, h=h: e.tensor_tensor(out=KTb_st[0:64, h, :], in0=krr[0:64, :], in1=rs[ii][0:64, :], op=ALU.mult),
                     reads=[B_krr, B_rs[ii]], writes=[B_KTbst])
            dKa = (KTa_p if is_prompt else KTa_s).rearrange("(h p) n -> p h n", p=128)
            dKb = (KTb_p if is_prompt else KTb_s).rearrange("(h p) n -> p h n", p=64)
            BK = B_Kp if is_prompt else B_Ks
            k.dma('pool', dKa[:, :, tloc:tloc + T], KTa_st, B_AT, reads=[B_AT], writes=[BK])
            k.dma('pool', dKb[:, :, tloc:tloc + T], KTb_st[0:64, :, :], B_KTbst, reads=[B_KTbst], writes=[BK])
            for b in range(4):
                for hf in range(2):
                    pp, Bp = psum_next()
                    for kc in range(2):
                        k.op('pe', lambda e, pp=pp, wv=wv, kc=kc, b=b, hf=hf: e.matmul(
                            pp[:], lhsT=ckvn[:, kc, b * 128:(b + 1) * 128], rhs=wv[:, kc, 1024 + hf * 512:1024 + (hf + 1) * 512],
                            start=(kc == 0), stop=(kc == 1)), reads=[Bw, B_ckvn], writes=[Bp])
                    k.op('act', lambda e, pp=pp, b=b, hf=hf: e.copy(out=Vm_st[:, hf * 4:(hf + 1) * 4, b, :], in_=pp[:].rearrange("p (h e) -> p h e", h=4)),
                         reads=[Bp], writes=[B_Vmst])
            vd = (V_p if is_prompt else V_s).rearrange("(h p) (b e) -> p h b e", p=128, e=128)
            b0 = tloc // 128
            k.dma('pool', vd[:, :, b0:b0 + 4, :], Vm_st, B_Vmst, reads=[B_Vmst], writes=[BK])

        ntile = len(tiles)

        def load_x(ti):
            t0 = ti * T
            xb, Bxb = xT[ti % 2], B_x[ti % 2]
            if p == 0:
                return
            k.dma('sp', xb, xres[:, :, t0:t0 + T].rearrange("c p n -> p c n"), Bxb, reads=[B_xres[ti]], writes=[Bxb])

        if p > 0:
            load_x(0)
        for ti in tiles:
            t0 = ti * T
            is_prompt = t0 < NP
            tloc = t0 if is_prompt else t0 - NP
            x, Bx = xT[ti % 2], B_x[ti % 2]
            if p == 0:
                tm = r1a.bitcast(F32)
                for half in range(2):
                    tmv = tm.rearrange("p (b f) -> p b f", b=4)
                    k.dma('sp', tmv, x_in[t0:t0 + T, half * 512:(half + 1) * 512].rearrange("(b p) f -> p b f", p=128), B_r1a,
                          reads=[B_in], writes=[B_r1a])
                    for cc in range(4):
                        c = half * 4 + cc
                        pp, Bp = psum_next()
                        for b in range(4):
                            k.op('pe', lambda e, pp=pp, b=b, cc=cc, tmv=tmv: e.transpose(pp[:, b * 128:(b + 1) * 128], tmv[:, b, cc * 128:(cc + 1) * 128], ident[:]),
                                 reads=[B_r1a, B_const], writes=[Bp])
                        k.op('act', lambda e, pp=pp, c=c, x=x: e.copy(out=x[:, c, :], in_=pp[:]), reads=[Bp], writes=[Bx])
            else:
                if ti + 1 < ntile:
                    load_x(ti + 1)
            if l_out >= 0:
                if 'nomix' in DBG:
                    w_next('mo'); w_next('mo')
                else:
                    mixer_out(x, Bx, l_out, t0, is_prompt, tloc)
                if 'noffn' in DBG:
                    for _ in range(11): w_next('win')
                    for _ in range(4): w_next('wout')
                else:
                    ffn(x, Bx, l_out, 2)
            if l_in is not None:
                if 'noffn' in DBG:
                    for _ in range(11): w_next('win')
                    for _ in range(4): w_next('wout')
                else:
                    ffn(x, Bx, l_in, 1)
                if 'nomix' in DBG:
                    if l_in % 2 == 0:
                        for _ in range(6): w_next('ein')
                    else:
                        w_next('mdown'); w_next('muq'); w_next('mukv')
                elif l_in % 2 == 0:
                    do_proj_even(x, Bx, l_in, t0, is_prompt, tloc)
                else:
                    do_proj_mla(x, Bx, l_in, t0, is_prompt, tloc)
                k.dma('pool', xres[:, :, t0:t0 + T].rearrange("c p n -> p c n"), x, Bx, reads=[Bx], writes=[B_xres[ti]])
            else:
                tm = r1a.bitcast(F32).rearrange("p (b f) -> p b f", b=4)
                for half in range(2):
                    for b in range(4):
                        pp, Bp = psum_next()
                        for cc in range(4):
                            c = half * 4 + cc
                            k.op('pe', lambda e, pp=pp, b=b, cc=cc, c=c, x=x: e.transpose(pp[:, cc * 128:(cc + 1) * 128], x[:, c, b * 128:(b + 1) * 128], ident[:]),
                                 reads=[Bx, B_const], writes=[Bp])
                        k.op('act', lambda e, pp=pp, b=b: e.copy(out=tm[:, b, :], in_=pp[:]), reads=[Bp], writes=[B_r1a])
                    ev = k.dma('pool', y_out[t0:t0 + T, half * 512:(half + 1) * 512].rearrange("(b p) f -> p b f", p=128), tm, B_r1a,
                               reads=[B_r1a], writes=[B_out])
                    final_events.append(ev)
        assert wcur['g'] == total_units, (wcur['g'], total_units)

    B_out = k.buf("out")
    final_events = []

    RG = [[0, 1, 2, 3], [4, 5, 6, 7]]

    def allgather(src, dst, Bsrc, Bdst):
        def fn(e, src=src, dst=dst):
            return e.collective_compute("AllGather", ALU.bypass, replica_groups=RG, ins=[src], outs=[dst])
        if not isinstance(Bdst, list):
            Bdst = [Bdst]
        k.dma('pool', None, None, B_cc, reads=[Bsrc], writes=Bdst, inc=1, fn=fn)

    def exchange(l):
        even = (l % 2 == 0)
        nh = 4 if even else 8
        if even:
            zv = zP.rearrange("r (n o) -> r n o", o=1)
            zev = zedge.rearrange("r (n o) -> r n o", o=1)
            k.dma('pool', zev[:, 0:1, :], zv[:, 1:2, :], B_ze, reads=[B_z], writes=[B_ze], slow=True)
            k.dma('pool', zev[:, 1:2, :], zv[:, NP:NP + 1, :], B_ze, reads=[B_z], writes=[B_ze], slow=True)
            allgather(zedge, zedge_g, B_ze, B_zeg)
        for h in range(nh):
            allgather(KTa_p[h * 128:(h + 1) * 128, :], KTa_g[h * G * 128:(h + 1) * G * 128, :], B_Kp, [B_Kg[h]])
            if not even and h % 2 == 0:
                hp = h // 2
                allgather(KTb_p[hp * 128:(hp + 1) * 128, :], KTb_g[hp * G * 128:(hp + 1) * G * 128, :], B_Kp, [B_Kg[h], B_Kg[h + 1]])
            allgather(V_p[h * 128:(h + 1) * 128, :], V_g[h * G * 128:(h + 1) * G * 128, :], B_Kp, [B_Kg[h]])

    def halo_fix(l):
        Ev = Eg[:].rearrange("p (r c k) -> p r c k", r=G, c=4)
        k.dma('sp', Ev, zedge_g.rearrange("(r c p) k -> p r c k", r=G, c=4)[:, :, :, 0:2], B_halo, reads=[B_zeg], writes=[B_halo], slow=True)
        hv = halo[:].rearrange("p (c s) -> p c s", s=2)
        for side in range(2):
            kk = 1 - side
            for r in range(G):
                if r == 0:
                    k.op('dve', lambda e, side=side, kk=kk, r=r: e.tensor_scalar(out=hv[:, :, side], in0=Ev[:, r, :, kk], scalar1=hmask[:, side * G + r:side * G + r + 1],
                                                                             scalar2=None, op0=ALU.mult), reads=[B_halo, B_const], writes=[B_halo])
                else:
                    k.op('dve', lambda e, side=side, kk=kk, r=r: e.scalar_tensor_tensor(out=hv[:, :, side], in0=Ev[:, r, :, kk], scalar=hmask[:, side * G + r:side * G + r + 1],
                                                                                    in1=hv[:, :, side], op0=ALU.mult, op1=ALU.add),
                         reads=[B_halo, B_const], writes=[B_halo])
        zPv = zP.rearrange("(c p) n -> p c n", p=128)
        zSv = zS.rearrange("(c p) n -> p c n", p=128)
        k.dma('sp', zPv[:, :, 0:1], hv[:, :, 0:1], B_halo, reads=[B_halo], writes=[B_z], slow=True)
        k.dma('sp', zPv[:, :, NP + 1:NP + 2], hv[:, :, 1:2], B_halo, reads=[B_halo], writes=[B_z], slow=True)
        ztv = zt[:, 0:4].rearrange("p (c o) -> p c o", o=1)
        k.dma('sp', zSv[:, :, 0:1], ztv, B_halo, reads=[B_const], writes=[B_z], slow=True)
        k.dma('sp', zSv[:, :, NS + 1:NS + 2], ztv, B_halo, reads=[B_const], writes=[B_z], slow=True)

    def attention(l):
        even = (l % 2 == 0)
        i = l // 2
        nh = 4 if even else 8
        nmap = 2 if even else 1
        scale = (64 ** -0.5) if even else (192 ** -0.5)
        arena.reset()
        k.scope('at')
        NQM = max(NP, NS)
        SEGM = NQM
        qa = [arena.alloc([NQM], BF16) for _ in range(2)]
        B_qa = [k.buf() for _ in range(2)]
        if even:
            qa1 = [arena.alloc([NQM], BF16) for _ in range(2)]
            qam = [qa, qa1]
            for m_ in range(2):
                for s_ in range(2):
                    k.op('dve', lambda e, m_=m_, s_=s_: e.memset(qam[m_][s_][:], 0.0), writes=[B_qa[s_]])
        if not even:
            qb = [arena.alloc([NQM], BF16) for _ in range(2)]
            B_qb = [k.buf() for _ in range(2)]
        NKS = 2
        ka = [arena.alloc([SEGM], BF16) for _ in range(NKS)]
        B_ka = [k.buf() for _ in range(NKS)]
        if not even:
            kb_ = [arena.alloc([SEGM], BF16) for _ in range(NKS)]
            B_kb = [k.buf() for _ in range(NKS)]
        vv = [arena.alloc([SEGM // 128, 128], BF16) for _ in range(NKS)]
        B_vv = [k.buf() for _ in range(NKS)]
        NPT = 6
        pt = [arena.alloc([512], BF16) for _ in range(NPT)]
        B_pt = [k.buf() for _ in range(NPT)]
        NPS = 3
        pts = [arena.alloc([512], BF16) for _ in range(NPS)]
        B_pts = [k.buf() for _ in range(NPS)]
        acc_o = [arena.alloc([NQM], F32) for _ in range(nmap)]
        acc_l = [arena.alloc([NQM], F32) for _ in range(nmap)]
        B_acc = [k.buf() for _ in range(nmap)]
        ost = [arena.alloc([NQM], BF16) for _ in range(2)]
        B_ost = [k.buf() for _ in range(2)]
        f1 = [arena.alloc([512], F32) for _ in range(2)]
        f2 = [arena.alloc([512], F32) for _ in range(2)]
        f3 = [arena.alloc([512], BF16) for _ in range(2)]
        B_f1 = [k.buf() for _ in range(2)]
        B_f2 = [k.buf() for _ in range(2)]
        B_f3 = [k.buf() for _ in range(2)]
        ps_s = [(ps[j], B_ps[j]) for j in range(3)]
        ps_o = [(ps[3 + j], B_ps[3 + j]) for j in range(2)]
        ps_l = [(ps[5 + j], B_ps[5 + j]) for j in range(2)]
        ps_f = (ps[7], B_ps[7])
        cnt = {'s': 0, 'pt': 0, 'ol': 0, 'seg': 0, 'job': 0}

        jobs = []
        for h in range(nh):
            jobs.append(('s', h))
        for h in range(nh):
            jobs.append(('p', h))

        def load_q(ji):
            kind, h = jobs[ji]
            n0, nq = (NP, NS) if kind == 's' else (0, NP)
            s = ji % 2
            if even:
                for m_ in range(2):
                    k.dma('sp', qam[m_][s][m_ * 64:(m_ + 1) * 64, 0:nq], QTa[h * 128 + m_ * 64:h * 128 + (m_ + 1) * 64, n0:n0 + nq],
                          B_qa[s], reads=[B_Q], writes=[B_qa[s]])
            else:
                k.dma('sp', qa[s][:, 0:nq], QTa[h * 128:(h + 1) * 128, n0:n0 + nq], B_qa[s], reads=[B_Q], writes=[B_qa[s]])
            if not even:
                k.dma('sp', qb[s][0:64, 0:nq], QTb[h * 64:(h + 1) * 64, n0:n0 + nq], B_qb[s], reads=[B_Q], writes=[B_qb[s]])

        segs = []
        for ji, (kind, h) in enumerate(jobs):
            if kind == 's':
                segs.append((ji, 's', h, 0, NS))
            else:
                for r in range(G):
                    segs.append((ji, 'p', h, r, NP))
        seg_loaded = {'n': 0}

        def load_seg(si):
            ji, kind, h, r, nk = segs[si]
            s = si % NKS
            if kind == 's':
                srcKa = KTa_s[h * 128:(h + 1) * 128, :]
                srcV = V_s[h * 128:(h + 1) * 128, :]
                BK = B_Ks
                if not even:
                    srcKb = KTb_s[h * 64:(h + 1) * 64, :]
            else:
                srcKa = KTa_g[(h * G + r) * 128:(h * G + r + 1) * 128, :]
                srcV = V_g[(h * G + r) * 128:(h * G + r + 1) * 128, :]
                BK = B_Kg[h]
                if not even:
                    rb = ((h // 2) * G + r) * 128 + (h % 2) * 64
                    srcKb = KTb_g[rb:rb + 64, :]
            k.dma('sp', ka[s][:, 0:nk], srcKa, B_ka[s], reads=[BK], writes=[B_ka[s]])
            if not even:
                k.dma('sp', kb_[s][0:64, 0:nk], srcKb, B_kb[s], reads=[BK], writes=[B_kb[s]])
            k.dma('sp', vv[s][:, 0:nk // 128, :], srcV.rearrange("p (b e) -> p b e", e=128), B_vv[s], reads=[BK], writes=[B_vv[s]])

        def seg_ensure(upto):
            while seg_loaded['n'] <= min(upto, len(segs) - 1):
                load_seg(seg_loaded['n'])
                seg_loaded['n'] += 1

        def finalize(ji, h, nq, n0, nqc):
            osel = ji % 2
            for qc in range(nqc):
                sl = slice(qc * 512, (qc + 1) * 512)
                fi = qc % 2
                if not even:
                    k.op('dve', lambda e, sl=sl, fi=fi: e.reciprocal(out=f1[fi][:], in_=acc_l[0][:, sl]), reads=[B_acc[0]], writes=[B_f1[fi]])
                    k.op('dve', lambda e, sl=sl, fi=fi, osel=osel: e.tensor_tensor(out=ost[osel][:, sl], in0=acc_o[0][:, sl], in1=f1[fi][:], op=ALU.mult),
                         reads=[B_acc[0], B_f1[fi]], writes=[B_ost[osel]])
                else:
                    k.op('dve', lambda e, sl=sl, fi=fi: e.reciprocal(out=f1[fi][:], in_=acc_l[0][:, sl]), reads=[B_acc[0]], writes=[B_f1[fi]])
                    k.op('dve', lambda e, sl=sl, fi=fi: e.tensor_tensor(out=f1[fi][:], in0=acc_o[0][:, sl], in1=f1[fi][:], op=ALU.mult),
                         reads=[B_acc[0], B_f1[fi]], writes=[B_f1[fi]])
                    k.op('dve', lambda e, sl=sl, fi=fi: e.reciprocal(out=f2[fi][:], in_=acc_l[1][:, sl]), reads=[B_acc[1]], writes=[B_f2[fi]])
                    k.op('dve', lambda e, sl=sl, fi=fi: e.tensor_tensor(out=f2[fi][:], in0=acc_o[1][:, sl], in1=f2[fi][:], op=ALU.mult),
                         reads=[B_acc[1], B_f2[fi]], writes=[B_f2[fi]])
                    k.op('dve', lambda e, fi=fi: e.scalar_tensor_tensor(out=f1[fi][:], in0=f2[fi][:], scalar=lamw[:, 8 * i + 5:8 * i + 6], in1=f1[fi][:],
                                                                      op0=ALU.mult, op1=ALU.add),
                         reads=[B_f1[fi], B_f2[fi], B_const], writes=[B_f1[fi]])
                    k.op('dve', lambda e, fi=fi: e.tensor_tensor(out=f3[fi][:], in0=f1[fi][:], in1=f1[fi][:], op=ALU.mult),
                         reads=[B_f1[fi]], writes=[B_f3[fi]])
                    pf, Bpf = ps_f
                    k.op('pe', lambda e, pf=pf, fi=fi: e.matmul(pf[:], lhsT=onesb[:], rhs=f3[fi][:], start=True, stop=True),
                         reads=[B_f3[fi], B_const], writes=[Bpf])
                    k.op('dve', lambda e, pf=pf, fi=fi: e.tensor_scalar(out=f2[fi][:], in0=pf[:], scalar1=float(128 * EPS), scalar2=None, op0=ALU.add),
                         reads=[Bpf], writes=[B_f2[fi]])
                    k.op('act', lambda e, fi=fi: e.activation(out=f2[fi][:], in_=f2[fi][:], func=AF.Ln), reads=[B_f2[fi]], writes=[B_f2[fi]])
                    k.op('act', lambda e, fi=fi: e.activation(out=f2[fi][:], in_=f2[fi][:], func=AF.Exp, scale=-0.5), reads=[B_f2[fi]], writes=[B_f2[fi]])
                    k.op('dve', lambda e, fi=fi, sl=sl, osel=osel: e.scalar_tensor_tensor(out=ost[osel][:, sl], in0=f1[fi][:], scalar=gcol(('sub', i)), in1=f2[fi][:],
                                                                                      op0=ALU.mult, op1=ALU.mult),
                         reads=[B_f1[fi], B_f2[fi], B_const], writes=[B_ost[osel]])
            chunk = (4 + h) if even else h
            k.dma('pool', AO[chunk * 128:(chunk + 1) * 128, n0:n0 + nq], ost[osel][:, 0:nq], B_ost[osel], reads=[B_ost[osel]], writes=[B_AO])

        LAG = 2
        tasks = []

        def mk_qk(pss, Bpss, ks, qs, kb, qc, m, pti):
            def fn():
                if even:
                    k.op('pe', lambda e: e.matmul(
                        pss[:], lhsT=ka[ks][:, kb * 128:(kb + 1) * 128],
                        rhs=qam[m][qs][:, qc * 512:(qc + 1) * 512], start=True, stop=True),
                        reads=[B_ka[ks], B_qa[qs]], writes=[Bpss])
                else:
                    k.op('pe', lambda e: e.matmul(
                        pss[:], lhsT=ka[ks][:, kb * 128:(kb + 1) * 128], rhs=qa[qs][:, qc * 512:(qc + 1) * 512],
                        start=True, stop=False), reads=[B_ka[ks], B_qa[qs]], writes=[Bpss])
                    k.op('pe', lambda e: e.matmul(
                        pss[:], lhsT=kb_[ks][0:64, kb * 128:(kb + 1) * 128], rhs=qb[qs][0:64, qc * 512:(qc + 1) * 512],
                        start=False, stop=True), reads=[B_kb[ks], B_qb[qs]], writes=[Bpss])
                k.op('act', lambda e: e.activation(out=pt[pti][:], in_=pss[:], func=AF.Exp, scale=float(scale)),
                     reads=[Bpss], writes=[B_pt[pti]])
            return fn

        def mk_pv(po, Bpo, pl, Bpl, ks, kb, pti, nkb, pti_prev, psi, psi_prev):
            def fn():
                k.op('pe', lambda e: e.matmul(
                    po[:], lhsT=vv[ks][:, kb, :], rhs=pt[pti][:], start=(kb == 0), stop=(kb == nkb - 1)),
                    reads=[B_vv[ks], B_pt[pti]], writes=[Bpo])
                if kb % 2 == 1:
                    k.op('dve', lambda e: e.tensor_tensor(out=pts[psi][:], in0=pt[pti_prev][:], in1=pt[pti][:], op=ALU.add),
                         reads=[B_pt[pti_prev], B_pt[pti]], writes=[B_pts[psi]])
                    if kb >= 3:
                        k.op('pe', lambda e: e.matmul(pl[:], lhsT=onesb[:], rhs=pts[psi_prev][:], start=(kb == 3), stop=False),
                             reads=[B_pts[psi_prev], B_const], writes=[Bpl])
                    if kb == nkb - 1:
                        k.op('pe', lambda e: e.matmul(pl[:], lhsT=onesb[:], rhs=pts[psi][:], start=(nkb == 2), stop=True),
                             reads=[B_pts[psi], B_const], writes=[Bpl])
            return fn

        def mk_evac(po, Bpo, pl, Bpl, m, qc, sgi):
            def fn():
                ao_ = acc_o[m][:, qc * 512:(qc + 1) * 512]
                al_ = acc_l[m][:, qc * 512:(qc + 1) * 512]
                if sgi == 0:
                    k.op('dve', lambda e: e.tensor_copy(out=ao_, in_=po[:]), reads=[Bpo], writes=[B_acc[m]])
                    k.op('dve', lambda e: e.tensor_copy(out=al_, in_=pl[:]), reads=[Bpl], writes=[B_acc[m]])
                else:
                    k.op('dve', lambda e: e.tensor_tensor(out=ao_, in0=po[:], in1=ao_, op=ALU.add),
                         reads=[Bpo, B_acc[m]], writes=[B_acc[m]])
                    k.op('dve', lambda e: e.tensor_tensor(out=al_, in0=pl[:], in1=al_, op=ALU.add),
                         reads=[Bpl, B_acc[m]], writes=[B_acc[m]])
            return fn

        load_q(0)
        seg_ensure(1)
        si = 0
        for ji, (kind, h) in enumerate(jobs):
            nq = NS if kind == 's' else NP
            n0 = NP if kind == 's' else 0
            nqc = nq // 512
            qs = ji % 2
            nseg = 1 if kind == 's' else G
            for sgi in range(nseg):
                _, _, _, r, nk = segs[si]
                ks = si % NKS
                nkb = nk // 128
                for qc in range(nqc):
                    for m in range(nmap):
                        po, Bpo = ps_o[cnt['ol'] % 2]
                        pl, Bpl = ps_l[cnt['ol'] % 2]
                        cnt['ol'] += 1
                        for kb in range(nkb):
                            pss, Bpss = ps_s[cnt['s'] % 3]
                            cnt['s'] += 1
                            pti_prev = (cnt['pt'] - 1) % NPT
                            pti = cnt['pt'] % NPT
                            cnt['pt'] += 1
                            if kb % 2 == 1:
                                cnt['ps'] = cnt.get('ps', 0) + 1
                            psi = (cnt.get('ps', 0) - 1) % NPS
                            psi_prev = (cnt.get('ps', 0) - 2) % NPS
                            pre = []
                            post = []
                            if qc == 0 and m == 0 and kb == 0:
                                if sgi == 0 and ji + 1 < len(jobs):
                                    pre.append(lambda ji=ji: load_q(ji + 1))
                            if kb == nkb - 1:
                                post.append(mk_evac(po, Bpo, pl, Bpl, m, qc, sgi))
                                if qc == nqc - 1 and m == nmap - 1:
                                    post.append(lambda si=si: seg_ensure(si + 2))
                                if sgi == nseg - 1 and qc == nqc - 1 and m == nmap - 1:
                                    post.append(lambda ji=ji, h=h, nq=nq, n0=n0, nqc=nqc: finalize(ji, h, nq, n0, nqc))
                            tasks.append((pre, mk_qk(pss, Bpss, ks, qs, kb, qc, m, pti), mk_pv(po, Bpo, pl, Bpl, ks, kb, pti, nkb, pti_prev, psi, psi_prev), post))
                si += 1
        nt_ = len(tasks)
        for it in range(nt_ + LAG):
            if it < nt_:
                for f in tasks[it][0]:
                    f()
                tasks[it][1]()
            if it >= LAG:
                tasks[it - LAG][2]()
                for f in tasks[it - LAG][3]:
                    f()

    for p in range(L + 1):
        row_pass(p)
        if p < L:
            k.barrier()
            if 'nomix' not in DBG:
                exchange(p)
                attention(p)
                if p % 2 == 0:
                    halo_fix(p)
            k.barrier()
    k.finish(final_events)
    k.replay()
    return nc


def _host_consts(cfg):
    NT, NP, NS = cfg.NT, cfg.NP, cfg.NS
    ident = np.eye(128, dtype=np.float32)
    onesf = np.ones((128, 128), np.float32)
    rswap = np.zeros((128, 128), np.float32)
    for p in range(128):
        g, d = p // 64, p % 64
        rswap[g * 64 + (d + 32) % 64, p] = 1.0
    onesb = np.ones((128, 128), ml_dtypes.bfloat16)
    bd = np.zeros((128, 128), np.float32)
    bd[0:64, 0:64] = 1.0
    bd[64:128, 64:128] = 1.0
    bd64 = bd.astype(ml_dtypes.bfloat16)
    return ident, onesf, rswap, onesb, bd64


def _rope_tables(positions):
    d = 64
    inv = (1.0 / (np.float32(ROPE_THETA) ** (np.arange(0, d, 2, dtype=np.float32) / np.float32(d)))).astype(np.float32)
    ang = positions.astype(np.float32)[:, None] * inv[None, :]
    cos = np.cos(ang).astype(np.float32)
    sin = np.sin(ang).astype(np.float32)
    ct = np.zeros((128, len(positions)), np.float32)
    st = np.zeros((128, len(positions)), np.float32)
    for p in range(128):
        dd = p % 64
        j = dd % 32
        ct[p] = cos[:, j]
        st[p] = -sin[:, j] if dd < 32 else sin[:, j]
    return ct, st


_PROG_CACHE = {}


def kernel(**inputs):
    cfg = CFG
    L, NE, NO, NP, NS, NT = cfg.DEPTH, cfg.NE, cfg.NO, cfg.NP, cfg.NS, cfg.NT
    f32 = lambda a: np.ascontiguousarray(np.asarray(a, dtype=np.float32))
    xp = f32(inputs['x_prompt'])
    xs = f32(inputs['x_sample'])
    gcols, NG = gain_layout(cfg)
    gains = np.zeros((128, NG), np.float32)
    gscale = np.ones((128, NG), np.float32)
    P = np.arange(128)

    def put(name, j, vec, sc):
        c = gcols[name] + j
        gains[:, c] = vec
        gscale[:, c] = np.float32(sc)

    for l in range(L):
        for nm, key in (('f1', 'ffn1_norm'), ('mx', 'mix_norm'), ('f2', 'ffn2_norm')):
            g = f32(inputs[key])[l]
            for c in range(8):
                put((nm, l), c, g[c * 128:(c + 1) * 128], math.sqrt(D))
    for i in range(NE):
        cw = f32(inputs['even_conv_w'])[i]
        for kk in range(3):
            for c in range(4):
                put(('cw', i), kk * 4 + c, cw[kk, c * 128:(c + 1) * 128], 1.0)
        qn = f32(inputs['even_q_norm'])[i]
        kn = f32(inputs['even_k_norm'])[i]
        put(('qA', i), 0, qn[P % 64], 8.0)
        put(('qB', i), 0, qn[(P % 64 + 32) % 64], 8.0)
        put(('kA', i), 0, kn[P % 64], 8.0)
        put(('kB', i), 0, kn[(P % 64 + 32) % 64], 8.0)
        put(('sub', i), 0, f32(inputs['even_subln'])[i], math.sqrt(128.0) * (1.0 - lambda_init(2 * i)))
    for i in range(NO):
        ql = f32(inputs['mla_q_lat_norm'])[i]
        kvl = f32(inputs['mla_kv_lat_norm'])[i]
        for c in range(3):
            put(('ql', i), c, ql[c * 128:(c + 1) * 128], math.sqrt(384.0))
        for c in range(2):
            put(('kvl', i), c, kvl[c * 128:(c + 1) * 128], 16.0)
        for pre, key in (('q', 'mla_q_norm'), ('k', 'mla_k_norm')):
            g = f32(inputs[key])[i]
            s = math.sqrt(192.0)
            put((pre + 'n', i), 0, g[0:128], s)
            put((pre + 'rA', i), 0, g[128 + P % 64], s)
            put((pre + 'rB', i), 0, g[128 + (P % 64 + 32) % 64], s)
    lamT = np.zeros((128, 4 * NE), np.float32)
    lv = f32(inputs['even_lambda'])
    for i in range(NE):
        for r in range(4):
            lamT[0:64, 4 * i + r] = lv[i, r]
    ident, onesf, rswap, onesb, bd64 = _host_consts(cfg)
    zeros = np.zeros((128, 64), np.float32)

    common = {
        'ffn1_w_in': f32(inputs['ffn1_w_in']), 'ffn1_w_out': f32(inputs['ffn1_w_out']),
        'ffn2_w_in': f32(inputs['ffn2_w_in']), 'ffn2_w_out': f32(inputs['ffn2_w_out']),
        'even_w_in': f32(inputs['even_w_in']), 'even_w_out': f32(inputs['even_w_out']),
        'gains': gains, 'gscale': gscale, 'lamT': lamT, 'ident': ident, 'onesf': onesf, 'rswap': rswap,
        'onesb': onesb, 'bd64': bd64, 'zeros': zeros,
    }
    if NO:
        common.update({'mla_w_down': f32(inputs['mla_w_down']), 'mla_w_uq': f32(inputs['mla_w_uq']),
                       'mla_w_ukv': f32(inputs['mla_w_ukv']), 'mla_w_o': f32(inputs['mla_w_o'])})
    in_maps = []
    for c in range(NCORES):
        b, r = c // G, c % G
        x_in = np.concatenate([xp[b, r * NP:(r + 1) * NP, :], xs[c]], axis=0)
        pos = np.concatenate([np.arange(r * NP, (r + 1) * NP), np.arange(NS)])
        ct, st = _rope_tables(pos)
        hm = np.zeros((128, 2 * G), np.float32)
        if r > 0:
            hm[:, r - 1] = 1.0
        if r < G - 1:
            hm[:, G + r + 1] = 1.0
        m = dict(common)
        m.update({'x_in': np.ascontiguousarray(x_in), 'cos_t': ct, 'sin_t': st, 'hmask': hm})
        in_maps.append(m)

    key = (cfg.SEQ, cfg.DEC_SEQ, cfg.DEPTH)
    if key not in _PROG_CACHE:
        _PROG_CACHE[key] = build_program(cfg)
    nc = _PROG_CACHE[key]
    res = run_bass_kernel_spmd(nc, in_maps, core_ids=list(range(NCORES)))
    global LAST_RES
    LAST_RES = res
    yp = np.zeros((2, cfg.SEQ, D), np.float32)
    ys = np.zeros((NCORES, NS, D), np.float32)
    for c in range(NCORES):
        y = np.asarray(res.results[c]['y_out'])
        b, r = c // G, c % G
        yp[b, r * NP:(r + 1) * NP, :] = y[0:NP]
        ys[c] = y[NP:NT]
    return yp, ys
```
